# Optimizing a Trainium2 kernel written in Bass

```python
import jax
import jax.numpy as jnp
from jax import lax
import numpy as np

D_MODEL = 1024
BATCH = 32
SEQ = 256
DEPTH = 2
DEC_BATCH = 2
DEC_SEQ = 4096
PAST_LEN = 512

GRID_W = 64
POS_BASE = 10000.0
N_MIXERS = 4
D_MIX = D_MODEL
GROUP_W = D_MIX // N_MIXERS
N_DIRS = 2
CHUNK = 16
EPS = 1e-6
N_MOD = 6
GLA_HEADS = 4
GLA_DK = GROUP_W // GLA_HEADS
GLA_DV = GROUP_W // GLA_HEADS
GLA_LOWRANK = 16
GLA_GATE_TEMP = 16.0
RG_BLOCKS = 4
RG_BLOCK_W = GROUP_W // RG_BLOCKS
RG_CONV_W = 4
RG_C = 8.0
HY_ORDER = 2
HY_SHORT_W = 3
HY_POS_BANDS = 16
HY_POS_DIM = 2 * HY_POS_BANDS + 1
HY_FFN_W = 64
HY_DECAY_MIN = 3.07
HY_DECAY_MAX = 15.35
HG_HEADS = 4
HG_DK = GROUP_W // HG_HEADS
HG_DV = GROUP_W // HG_HEADS
D_FF = -(-(8 * D_MODEL) // (3 * 256)) * 256

IN_SPLITS = (GROUP_W, GROUP_W, GROUP_W, GROUP_W, GLA_LOWRANK,
             GROUP_W, GROUP_W,
             (HY_ORDER + 1) * GROUP_W,
             GROUP_W, GROUP_W, GROUP_W, GROUP_W, GROUP_W)
IN_OFFSETS = tuple(int(o) for o in np.cumsum(IN_SPLITS)[:-1])
D_IN = int(sum(IN_SPLITS))

kernel_name = 'hybrid_diffusion_gla_rglru_hyena_hgrn2_step'

F32 = jnp.float32


def rmsnorm(x, g):
    xf = x.astype(F32)
    y = xf * lax.rsqrt(jnp.mean(xf * xf, axis=-1, keepdims=True) + EPS)
    return (y * g.astype(F32)).astype(x.dtype)


def split_heads(t, n_heads):
    return t.reshape(t.shape[:-1] + (n_heads, t.shape[-1] // n_heads))


def head_rmsnorm_gate(o, gain, gate):
    o = o * lax.rsqrt(jnp.mean(o * o, axis=-1, keepdims=True) + EPS)
    o = o.reshape(o.shape[:2] + (-1,)) * gain.astype(F32)
    return (o * jax.nn.silu(gate.astype(F32))).astype(gate.dtype)


def depthwise_conv(x, w, b, pad_left):
    width, ch = w.shape
    y = lax.conv_general_dilated(x, w[:, None, :], (1,), [(pad_left, width - 1 - pad_left)],
                                 dimension_numbers=('NWC', 'WIO', 'NWC'), feature_group_count=ch)
    return y + b


def grid_position_embedding(n_tokens, dim):
    rows = n_tokens // GRID_W
    row = jnp.broadcast_to(jnp.arange(rows, dtype=F32)[:, None], (rows, GRID_W)).reshape(-1)
    col = jnp.broadcast_to(jnp.arange(GRID_W, dtype=F32)[None, :], (rows, GRID_W)).reshape(-1)
    quarter = dim // 4
    omega = 1.0 / (POS_BASE ** (jnp.arange(quarter, dtype=F32) / quarter))

    def enc(pos):
        ang = pos[:, None] * omega[None, :]
        return jnp.concatenate([jnp.sin(ang), jnp.cos(ang)], axis=-1)

    return jnp.concatenate([enc(row), enc(col)], axis=-1)


def chunked_gated_state(q, k, v, log_a, s0):
    B, L, H, _ = q.shape
    V = v.shape[-1]
    n = L // CHUNK

    def chunks(t):
        return t.astype(F32).reshape(B, n, CHUNK, H, t.shape[-1]).transpose(1, 0, 3, 2, 4)

    qc, kc, vc, ac = chunks(q), chunks(k), chunks(v), chunks(log_a)
    b = jnp.cumsum(ac, axis=3)
    causal = jnp.tril(jnp.ones((CHUNK, CHUNK), dtype=bool))
    rel = jnp.where(causal[:, :, None], b[:, :, :, :, None, :] - b[:, :, :, None, :, :], -jnp.inf)
    scores = jnp.einsum('nbhtk,nbhsk,nbhtsk->nbhts', qc, kc, jnp.exp(rel))
    o_intra = jnp.einsum('nbhts,nbhsv->nbhtv', scores, vc)
    b_end = b[:, :, :, -1:, :]
    q_dec = qc * jnp.exp(b)
    kv = jnp.einsum('nbhck,nbhcv->nbhkv', kc * jnp.exp(b_end - b), vc)
    decay_end = jnp.exp(b_end[:, :, :, 0, :])

    def step(S, inp):
        qd, dec, kv_c = inp
        o = jnp.einsum('bhck,bhkv->bhcv', qd, S)
        return dec[..., None] * S + kv_c, o

    s_final, o_inter = lax.scan(step, s0.astype(F32), (q_dec, decay_end, kv))
    o = (o_intra + o_inter).transpose(1, 0, 3, 2, 4).reshape(B, L, H, V)
    return o, s_final


def diag_linear_scan(a, u, h0):
    def combine(e1, e2):
        a1, b1 = e1
        a2, b2 = e2
        return a1 * a2, a2 * b1 + b2

    a_cum, u_cum = lax.associative_scan(combine, (a, u), axis=1)
    h = a_cum * h0[:, None, :] + u_cum
    return h, h[:, -1]


def gla_mixer(q, k, v, g, lr, w_gate, b_gate, norm_g, s0):
    qh = split_heads(q, GLA_HEADS) * GLA_DK ** -0.5
    kh = split_heads(k, GLA_HEADS)
    vh = split_heads(v, GLA_HEADS)
    outs, finals = [], []
    for d in range(N_DIRS):
        log_a = jax.nn.log_sigmoid((lr @ w_gate[d] + b_gate[d]).astype(F32)) / GLA_GATE_TEMP
        seq = (qh, kh, vh, split_heads(log_a, GLA_HEADS))
        if d == 1:
            seq = tuple(jnp.flip(t, axis=1) for t in seq)
        o, s_f = chunked_gated_state(*seq, s0[:, d])
        outs.append(o if d == 0 else jnp.flip(o, axis=1))
        finals.append(s_f)
    return head_rmsnorm_gate(outs[0] + outs[1], norm_g, g), jnp.stack(finals, axis=1)


def rglru_mixer(xb, gb, conv_w, conv_b, w_a, b_a, w_x, b_x, lam, h0):
    B, L, C = xb.shape
    xc = depthwise_conv(xb, conv_w, conv_b, RG_CONV_W // 2)
    xblk = split_heads(xc, RG_BLOCKS)
    xf = xc.astype(F32)
    outs, finals = [], []
    for d in range(N_DIRS):
        r = jax.nn.sigmoid((jnp.einsum('blhi,hij->blhj', xblk, w_a[d]).reshape(B, L, C) + b_a[d]).astype(F32))
        i = jax.nn.sigmoid((jnp.einsum('blhi,hij->blhj', xblk, w_x[d]).reshape(B, L, C) + b_x[d]).astype(F32))
        log_a = -RG_C * r * jax.nn.softplus(-lam[d].astype(F32))
        a = jnp.exp(log_a)
        u = jnp.sqrt(-jnp.expm1(2.0 * log_a)) * (i * xf)
        if d == 1:
            a, u = jnp.flip(a, axis=1), jnp.flip(u, axis=1)
        h, h_last = diag_linear_scan(a, u, h0[:, d].astype(F32))
        outs.append(h if d == 0 else jnp.flip(h, axis=1))
        finals.append(h_last)
    y = (outs[0] + outs[1]) * jax.nn.gelu(gb.astype(F32))
    return y.astype(xb.dtype), jnp.stack(finals, axis=1)


def hyena_filters(L, w1, b1, w2, b2, w3, decay_rate):
    pos = jnp.arange(L, dtype=F32)
    t = pos / (L - 1)
    ang = (2.0 * jnp.pi * pos / L)[:, None] * jnp.linspace(1e-4, HY_POS_BANDS - 1, HY_POS_BANDS, dtype=F32)[None, :]
    pe = jnp.concatenate([t[:, None], jnp.cos(ang), -jnp.sin(ang)], axis=-1)
    h = jnp.sin(pe @ w1.astype(F32) + b1.astype(F32))
    h = jnp.sin(h @ w2.astype(F32) + b2.astype(F32))
    h = h @ w3.astype(F32)
    half = L // 2
    dist = jnp.abs(pos - half) / half
    h = h * jnp.exp(-dist[:, None] * decay_rate.astype(F32)[None, :])
    return h / jnp.sum(jnp.abs(h), axis=0, keepdims=True)


def fft_long_conv(z, h):
    L = z.shape[1]
    n_fft = 2 * L
    zf = jnp.fft.rfft(z, n=n_fft, axis=1)
    hf = jnp.fft.rfft(h, n=n_fft, axis=0)
    full = jnp.fft.irfft(zf * hf[None], n=n_fft, axis=1)
    return full[:, L // 2: L // 2 + L]


def hyena_mixer(proj, conv_w, conv_b, w1, b1, w2, b2, w3, decay_rate, skip):
    L = proj.shape[1]
    uc = depthwise_conv(proj, conv_w, conv_b, HY_SHORT_W // 2).astype(F32)
    v, x1, x2 = jnp.split(uc, HY_ORDER + 1, axis=-1)
    filt = hyena_filters(L, w1, b1, w2, b2, w3, decay_rate)
    skip = skip.astype(F32)
    z = v
    for n, gate in enumerate((x1, x2)):
        sl = slice(n * GROUP_W, (n + 1) * GROUP_W)
        z = gate * (fft_long_conv(z, filt[:, sl]) + skip[sl] * z)
    return z.astype(proj.dtype)


def hgrn2_mixer(q, f_fwd, f_bwd, i, g, lb, norm_g, s0):
    qh = split_heads(jax.nn.silu(q.astype(F32)), HG_HEADS)
    vh = split_heads(i, HG_HEADS)
    log_lb, log_1m_lb = jnp.log(lb), jnp.log1p(-lb)
    outs, finals = [], []
    for d, fl in enumerate((f_fwd, f_bwd)):
        fl = fl.astype(F32)
        log_f = jnp.logaddexp(log_lb, log_1m_lb + jax.nn.log_sigmoid(fl))
        k = (1.0 - lb) * jax.nn.sigmoid(-fl)
        seq = (qh, split_heads(k, HG_HEADS), vh, split_heads(log_f, HG_HEADS))
        if d == 1:
            seq = tuple(jnp.flip(t, axis=1) for t in seq)
        o, s_f = chunked_gated_state(*seq, s0[:, d])
        outs.append(o if d == 0 else jnp.flip(o, axis=1))
        finals.append(s_f)
    return head_rmsnorm_gate(outs[0] + outs[1], norm_g, g), jnp.stack(finals, axis=1)


def token_mixers(u, p, s_gla, s_rg, s_hg):
    proj = u @ p['w_in']
    (a_q, a_k, a_v, a_g, a_lr, b_x, b_g, c_in, d_q, d_ff, d_fb, d_i, d_g) = jnp.split(proj, IN_OFFSETS, axis=-1)
    y_a, st_a = gla_mixer(a_q, a_k, a_v, a_g, a_lr, p['gla_w_gate'], p['gla_b_gate'], p['gla_norm_g'], s_gla)
    y_b, st_b = rglru_mixer(b_x, b_g, p['rg_conv_w'], p['rg_conv_b'], p['rg_w_a'], p['rg_b_a'],
                            p['rg_w_x'], p['rg_b_x'], p['rg_lambda'], s_rg)
    y_c = hyena_mixer(c_in, p['hy_conv_w'], p['hy_conv_b'], p['hy_w1'], p['hy_b1'], p['hy_w2'],
                      p['hy_b2'], p['hy_w3'], p['hy_decay'], p['hy_skip'])
    y_d, st_d = hgrn2_mixer(d_q, d_ff, d_fb, d_i, d_g, p['hg_lb'], p['hg_norm_g'], s_hg)
    mixed = jnp.concatenate([y_a, y_b, y_c, y_d], axis=-1) @ p['w_out']
    return mixed, (st_a.astype(u.dtype), st_b.astype(u.dtype), st_d.astype(u.dtype))


def trunk_layer(x, mod, p, s_gla, s_rg, s_hg):
    sh1, sc1, g1, sh2, sc2, g2 = jnp.split(mod, N_MOD, axis=-1)
    u = rmsnorm(x, p['norm1_g']) * (1 + sc1) + sh1
    mixed, finals = token_mixers(u, p, s_gla, s_rg, s_hg)
    x = x + g1 * mixed
    u = rmsnorm(x, p['norm2_g']) * (1 + sc2) + sh2
    hidden = jax.nn.silu(u @ p['ffn_w1']) * (u @ p['ffn_w3'])
    x = x + g2 * (hidden @ p['ffn_w2'])
    return x, finals


def setup_inputs(seed: int = 0) -> dict:
    key = jax.random.key(seed)
    keys = iter(jax.random.split(key, 64))

    def nrm(shape, scale=1.0):
        return scale * jax.random.normal(next(keys), shape, F32)

    def gain(shape):
        return 1.0 + nrm(shape, 0.02)

    D, W, NL = D_MODEL, GROUP_W, DEPTH
    u = jax.random.uniform(next(keys), (NL, N_DIRS, W), F32, 0.9, 0.999)
    a_base = u ** (1.0 / RG_C)
    rg_lambda = jnp.log(a_base) - jnp.log1p(-a_base)
    hy_decay = (jnp.tile(jnp.linspace(HY_DECAY_MIN, HY_DECAY_MAX, W, dtype=F32), (NL, HY_ORDER))
                + nrm((NL, HY_ORDER * W), 0.1))
    return {
        'x_prompt': nrm((BATCH, SEQ, D)),
        'x_sample': nrm((DEC_BATCH, DEC_SEQ, D)),
        'state_gla': nrm((DEC_BATCH, NL, N_DIRS, GLA_HEADS, GLA_DK, GLA_DV), 0.5),
        'state_rglru': nrm((DEC_BATCH, NL, N_DIRS, W), 0.5),
        'state_hgrn': nrm((DEC_BATCH, NL, N_DIRS, HG_HEADS, HG_DK, HG_DV), 0.5),
        'c': nrm((DEC_BATCH, D)),
        'c_ctx': nrm((D,)),
        'norm1_g': gain((NL, D)),
        'norm2_g': gain((NL, D)),
        'final_norm_g': gain((D,)),
        'w_mod': nrm((NL, D, N_MOD * D), 0.5 * D ** -0.5),
        'b_mod': nrm((NL, N_MOD * D), 0.02),
        'w_in': nrm((NL, D, D_IN), D ** -0.5),
        'w_out': nrm((NL, D_MIX, D), D_MIX ** -0.5),
        'gla_w_gate': nrm((NL, N_DIRS, GLA_LOWRANK, W), GLA_LOWRANK ** -0.5),
        'gla_b_gate': nrm((NL, N_DIRS, W), 0.1),
        'gla_norm_g': gain((NL, W)),
        'rg_conv_w': nrm((NL, RG_CONV_W, W), RG_CONV_W ** -0.5),
        'rg_conv_b': nrm((NL, W), 0.02),
        'rg_w_a': nrm((NL, N_DIRS, RG_BLOCKS, RG_BLOCK_W, RG_BLOCK_W), RG_BLOCK_W ** -0.5),
        'rg_b_a': nrm((NL, N_DIRS, W), 0.1),
        'rg_w_x': nrm((NL, N_DIRS, RG_BLOCKS, RG_BLOCK_W, RG_BLOCK_W), RG_BLOCK_W ** -0.5),
        'rg_b_x': nrm((NL, N_DIRS, W), 0.1),
        'rg_lambda': rg_lambda,
        'hy_conv_w': nrm((NL, HY_SHORT_W, (HY_ORDER + 1) * W), HY_SHORT_W ** -0.5),
        'hy_conv_b': nrm((NL, (HY_ORDER + 1) * W), 0.02),
        'hy_w1': nrm((NL, HY_POS_DIM, HY_FFN_W), HY_POS_DIM ** -0.5),
        'hy_b1': nrm((NL, HY_FFN_W), 0.1),
        'hy_w2': nrm((NL, HY_FFN_W, HY_FFN_W), HY_FFN_W ** -0.5),
        'hy_b2': nrm((NL, HY_FFN_W), 0.1),
        'hy_w3': nrm((NL, HY_FFN_W, HY_ORDER * W), HY_FFN_W ** -0.5),
        'hy_decay': hy_decay,
        'hy_skip': nrm((NL, HY_ORDER * W)),
        'hg_lower': nrm((NL, W), 0.1),
        'hg_norm_g': gain((NL, W)),
        'ffn_w1': nrm((NL, D, D_FF), D ** -0.5),
        'ffn_w3': nrm((NL, D, D_FF), D ** -0.5),
        'ffn_w2': nrm((NL, D_FF, D), D_FF ** -0.5),
    }


def reference(x_prompt, x_sample, state_gla, state_rglru, state_hgrn, c, c_ctx,
              norm1_g, norm2_g, final_norm_g, w_mod, b_mod, w_in, w_out,
              gla_w_gate, gla_b_gate, gla_norm_g,
              rg_conv_w, rg_conv_b, rg_w_a, rg_b_a, rg_w_x, rg_b_x, rg_lambda,
              hy_conv_w, hy_conv_b, hy_w1, hy_b1, hy_w2, hy_b2, hy_w3, hy_decay, hy_skip,
              hg_lower, hg_norm_g, ffn_w1, ffn_w3, ffn_w2):
    n_ctx_req = x_prompt.shape[0]
    hg_lb = jnp.cumsum(jax.nn.softmax(hg_lower.astype(F32), axis=0), axis=0)
    hg_lb = hg_lb - hg_lb[0:1]
    xp = x_prompt
    xs = x_sample + grid_position_embedding(x_sample.shape[1], x_sample.shape[2]).astype(x_sample.dtype)[None]
    zero_gla = jnp.zeros((n_ctx_req, N_DIRS, GLA_HEADS, GLA_DK, GLA_DV), x_prompt.dtype)
    zero_rg = jnp.zeros((n_ctx_req, N_DIRS, GROUP_W), x_prompt.dtype)
    zero_hg = jnp.zeros((n_ctx_req, N_DIRS, HG_HEADS, HG_DK, HG_DV), x_prompt.dtype)
    gla_states, rg_states, hg_states = [], [], []
    for l in range(DEPTH):
        p = dict(norm1_g=norm1_g[l], norm2_g=norm2_g[l], w_in=w_in[l], w_out=w_out[l],
                 gla_w_gate=gla_w_gate[l], gla_b_gate=gla_b_gate[l], gla_norm_g=gla_norm_g[l],
                 rg_conv_w=rg_conv_w[l], rg_conv_b=rg_conv_b[l], rg_w_a=rg_w_a[l], rg_b_a=rg_b_a[l],
                 rg_w_x=rg_w_x[l], rg_b_x=rg_b_x[l], rg_lambda=rg_lambda[l],
                 hy_conv_w=hy_conv_w[l], hy_conv_b=hy_conv_b[l], hy_w1=hy_w1[l], hy_b1=hy_b1[l],
                 hy_w2=hy_w2[l], hy_b2=hy_b2[l], hy_w3=hy_w3[l], hy_decay=hy_decay[l], hy_skip=hy_skip[l],
                 hg_lb=hg_lb[l], hg_norm_g=hg_norm_g[l],
                 ffn_w1=ffn_w1[l], ffn_w3=ffn_w3[l], ffn_w2=ffn_w2[l])
        mod_ctx = (jax.nn.silu(c_ctx)[None, :] @ w_mod[l] + b_mod[l])[:, None, :]
        mod_lat = (jax.nn.silu(c) @ w_mod[l] + b_mod[l])[:, None, :]
        xp, (sg, sr, sh) = trunk_layer(xp, mod_ctx, p, zero_gla, zero_rg, zero_hg)
        xs, _ = trunk_layer(xs, mod_lat, p, state_gla[:, l], state_rglru[:, l], state_hgrn[:, l])
        gla_states.append(sg)
        rg_states.append(sr)
        hg_states.append(sh)
    y_prompt = rmsnorm(xp, final_norm_g)
    y_sample = rmsnorm(xs, final_norm_g)
    new_state_gla = jnp.stack(gla_states, axis=1)
    new_state_rglru = jnp.stack(rg_states, axis=1)
    new_state_hgrn = jnp.stack(hg_states, axis=1)
    return (y_prompt, y_sample, new_state_gla, new_state_rglru, new_state_hgrn)
```

```python
import numpy as np
from contextlib import ExitStack
import ml_dtypes
import concourse.bass as bass
import concourse.mybir as mybir
from concourse.bass_utils import run_bass_kernel_spmd

F32 = mybir.dt.float32
BF16 = mybir.dt.bfloat16
I32 = mybir.dt.int32
U8 = mybir.dt.uint8
AF = mybir.ActivationFunctionType
ALU = mybir.AluOpType
AX = mybir.AxisListType

D = 1024
DIN = 3600
DFF = 2816
NL = 2
EPS = 1e-6
PI = float(np.pi)


class Dep:
    __slots__ = ("w", "r", "x", "rg")

    def __init__(self, x=False):
        self.w = None
        self.r = {}
        self.x = x
        self.rg = None


class Sched:
    ENG = ("pe", "act", "dve", "pool", "sp")

    def __init__(self, nc, stack, n_dma_sems=96):
        self.nc = nc
        self.semh = {}
        self.cnt = {}
        for e in self.ENG:
            self.semh[e] = stack.enter_context(nc.semaphore("s_" + e))
            self.cnt[e] = 0
        for i in range(n_dma_sems):
            k = "d%d" % i
            self.semh[k] = stack.enter_context(nc.semaphore("s_" + k))
            self.cnt[k] = 0
        self.n_dma_sems = n_dma_sems
        self.next_dsem = 0
        self.prog = {e: [] for e in self.ENG}
        self.known = {e: {} for e in self.ENG}
        self.ninstr = 0
        self.pe_self = False
        self.pe_mode = None

    def new_dsem(self):
        k = "d%d" % self.next_dsem
        self.next_dsem += 1
        assert self.next_dsem <= self.n_dma_sems, "out of dma sems"
        return k

    def _waits(self, eng, reads, writes, pe_rg=2):
        waits = {}
        if eng.startswith("dmaq:"):
            known = self.known[eng[5:]]
            eng = "dma"
        else:
            known = self.known[eng]

        def need(k, v):
            if known.get(k, 0) >= v:
                return
            if waits.get(k, 0) < v:
                waits[k] = v
        for t in reads:
            if t.w is not None and not (eng == "pe" and t.w[0] == "pe" and not self.pe_self):
                need(*t.w)
            if t.x:
                for k, v in t.r.items():
                    if k != eng:
                        need(k, v)
        for t in writes:
            if eng == "pe":
                switch = (t.rg is not None and t.rg != pe_rg)
                t.rg = pe_rg
            else:
                switch = False
            if t.w is not None and not (eng == "pe" and t.w[0] == "pe" and not (self.pe_self or switch)):
                need(*t.w)
            for k, v in t.r.items():
                if not (eng == "pe" and k == "pe" and not self.pe_self):
                    need(k, v)
        for k, v in waits.items():
            known[k] = v
        return list(waits.items())

    def _mark(self, pt, reads, writes):
        k, v = pt
        for t in writes:
            t.w = pt
            t.r = {}
        for t in reads:
            if t.r.get(k, 0) < v:
                t.r[k] = v

    def op(self, eng, fn, reads=(), writes=(), inc=True, pe_rg=2, pe_mode=(128, 128)):
        if eng != "pe":
            inc = True
        waits = self._waits(eng, reads, writes, pe_rg)
        if eng == "pe":
            if self.pe_mode is not None and self.pe_mode != pe_mode and self.cnt["pe"] > 0:
                v = self.cnt["pe"]
                if self.known["pe"].get("pe", 0) < v:
                    waits = [w for w in waits if w[0] != "pe"] + [("pe", v)]
                    self.known["pe"]["pe"] = v
            self.pe_mode = pe_mode
        if inc:
            self.cnt[eng] += 1
            val = self.cnt[eng]
        else:
            val = self.cnt[eng] + 1
        ws = set(id(t) for t in writes)
        self._mark((eng, val), [t for t in reads if id(t) not in ws], writes)
        self.prog[eng].append((waits, fn, eng if inc else None, 1))
        self.ninstr += 1

    def dma(self, queue, out, in_, reads=(), writes=(), sem=None, **kw):
        waits = self._waits("dmaq:" + queue, reads, writes)
        self.cnt[sem] += 16
        ws = set(id(t) for t in writes)
        self._mark((sem, self.cnt[sem]), [t for t in reads if id(t) not in ws], writes)
        self.prog[queue].append((waits, (lambda e, o=out, i=in_, kw=kw: e.dma_start(out=o, in_=i, **kw)), sem, 16))
        self.ninstr += 1

    def barrier(self):
        for e in self.ENG:
            waits = []
            for k, v in self.cnt.items():
                if v > 0 and self.known[e].get(k, 0) < v:
                    waits.append((k, v))
                    self.known[e][k] = v
            if waits:
                self.prog[e].append((waits, None, None, 0))

    def emit(self):
        nc = self.nc
        semh = self.semh

        def run(e, name):
            for waits, fn, incsem, incv in self.prog[name]:
                for k, v in waits:
                    e.wait_ge(semh[k], v)
                if fn is None:
                    continue
                ins = fn(e)
                if incsem is not None:
                    ins.then_inc(semh[incsem], incv)

        with nc.Block() as block:
            @block.tensor
            def _(e):
                run(e, "pe")

            @block.scalar
            def _(e):
                run(e, "act")

            @block.vector
            def _(e):
                run(e, "dve")

            @block.gpsimd
            def _(e):
                run(e, "pool")

            @block.sync
            def _(e):
                run(e, "sp")


class T:
    def __init__(self, ap, ndeps=1):
        self.ap = ap
        self.d = [Dep() for _ in range(ndeps)]

    def __getitem__(self, idx):
        return self.ap[idx]


def _dft_tables(L, N):
    nb = L // 128
    a = np.arange(L, dtype=np.float64) + 0.5
    ang = 2.0 * np.pi * np.outer(a, a) / N
    out = []
    for fn in (np.cos, np.sin):
        G = fn(ang)
        Gt = G.reshape(nb, 128, nb, 128).transpose(2, 1, 0, 3)
        out.append(np.ascontiguousarray(Gt).astype(ml_dtypes.bfloat16))
    nfb = (N // 2) // 128
    f = np.arange(N // 2, dtype=np.float64) + 0.5
    phi = 2.0 * np.pi * f / N * (L / 2 + 0.5)
    ec = (2.0 / N) * np.cos(phi)
    es = (2.0 / N) * np.sin(phi)
    E = np.stack([ec.reshape(nfb, 128).T, es.reshape(nfb, 128).T], axis=1)
    return out[0], out[1], np.ascontiguousarray(E).astype(np.float32), nfb


def _pe_consts(L):
    bands = np.linspace(1e-4, 15, 16, dtype=np.float32)
    w = (np.float32(2.0 * np.pi) / np.float32(L)) * bands
    om = np.zeros(33, np.float32); ph = np.zeros(33, np.float32)
    om[1:17] = w;  ph[1:17] = np.float32(np.pi / 2)
    om[17:33] = w; ph[17:33] = np.float32(np.pi)
    om[0] = np.float32(1.0) / np.float32(L - 1)
    return np.stack([om, ph], axis=1).astype(np.float32)


def _grid_consts(dim):
    quarter = dim // 4
    omega = (1.0 / (np.float32(10000.0) ** (np.arange(quarter, dtype=np.float32) / np.float32(quarter)))).astype(np.float32)
    om = np.concatenate([omega, omega, omega, omega])
    ph = np.concatenate([np.zeros(quarter), np.full(quarter, np.pi / 2)] * 2).astype(np.float32)
    p = np.arange(128)
    rc = np.stack([(p >= 64).astype(np.float32), (p % 64).astype(np.float32)], axis=1)
    return om.astype(np.float32), ph, rc.astype(np.float32)


def _sp_layout():
    rows = {}
    n = 0

    def add(name, cnt):
        nonlocal n
        rows[name] = n
        n += cnt
    for l in range(NL):
        add("n1g%d" % l, 8); add("n2g%d" % l, 8); add("bmod%d" % l, 48)
        add("glang%d" % l, 2); add("rgcw%d" % l, 8); add("rgcb%d" % l, 2)
        add("rgba%d" % l, 4); add("rgbx%d" % l, 4); add("rglam%d" % l, 4)
        add("hycw%d" % l, 18); add("hycb%d" % l, 6); add("hyb1%d" % l, 1); add("hyb2%d" % l, 1)
        add("hglow%d" % l, 2); add("hgng%d" % l, 2)
    add("fng", 8); add("c", 8); add("cctx", 8); add("pec4096", 2); add("pec256", 2)
    return rows, ((n + 127) // 128) * 128


SP_ROWS, SP_N = _sp_layout()


def _build_sp(inp, b):
    sp = np.zeros((SP_N, 128), np.float32)

    def put(name, arr):
        a = np.asarray(arr, np.float32).reshape(-1, 128)
        sp[SP_ROWS[name]:SP_ROWS[name] + a.shape[0]] = a
    for l in range(NL):
        put("n1g%d" % l, inp["norm1_g"][l]); put("n2g%d" % l, inp["norm2_g"][l]); put("bmod%d" % l, inp["b_mod"][l])
        put("glang%d" % l, inp["gla_norm_g"][l]); put("rgcw%d" % l, inp["rg_conv_w"][l]); put("rgcb%d" % l, inp["rg_conv_b"][l])
        put("rgba%d" % l, inp["rg_b_a"][l]); put("rgbx%d" % l, inp["rg_b_x"][l]); put("rglam%d" % l, inp["rg_lambda"][l])
        put("hycw%d" % l, inp["hy_conv_w"][l]); put("hycb%d" % l, inp["hy_conv_b"][l])
        r = np.zeros(128, np.float32); r[:64] = inp["hy_b1"][l]; put("hyb1%d" % l, r)
        r = np.zeros(128, np.float32); r[:64] = inp["hy_b2"][l]; put("hyb2%d" % l, r)
        put("hglow%d" % l, inp["hg_lower"][l]); put("hgng%d" % l, inp["hg_norm_g"][l])
    put("fng", inp["final_norm_g"]); put("c", inp["c"][b]); put("cctx", inp["c_ctx"])
    for L in (4096, 256):
        pc = _pe_consts(L)
        r = np.zeros((2, 128), np.float32); r[0, :33] = pc[:, 0]; r[1, :33] = pc[:, 1]
        put("pec%d" % L, r)
    return sp


IN_OFF = dict(a_q=0, a_k=256, a_v=512, a_g=768, a_lr=1024, b_x=1040, b_g=1296, c_v=1552, c_x1=1808, c_x2=2064,
              d_q=2320, d_ff=2576, d_fb=2832, d_i=3088, d_g=3344)

ARENA_BYTES = 206 * 1024


class Arena:
    def __init__(self, nc, stack):
        self.t = stack.enter_context(nc.sbuf_tensor("arena", [128, ARENA_BYTES // 4], F32))
        self.off = 0

    def alloc(self, dtype, shape, ndeps=1):
        esz = 4 if dtype in (F32, I32) else (1 if dtype == U8 else 2)
        n = int(np.prod(shape[1:]))
        nb = ((n * esz + 31) // 32) * 32
        assert self.off + nb <= ARENA_BYTES, "arena overflow %d" % (self.off + nb)
        ap = self.t[:, self.off // 4:(self.off + nb) // 4]
        if dtype != F32:
            ap = ap.bitcast(dtype)
        ap = ap[0:shape[0], 0:n]
        if len(shape) > 2:
            names = "abcdefg"[:len(shape) - 1]
            kw = {names[i]: shape[i + 1] for i in range(len(shape) - 2)}
            ap = ap.rearrange("p (%s) -> p %s" % (" ".join(names), " ".join(names)), **kw)
        self.off += nb
        return T(ap, ndeps)

    def mark(self):
        return self.off

    def release(self, m):
        self.off = m


def build_program(flags=("gla", "rg", "hy", "hg"), dbg=False, stop=99):
    nc = bass.Bass("TRN2", target_bir_lowering=False)
    st = ExitStack()
    S = Sched(nc, st)
    A = Arena(nc, st)

    def din(name, shape, dt=F32):
        return nc.dram_tensor(name, list(shape), dt, kind="ExternalInput").ap()

    def dout(name, shape, dt=F32):
        return nc.dram_tensor(name, list(shape), dt, kind="ExternalOutput").ap()

    def dscr(name, shape, dt=F32):
        return nc.dram_tensor(name, list(shape), dt, kind="Internal").ap()

    I = {}
    I["xs"] = din("xs", [4096, D]); I["xp"] = din("xp", [1024, D])
    I["st_gla"] = din("st_gla", [NL, 2, 256, 64]); I["st_hg"] = din("st_hg", [NL, 2, 256, 64]); I["st_rg"] = din("st_rg", [NL, 2, 256])
    I["w_mod"] = din("w_mod", [NL, D, 6 * D]); I["w_in"] = din("w_in", [NL, D, DIN]); I["w_out"] = din("w_out", [NL, D, D])
    I["w1"] = din("w1", [NL, D, DFF]); I["w3"] = din("w3", [NL, D, DFF]); I["w2"] = din("w2", [NL, DFF, D])
    I["sp"] = din("sp", [SP_N, 128])
    I["gla_wg"] = din("gla_wg", [NL, 16, 512]); I["gla_bg"] = din("gla_bg", [NL, 512])
    I["rg_wa"] = din("rg_wa", [NL, 2, 4, 64, 64]); I["rg_wx"] = din("rg_wx", [NL, 2, 4, 64, 64])
    I["hy_w1"] = din("hy_w1", [NL, 33, 64]); I["hy_w2"] = din("hy_w2", [NL, 64, 64]); I["hy_w3"] = din("hy_w3", [NL, 64, 512])
    I["hy_dec"] = din("hy_dec", [NL, 512]); I["hy_skip"] = din("hy_skip", [NL, 512]); I["hg_low"] = din("hg_low", [NL, 256])
    I["cmat"] = din("cmat", [128, 6, 128])
    I["gc4096"] = din("gc4096", [32, 128, 32, 128], BF16); I["gs4096"] = din("gs4096", [32, 128, 32, 128], BF16)
    I["gc256"] = din("gc256", [2, 128, 2, 128], BF16); I["gs256"] = din("gs256", [2, 128, 2, 128], BF16)
    I["e4096"] = din("e4096", [128, 2, 24]); I["e256"] = din("e256", [128, 2, 2])
    I["gridc"] = din("gridc", [2, D]); I["gridrc"] = din("gridrc", [128, 2])
    O = {}
    O["ys"] = dout("ys", [4096, D]); O["yp"] = dout("yp", [1024, D])
    O["ns_gla"] = dout("ns_gla", [4, NL, 2, 256, 64]); O["ns_hg"] = dout("ns_hg", [4, NL, 2, 256, 64]); O["ns_rg"] = dout("ns_rg", [4, NL, 2, 256])
    out_deps = []
    DBG = {}

    cm = A.alloc(F32, [128, 6, 128])
    ident = cm[:, 0, :]; tril = cm[:, 1, :]; triu = cm[:, 2, :]; su = cm[:, 3, :]; sl = cm[:, 4, :]
    bd64 = A.alloc(BF16, [128, 128]); onesm = A.alloc(BF16, [128, 128])
    colT = A.alloc(F32, [128, SP_N])
    modc = A.alloc(F32, [128, NL, 48, 2])
    acol = A.alloc(F32, [128, NL, 2, 2, 8])
    nsp = A.alloc(F32, [128, NL, 2, 2, 2])
    lbc = A.alloc(F32, [128, NL, 2, 2])
    WSL = 4
    wring = [A.alloc(BF16, [128, 8, 512]) for _ in range(WSL)]
    wsem = [S.new_dsem() for _ in range(WSL)]
    wnext = [0]
    uT = A.alloc(BF16, [128, 8, 4096], ndeps=8)
    PS = [T(st.enter_context(nc.psum_tensor("ps%d" % i, [128, 512], F32))[:, :]) for i in range(8)]
    for p_ in PS:
        p_.d = [Dep(x=True)]
    psn = [0]

    def nps():
        p = PS[psn[0] % 7]
        psn[0] += 1
        return p

    gsem = [S.new_dsem() for _ in range(8)]
    SPL = [S.new_dsem() for _ in range(24)]

    def C(row, n=1):
        return colT[:, row:row + n]

    def mm(out, lhsT, rhs, start, stop, reads, writes, last=True, rg=2, mode=(128, 128)):
        S.op("pe", lambda e: e.matmul(out, lhsT=lhsT, rhs=rhs, start=start, stop=stop), reads=reads, writes=writes, inc=last, pe_rg=rg, pe_mode=mode)

    def act(out, in_, func, reads, writes, **kw):
        S.op("act", lambda e: e.activation(out=out, in_=in_, func=func, **kw), reads=reads, writes=writes)

    def tt(eng, out, in0, in1, op, reads, writes):
        S.op(eng, lambda e: e.tensor_tensor(out=out, in0=in0, in1=in1, op=op), reads=reads, writes=writes)

    def ts(eng, out, in0, s1, s2, op0, op1, reads, writes):
        if op1 is None:
            S.op(eng, lambda e: e.tensor_scalar(out=out, in0=in0, scalar1=s1, scalar2=None, op0=op0), reads=reads, writes=writes)
        else:
            S.op(eng, lambda e: e.tensor_scalar(out=out, in0=in0, scalar1=s1, scalar2=s2, op0=op0, op1=op1), reads=reads, writes=writes)

    def stt(out, in0, scalar, in1, op0, op1, reads, writes):
        S.op("dve", lambda e: e.scalar_tensor_tensor(out=out, in0=in0, scalar=scalar, in1=in1, op0=op0, op1=op1), reads=reads, writes=writes)

    def cp(eng, out, in_, reads, writes):
        if eng == "act":
            S.op("act", lambda e: e.copy(out=out, in_=in_), reads=reads, writes=writes)
        else:
            S.op(eng, lambda e: e.tensor_copy(out=out, in_=in_), reads=reads, writes=writes)

    def load_w(w2d, r0, nr, c0, ncol):
        i = wnext[0] % WSL
        wnext[0] += 1
        slot = wring[i]
        kc = nr // 128
        src = w2d[r0:r0 + nr, c0:c0 + ncol].rearrange("(c p) n -> p c n", p=128)
        flat = slot.ap.rearrange("p a b -> p (a b)")[:, 0:kc * ncol].rearrange("p (c n) -> p c n", c=kc)
        S.dma("pool", flat, src, writes=slot.d, sem=wsem[i])
        return T(flat, 0), slot.d

    def sin_rr(out, arg, tmp_i, tmp_f, reads, writes_t):
        ts("dve", tmp_i.ap, arg.ap, 1.0 / (2 * PI), None, ALU.mult, None, arg.d + reads, tmp_i.d)
        cp("dve", tmp_f.ap, tmp_i.ap, tmp_i.d, tmp_f.d)
        stt(arg.ap, tmp_f.ap, -2 * PI, arg.ap, ALU.mult, ALU.add, tmp_f.d + arg.d, arg.d)
        ts("dve", arg.ap, arg.ap, -PI, PI, ALU.max, ALU.min, arg.d, arg.d)
        act(out, arg.ap, AF.Sin, arg.d, writes_t)

    S.dma("sp", cm.ap, I["cmat"], writes=cm.d, sem=gsem[0])
    cp("dve", bd64.ap, cm[:, 5, :], cm.d, bd64.d)
    S.op("pool", lambda e: e.memset(onesm.ap, 1.0 / D), writes=onesm.d)
    m0 = A.mark()
    if stop == -3:
        S.emit(); return nc, st, S
    sprow = A.alloc(F32, [128, 128])
    for k in range(SP_N // 128):
        S.dma("sp", sprow.ap, I["sp"][k * 128:(k + 1) * 128, :], writes=sprow.d, sem=gsem[1])
        p = nps()
        S.op("pe", lambda e, p=p: e.transpose(p[:, 0:128], sprow.ap, ident), reads=sprow.d + cm.d, writes=p.d)
        cp("dve", colT[:, k * 128:(k + 1) * 128], p[:, 0:128], p.d, colT.d)
    if stop == -2:
        S.emit(); return nc, st, S
    scT = A.alloc(BF16, [128, 8, 128])
    S.op("pool", lambda e: e.memset(scT.ap.rearrange("p a b -> p (a b)"), 0.0), writes=scT.d)
    act(scT[:, :, 0], C(SP_ROWS["c"], 8), AF.Silu, colT.d, scT.d)
    act(scT[:, :, 1], C(SP_ROWS["cctx"], 8), AF.Silu, colT.d, scT.d)
    tmpc = A.alloc(F32, [128, 16])
    for l in range(NL):
        lam = C(SP_ROWS["rglam%d" % l], 4)
        act(tmpc[:, 0:4], lam, AF.Exp, colT.d, tmpc.d, scale=-1.0)
        act(tmpc[:, 4:8], tmpc[:, 0:4], AF.Ln, tmpc.d, tmpc.d, bias=1.0)
        nv = nsp[:, l, :, :, :].rearrange("p d c k -> p (d c) k")
        ts("dve", nv[:, :, 0], tmpc[:, 4:8], -8.0, None, ALU.mult, None, tmpc.d, nsp.d)
        ts("dve", nv[:, :, 1], tmpc[:, 4:8], -16.0, None, ALU.mult, None, tmpc.d, nsp.d)
    S.op("pool", lambda e: e.memset(lbc[:, 0, :, 0], 0.0), writes=lbc.d)
    S.op("pool", lambda e: e.memset(lbc[:, 0, :, 1], 1.0), writes=lbc.d)
    tt("dve", tmpc[:, 8:10], C(SP_ROWS["hglow1"], 2), C(SP_ROWS["hglow0"], 2), ALU.subtract, colT.d, tmpc.d)
    act(lbc[:, 1, :, 0], tmpc[:, 8:10], AF.Sigmoid, tmpc.d, lbc.d)
    act(lbc[:, 1, :, 1], tmpc[:, 8:10], AF.Sigmoid, tmpc.d, lbc.d, scale=-1.0)
    if stop == -1:
        S.emit(); return nc, st, S
    modrow = A.alloc(F32, [128, 6 * D])
    for l in range(NL):
        for g in range(12):
            w, wd = load_w(I["w_mod"][l], 0, D, g * 512, 512)
            pm = nps()
            for c in range(8):
                mm(pm.ap, scT[:, c, :], w[:, c, :], c == 0, c == 7, wd + scT.d, pm.d, last=(c == 7))
            cp("dve", modrow[:, g * 512:(g + 1) * 512], pm.ap, pm.d, modrow.d)
        if stop == -0.6:
            S.barrier(); S.emit(); return nc, st, S
        for g in range(12):
            pt = nps()
            for j in range(4):
                blk = g * 4 + j
                S.op("pe", lambda e, pt=pt, j=j, blk=blk: e.transpose(pt[:, j * 128:(j + 1) * 128], modrow[:, blk * 128:(blk + 1) * 128], ident),
                     reads=modrow.d + cm.d, writes=pt.d, inc=(j == 3))
            cp("dve", modc[:, l, g * 4:g * 4 + 4, :], pt.ap.rearrange("p (j n) -> p j n", j=4)[:, :, 0:2], pt.d, modc.d)
        if stop == -0.4:
            S.barrier(); S.emit(); return nc, st, S
        bm = C(SP_ROWS["bmod%d" % l], 48)
        tt("dve", modc[:, l, :, :], modc[:, l, :, :], bm.unsqueeze(2).to_broadcast([128, 48, 2]), ALU.add, modc.d + colT.d, modc.d)
        if stop == -0.2:
            S.barrier(); S.emit(); return nc, st, S
        for wn, (grow, scoff) in enumerate(((SP_ROWS["n1g%d" % l], 8), (SP_ROWS["n2g%d" % l], 32))):
            for part in range(2):
                stt(acol[:, l, wn, part, :], modc[:, l, scoff:scoff + 8, part], 1.0, C(grow, 8), ALU.add, ALU.mult,
                    modc.d + colT.d, acol.d)
    A.release(m0)
    S.barrier()


    def seg_of(L, t):
        if L >= 512:
            return [((t * 512) // L, (t * 512) % L, 512, 0)]
        k = 512 // L
        return [(t * k + j, 0, L, j * L) for j in range(k)]

    def proj_fm(w, wd, col0, ncol, t):
        p = nps()
        for c in range(8):
            mm(p[0:ncol, :], w[:, c, col0:col0 + ncol], uT[:, c, t * 512:(t + 1) * 512], c == 0, c == 7, wd + [uT.d[t]], p.d, last=(c == 7))
        return p

    def mixer_rg(pi, l, L, NS, yTs, yTs_d, ysem):
        m = A.mark()
        Tn = L * NS; NT = Tn // 512
        w, wd = load_w(I["w_in"][l], 0, D, IN_OFF["b_x"], 512)
        bd = A.alloc(BF16, [128, 8, 128])
        S.op("pool", lambda e: e.memset(bd.ap.rearrange("p a b -> p (a b)"), 0.0), writes=bd.d)
        bsem = SPL[0]
        for gate, key in enumerate(("rg_wa", "rg_wx")):
            for d in range(2):
                for ct in range(2):
                    for blk in range(2):
                        S.dma("pool", bd[blk * 64:(blk + 1) * 64, (gate * 2 + d) * 2 + ct, blk * 64:(blk + 1) * 64],
                              I[key][l, d, ct * 2 + blk], writes=bd.d, sem=bsem)
        xpad = A.alloc(F32, [128, NS, L + 3]); xc = A.alloc(F32, [128, NS, L]); xcb = A.alloc(BF16, [128, NS, L])
        hf = A.alloc(F32, [128, NS, L])
        h0 = A.alloc(F32, [128, 2])
        NR = 2
        tmps = [dict((k, A.alloc(F32, [128, 512])) for k in ("r", "i", "a", "u", "hb", "g")) for _ in range(NR)]
        yb = [A.alloc(BF16, [128, 512]) for _ in range(2)]
        hsem = SPL[1]; ssem = SPL[2]; ysems = [SPL[3], SPL[4]]
        rcw = SP_ROWS["rgcw%d" % l]; rcb = SP_ROWS["rgcb%d" % l]; rba = SP_ROWS["rgba%d" % l]; rbx = SP_ROWS["rgbx%d" % l]
        for ct in range(2):
            S.op("pool", lambda e: e.memset(xpad.ap.rearrange("p a b -> p (a b)"), 0.0), writes=xpad.d)
            if pi == 0:
                for d in range(2):
                    S.dma("sp", h0[:, d:d + 1], I["st_rg"][l, d, ct * 128:(ct + 1) * 128].rearrange("(p o) -> p o", o=1), writes=h0.d, sem=hsem)
            else:
                S.op("pool", lambda e: e.memset(h0.ap, 0.0), writes=h0.d)
            for t in range(NT):
                p = proj_fm(w, wd, ct * 128, 128, t)
                for (s_, off, n, a0) in seg_of(L, t):
                    cp("act", xpad[:, s_, 2 + off:2 + off + n], p[:, a0:a0 + n], p.d, xpad.d)
            for s_ in range(NS):
                ts("dve", xc[:, s_, :], xpad[:, s_, 0:L], C(rcw + 0 * 2 + ct), C(rcb + ct), ALU.mult, ALU.add, xpad.d + colT.d, xc.d)
                for j in range(1, 4):
                    stt(xc[:, s_, :], xpad[:, s_, j:j + L], C(rcw + j * 2 + ct), xc[:, s_, :], ALU.mult, ALU.add, xpad.d + colT.d + xc.d, xc.d)
                cp("act", xcb[:, s_, :], xc[:, s_, :], xc.d, xcb.d)

            def gates(d, s_, off, n, tm):
                sl_ = slice(off, off + n)
                pr = nps(); pq = nps()
                mm(pr[:, 0:n], bd[:, (0 * 2 + d) * 2 + ct, :], xcb[:, s_, sl_], True, True, bd.d + xcb.d, pr.d)
                mm(pq[:, 0:n], bd[:, (1 * 2 + d) * 2 + ct, :], xcb[:, s_, sl_], True, True, bd.d + xcb.d, pq.d)
                act(tm["r"][:, 0:n], pr[:, 0:n], AF.Sigmoid, pr.d + colT.d, tm["r"].d, bias=C(rba + d * 2 + ct))
                act(tm["i"][:, 0:n], pq[:, 0:n], AF.Sigmoid, pq.d + colT.d, tm["i"].d, bias=C(rbx + d * 2 + ct))
                act(tm["a"][:, 0:n], tm["r"][:, 0:n], AF.Exp, tm["r"].d + nsp.d, tm["a"].d, scale=nsp[:, l, d, ct, 0:1])
                act(tm["u"][:, 0:n], tm["r"][:, 0:n], AF.Exp, tm["r"].d + nsp.d, tm["u"].d, scale=nsp[:, l, d, ct, 1:2])
                ts("dve", tm["u"][:, 0:n], tm["u"][:, 0:n], -1.0, 1.0, ALU.mult, ALU.add, tm["u"].d, tm["u"].d)
                act(tm["u"][:, 0:n], tm["u"][:, 0:n], AF.Sqrt, tm["u"].d, tm["u"].d)
                tt("pool", tm["i"][:, 0:n], tm["i"][:, 0:n], xc[:, s_, sl_], ALU.mult, tm["i"].d + xc.d, tm["i"].d)
                tt("dve", tm["u"][:, 0:n], tm["u"][:, 0:n], tm["i"][:, 0:n], ALU.mult, tm["u"].d + tm["i"].d, tm["u"].d)

            k = 0
            for t in range(NT):
                for (s_, off, n, a0) in seg_of(L, t):
                    tm = tmps[k % NR]; k += 1
                    gates(0, s_, off, n, tm)
                    init = h0[:, 0:1] if off == 0 else hf[:, s_, off - 1:off]
                    S.op("dve", lambda e, tm=tm, s_=s_, off=off, n=n, init=init: e.tensor_tensor_scan(
                        out=hf[:, s_, off:off + n], data0=tm["a"][:, 0:n], data1=tm["u"][:, 0:n], initial=init, op0=ALU.mult, op1=ALU.add),
                        reads=tm["a"].d + tm["u"].d + hf.d + h0.d, writes=hf.d)
                    if pi == 1 and off + n == L:
                        dd = Dep()
                        S.dma("sp", O["ns_rg"][s_, l, 0, ct * 128:(ct + 1) * 128].rearrange("(p o) -> p o", o=1), hf[:, s_, L - 1:L],
                              reads=hf.d, writes=[dd], sem=ssem)
                        out_deps.append(dd)
            prev = None
            for t in reversed(range(NT)):
                pg = proj_fm(w, wd, 256 + ct * 128, 128, t)
                for (s_, off, n, a0) in seg_of(L, t):
                    tm = tmps[k % NR]; k += 1
                    gates(1, s_, off, n, tm)
                    init = h0[:, 1:2] if off + n == L else prev["hb"][:, 0:1]
                    rd = tm["a"].d + tm["u"].d + h0.d + (prev["hb"].d if prev is not None else [])
                    S.op("dve", lambda e, tm=tm, n=n, init=init: e.tensor_tensor_scan(
                        out=tm["hb"][:, n - 1::-1] if False else tm["hb"][:, 0:n][:, ::-1], data0=tm["a"][:, 0:n][:, ::-1], data1=tm["u"][:, 0:n][:, ::-1],
                        initial=init, op0=ALU.mult, op1=ALU.add), reads=rd, writes=tm["hb"].d)
                    prev = tm
                    if pi == 1 and off == 0:
                        dd = Dep()
                        S.dma("sp", O["ns_rg"][s_, l, 1, ct * 128:(ct + 1) * 128].rearrange("(p o) -> p o", o=1), tm["hb"][:, 0:1],
                              reads=tm["hb"].d, writes=[dd], sem=ssem)
                        out_deps.append(dd)
                    g = tm["g"]; ps_ = slice(a0, a0 + n)
                    act(g[:, 0:n], pg[:, ps_], AF.Square, pg.d, g.d)
                    ts("dve", g[:, 0:n], g[:, 0:n], 0.044715, 1.0, ALU.mult, ALU.add, g.d, g.d)
                    tt("dve", g[:, 0:n], g[:, 0:n], pg[:, ps_], ALU.mult, g.d + pg.d, g.d)
                    act(g[:, 0:n], g[:, 0:n], AF.Sigmoid, g.d, g.d, scale=1.5957691216057308)
                    tt("dve", g[:, 0:n], g[:, 0:n], pg[:, ps_], ALU.mult, g.d + pg.d, g.d)
                    tt("pool", tm["r"][:, 0:n], tm["hb"][:, 0:n], hf[:, s_, off:off + n], ALU.add, tm["hb"].d + hf.d, tm["r"].d)
                    y_ = yb[t % 2]
                    tt("dve", y_[:, a0:a0 + n], tm["r"][:, 0:n], g[:, 0:n], ALU.mult, tm["r"].d + g.d, y_.d)
                S.dma("sp", yTs[2 + ct, :, t * 512:(t + 1) * 512], yb[t % 2].ap, reads=yb[t % 2].d, writes=[yTs_d[t]], sem=ysems[t % 2])
        A.release(m)
        S.barrier()


    def mixer_gated(pi, l, L, NS, yTs, yTs_d, ysem, kind):
        m = A.mark()
        CH = 128
        Tn = L * NS; NT = Tn // 512; NCH = Tn // CH; CPS = L // CH
        gla = (kind == "gla")
        yc0 = 0 if gla else 6
        st_in = I["st_gla"] if gla else I["st_hg"]
        st_out = O["ns_gla"] if gla else O["ns_hg"]
        grow = SP_ROWS[("glang%d" if gla else "hgng%d") % l]
        wl = I["w_in"][l]
        la_early = A.alloc(F32, [128, 512])
        wnext[0] = 0
        scr = wring[3].ap.rearrange("p a b -> p (a b)").bitcast(F32)
        EB = [T(scr[:, i * 512:(i + 1) * 512].rearrange("p (b c) -> p b c", c=128), 1) for i in range(4)]
        if gla:
            wA, wAd = load_w(wl, 0, D, 0, 512)
            wB, wBd = load_w(wl, 0, D, 256, 512)
            wC, wCd = load_w(wl, 0, D, 768, 272)
            wg = A.alloc(BF16, [128, 512]); bgb = A.alloc(F32, [128, 512])
            S.op("pool", lambda e: e.memset(wg.ap, 0.0), writes=wg.d)
            S.dma("pool", wg[112:128, :], I["gla_wg"][l], writes=wg.d, sem=SPL[0])
            S.dma("sp", bgb.ap, I["gla_bg"][l].partition_broadcast(128), writes=bgb.d, sem=SPL[1])
            lrT = A.alloc(BF16, [128, 512])
        else:
            wA, wAd = load_w(wl, 0, D, 2320, 512)
            wB, wBd = load_w(wl, 0, D, 2832, 512)
            wC, wCd = load_w(wl, 0, D, 3344, 256)
            lb2 = A.alloc(F32, [128, 256]); om2 = A.alloc(F32, [128, 256])
            if l == 0:
                S.op("pool", lambda e: e.memset(lb2.ap, 0.0), writes=lb2.d)
                S.op("pool", lambda e: e.memset(om2.ap, 1.0), writes=om2.d)
            else:
                hl = T(la_early.ap.rearrange("p (a b) -> p a b", a=2), 0); hl.d = la_early.d
                S.dma("sp", hl[:, 0, :], I["hg_low"][0].partition_broadcast(128), writes=hl.d, sem=SPL[0])
                S.dma("sp", hl[:, 1, :], I["hg_low"][1].partition_broadcast(128), writes=hl.d, sem=SPL[1])
                tt("dve", hl[:, 0, :], hl[:, 1, :], hl[:, 0, :], ALU.subtract, hl.d, hl.d)
                act(lb2.ap, hl[:, 0, :], AF.Sigmoid, hl.d, lb2.d)
                act(om2.ap, hl[:, 0, :], AF.Sigmoid, hl.d, om2.d, scale=-1.0)
        oT = A.alloc(F32, [128, 2, Tn]); qbb = A.alloc(BF16, [128, 2, Tn])
        kvb = A.alloc(F32, [128, NCH, 2, 64]); decb = A.alloc(F32, [128, NCH, 2])
        Sf = A.alloc(F32, [128, 2, 64]); Sb = A.alloc(F32, [128, 2, 64]); Ss = A.alloc(BF16, [128, 2, 64])
        qT = A.alloc(BF16, [128, 2, 512])
        kT = [A.alloc(BF16, [128, 2, 512])] if gla else [A.alloc(BF16, [128, 2, 512]) for _ in range(2)]
        ktok = A.alloc(F32, [128, 512]); vbf = A.alloc(BF16, [128, 256]); la = la_early
        xs = A.alloc(F32, [128, 4, CH]); eq = xs; ekn = A.alloc(F32, [128, 4, CH]); ek = A.alloc(F32, [128, 512])
        dec = A.alloc(F32, [128, 4])
        qhf = A.alloc(BF16, [128, 2, CH])
        Q1 = [A.alloc(BF16, [128, 2, CH]) for _ in range(2)]; ktl = [A.alloc(BF16, [128, 2, CH]) for _ in range(2)]
        Q2 = [A.alloc(BF16, [128, 2, CH]) for _ in range(2)]; K2 = [A.alloc(BF16, [128, 2, CH]) for _ in range(2)]
        khat = [A.alloc(BF16, [128, 256]) for _ in range(2)]; Am = [A.alloc(BF16, [128, 4, CH]) for _ in range(2)]
        tri = (tril, triu)
        ssem = SPL[2]; isem = SPL[3]
        mask8 = [A.alloc(U8, [128, 4, CH]) for _ in range(2)]
        for d in range(2):
            cp("dve", mask8[d].ap, tri[d].unsqueeze(1).to_broadcast([128, 4, CH]), cm.d, mask8[d].d)
            S.op("pool", lambda e, d=d: e.memset(Am[d].ap.rearrange("p a b -> p (a b)"), 0.0), writes=Am[d].d)

        def proj_tm(w, wd, col0, ncol, ch):
            p = nps()
            for c in range(8):
                mm(p[:, 0:ncol], uT[:, c, ch * CH:(ch + 1) * CH], w[:, c, col0:col0 + ncol], c == 0, c == 7, wd + [uT.d[ch // 4]], p.d, last=(c == 7))
            return p

        def init_state(Sx, s_, d):
            if pi == 0:
                S.dma("sp", Sx.ap, st_in[l, d].rearrange("(h p) v -> p h v", p=128), writes=Sx.d, sem=(isem if d == 0 else SPL[20]))
            else:
                S.op("pool", lambda e: e.memset(Sx.ap.rearrange("p a b -> p (a b)"), 0.0), writes=Sx.d)

        def out_state(Sx, s_, d):
            if pi == 1:
                dd = Dep()
                S.dma("sp", st_out[s_, l, d].rearrange("(h p) v -> p h v", p=128), Sx.ap, reads=Sx.d, writes=[dd], sem=ssem)
                out_deps.append(dd)

        for t in range(NT):
            if gla:
                for hp in range(2):
                    p = proj_fm(wA, wAd, hp * 128, 128, t)
                    act(qT[:, hp, :], p.ap, AF.Identity, p.d, qT.d, scale=0.125, bias=0.0)
                    p = proj_fm(wA, wAd, 256 + hp * 128, 128, t)
                    cp("dve", kT[0][:, hp, :], p.ap, p.d, kT[0].d)
                p = proj_fm(wC, wCd, 144, 128, t)
                cp("dve", lrT.ap, p.ap, p.d, lrT.d)
            else:
                for hp in range(2):
                    p = proj_fm(wA, wAd, hp * 128, 128, t)
                    act(qT[:, hp, :], p.ap, AF.Silu, p.d, qT.d)
                    p = proj_fm(wA, wAd, 256 + hp * 128, 128, t)
                    act(kT[0][:, hp, :], p.ap, AF.Sigmoid, p.d, kT[0].d, scale=-1.0)
                    ts("dve", kT[0][:, hp, :], kT[0][:, hp, :], lbc[:, l, hp, 1:2], None, ALU.mult, None, kT[0].d + lbc.d, kT[0].d)
                    p = proj_fm(wB, wBd, hp * 128, 128, t)
                    act(kT[1][:, hp, :], p.ap, AF.Sigmoid, p.d, kT[1].d, scale=-1.0)
                    ts("dve", kT[1][:, hp, :], kT[1][:, hp, :], lbc[:, l, hp, 1:2], None, ALU.mult, None, kT[1].d + lbc.d, kT[1].d)
            for ci in range(4):
                ch = t * 4 + ci; s_ = ch // CPS; cpos = ch % CPS
                cs = slice(ci * CH, (ci + 1) * CH); gs = slice(ch * CH, (ch + 1) * CH)
                if cpos == 0:
                    init_state(Sf, s_, 0)
                if GSTOP == 2:
                    raise _Stop()
                if gla:
                    p = proj_tm(wB, wBd, 0, 512, ch)
                    cp("act", ktok[:, 0:256], p[:, 0:256], p.d, ktok.d)
                    cp("dve", vbf.ap, p[:, 256:512], p.d, vbf.d)
                    if GSTOP == 21:
                        raise _Stop()
                    pz = nps()
                    mm(pz.ap, lrT[:, cs], wg.ap, True, True, lrT.d + wg.d, pz.d)
                    tt("dve", la.ap, pz.ap, bgb.ap, ALU.add, pz.d + bgb.d, la.d)
                    if GSTOP == 22:
                        raise _Stop()
                    act(la.ap, la.ap, AF.Exp, la.d, la.d, scale=-1.0)
                    act(la.ap, la.ap, AF.Ln, la.d, la.d, bias=1.0)
                    ts("dve", la.ap, la.ap, -1.0 / 16.0, None, ALU.mult, None, la.d, la.d)
                    kt_d = [ktok[:, 0:256], ktok[:, 0:256]]
                else:
                    p = nps()
                    for c in range(8):
                        mm(p[:, 0:256], uT[:, c, ch * CH:(ch + 1) * CH], wA[:, c, 256:512], c == 0, c == 7, wAd + [uT.d[ch // 4]], p.d)
                    for c in range(8):
                        mm(p[:, 256:512], uT[:, c, ch * CH:(ch + 1) * CH], wB[:, c, 0:256], c == 0, c == 7, wBd + [uT.d[ch // 4]], p.d)
                    act(la.ap, p.ap, AF.Sigmoid, p.d, la.d)
                    la3 = la.ap.rearrange("p (a b) -> p a b", a=2); kt3 = ktok.ap.rearrange("p (a b) -> p a b", a=2)
                    omB = om2.ap.unsqueeze(1).to_broadcast([128, 2, 256]); lbB = lb2.ap.unsqueeze(1).to_broadcast([128, 2, 256])
                    tt("dve", la3, la3, omB, ALU.mult, la.d + om2.d, la.d)
                    tt("dve", kt3, omB, la3, ALU.subtract, la.d + om2.d, ktok.d)
                    tt("dve", la3, la3, lbB, ALU.add, la.d + lb2.d, la.d)
                    act(la.ap, la.ap, AF.Ln, la.d, la.d)
                    p = proj_tm(wB, wBd, 256, 256, ch)
                    cp("dve", vbf.ap, p[:, 0:256], p.d, vbf.d)
                    kt_d = [ktok[:, 0:256], ktok[:, 256:512]]
                if GSTOP == 3:
                    raise _Stop()
                pb = nps(); pt = nps()
                for d in range(2):
                    for hp in range(2):
                        blk = d * 2 + hp
                        mm(pb[:, blk * CH:(blk + 1) * CH], la[:, d * 256 + hp * 128:d * 256 + (hp + 1) * 128], tri[d], True, True, la.d + cm.d, pb.d)
                mm(pt[:, 0:256], su, la[:, 0:256], True, True, la.d + cm.d, pt.d)
                mm(pt[:, 256:512], sl, la[:, 256:512], True, True, la.d + cm.d, pt.d)
                H2 = CH // 2
                pbv = pb.ap.rearrange("p (b c) -> p b c", c=CH)
                for d in range(2):
                    endc = CH - 1 if d == 0 else 0
                    act(dec[:, d * 2:d * 2 + 2], pbv[:, d * 2:d * 2 + 2, endc], AF.Exp, pb.d, dec.d)
                act(ek.ap, pt.ap, AF.Exp, pt.d, ek.d)
                for d in range(2):
                    kTd = kT[0] if gla else kT[d]
                    tt("dve", khat[d].ap, kt_d[d], ek[:, d * 256:(d + 1) * 256], ALU.mult, ktok.d + ek.d, khat[d].d)
                x2 = EB[0]
                for d in range(2):
                    bcol = H2 - 1 + d
                    for hp in range(2):
                        blk = d * 2 + hp
                        for hf_ in range(2):
                            mid = hf_ * H2 + H2 // 2 - 1 + d
                            cols = slice(hf_ * H2, (hf_ + 1) * H2)
                            ts("dve", xs[:, blk, cols], pb[:, blk * CH + hf_ * H2:blk * CH + (hf_ + 1) * H2], pb[:, blk * CH + mid:blk * CH + mid + 1], -80.0,
                               ALU.subtract, ALU.max, pb.d, xs.d)
                        ts("dve", x2[:, blk, :], pb[:, blk * CH:(blk + 1) * CH], pb[:, blk * CH + bcol:blk * CH + bcol + 1], -80.0,
                           ALU.subtract, ALU.max, pb.d, x2.d)
                ts("dve", xs.ap, xs.ap, 80.0, None, ALU.min, None, xs.d, xs.d)
                ts("dve", x2.ap, x2.ap, 80.0, None, ALU.min, None, x2.d, x2.d)
                e1p = ekn; e1n = EB[1]; e2p = EB[2]; e2n = EB[3]
                act(e1p.ap, xs.ap, AF.Exp, xs.d, e1p.d)
                act(e1n.ap, xs.ap, AF.Exp, xs.d, e1n.d, scale=-1.0)
                act(e2p.ap, x2.ap, AF.Exp, x2.d, e2p.d)
                act(e2n.ap, x2.ap, AF.Exp, x2.d, e2n.d, scale=-1.0)
                for d in range(2):
                    kTd = kT[0] if gla else kT[d]
                    tt("pool" if d else "dve", Q1[d].ap, qT[:, :, cs], e1p[:, d * 2:d * 2 + 2, :], ALU.mult, qT.d + e1p.d, Q1[d].d)
                    tt("pool" if d else "dve", ktl[d].ap, kTd[:, :, cs], e1n[:, d * 2:d * 2 + 2, :], ALU.mult, kTd.d + e1n.d, ktl[d].d)
                    tt("pool", Q2[d].ap, qT[:, :, cs], e2p[:, d * 2:d * 2 + 2, :], ALU.mult, qT.d + e2p.d, Q2[d].d)
                    tt("pool" if d else "dve", K2[d].ap, kTd[:, :, cs], e2n[:, d * 2:d * 2 + 2, :], ALU.mult, kTd.d + e2n.d, K2[d].d)
                ebq = xs
                act(ebq.ap, pbv, AF.Exp, pb.d + xs.d, ebq.d)
                tt("dve", qhf.ap, qT[:, :, cs], ebq[:, 0:2, :], ALU.mult, qT.d + ebq.d, qhf.d)
                tt("pool", qbb[:, :, gs], qT[:, :, cs], ebq[:, 2:4, :], ALU.mult, qT.d + ebq.d, qbb.d)
                if GSTOP == 5:
                    raise _Stop()
                for d in range(2):
                    psc = nps()
                    for h in (0, 2, 1, 3):
                        hp, j = h // 2, h % 2
                        r_ = slice(j * 64, (j + 1) * 64)
                        lo = slice(0, H2); hi = slice(H2, CH)
                        mm(psc[0:H2, h * CH:h * CH + H2], ktl[d][r_, hp, lo], Q1[d][r_, hp, lo], True, True, ktl[d].d + Q1[d].d, psc.d, rg=j, mode=(64, 64))
                        mm(psc[H2:CH, h * CH + H2:(h + 1) * CH], ktl[d][r_, hp, hi], Q1[d][r_, hp, hi], True, True, ktl[d].d + Q1[d].d, psc.d, rg=j, mode=(64, 64))
                        if d == 0:
                            mm(psc[0:H2, h * CH + H2:(h + 1) * CH], K2[0][r_, hp, lo], Q2[0][r_, hp, hi], True, True, K2[0].d + Q2[0].d, psc.d, rg=j, mode=(64, 64))
                        else:
                            mm(psc[H2:CH, h * CH:h * CH + H2], K2[1][r_, hp, hi], Q2[1][r_, hp, lo], True, True, K2[1].d + Q2[1].d, psc.d, rg=j, mode=(64, 64))
                    S.op("dve", lambda e, d=d, psc=psc: e.copy_predicated(out=Am[d].ap, mask=mask8[d].ap, data=psc.ap.rearrange("p (h c) -> p h c", c=CH)),
                         reads=psc.d + mask8[d].d + Am[d].d, writes=Am[d].d)
                if GSTOP == 6:
                    raise _Stop()
                cp("dve", Ss.ap, Sf.ap, Sf.d, Ss.d)
                po = nps(); pq = nps()
                for h in range(4):
                    hp, j = h // 2, h % 2
                    o_ = po[j * 64:(j + 1) * 64, hp * CH:(hp + 1) * CH]
                    mm(o_, vbf[:, h * 64:(h + 1) * 64], Am[0][:, h, :], True, False, vbf.d + Am[0].d, po.d, mode=(128, 64))
                    mm(o_, vbf[:, h * 64:(h + 1) * 64], Am[1][:, h, :], False, True, vbf.d + Am[1].d, po.d, mode=(128, 64))
                for h in (0, 2, 1, 3):
                    hp, j = h // 2, h % 2
                    mm(pq[j * 64:(j + 1) * 64, hp * CH:(hp + 1) * CH], Ss[j * 64:(j + 1) * 64, hp, :], qhf[j * 64:(j + 1) * 64, hp, :], True, True,
                       Ss.d + qhf.d, pq.d, rg=j, mode=(64, 64))
                cp("act", oT[:, :, gs], po[:, 0:2 * CH].rearrange("p (h c) -> p h c", c=CH), po.d, oT.d)
                tt("dve", oT[:, :, gs], oT[:, :, gs], pq[:, 0:2 * CH].rearrange("p (h c) -> p h c", c=CH), ALU.add, oT.d + pq.d, oT.d)
                if GSTOP == 7:
                    raise _Stop()
                pkv = nps()
                for d in range(2):
                    for hp in range(2):
                        blk = d * 2 + hp
                        mm(pkv[:, blk * 128:(blk + 1) * 128], khat[d][:, hp * 128:(hp + 1) * 128], vbf[:, hp * 128:(hp + 1) * 128], True, True,
                           khat[d].d + vbf.d, pkv.d)
                for hp in range(2):
                    for j in range(2):
                        r_ = slice(j * 64, (j + 1) * 64)
                        stt(Sf[r_, hp, :], Sf[r_, hp, :], dec[r_, hp:hp + 1], pkv[r_, hp * 128 + j * 64:hp * 128 + (j + 1) * 64], ALU.mult, ALU.add,
                            Sf.d + dec.d + pkv.d, Sf.d)
                        cp("act", kvb[r_, ch, hp, :], pkv[r_, (2 + hp) * 128 + j * 64:(2 + hp) * 128 + (j + 1) * 64], pkv.d, kvb.d)
                cp("act", decb[:, ch, :], dec[:, 2:4], dec.d, decb.d)
                if cpos == CPS - 1:
                    out_state(Sf, s_, 0)
                if GSTOP == 8:
                    raise _Stop()
        if GSTOP == 9:
            raise _Stop()
        for ch in reversed(range(NCH)):
            s_ = ch // CPS; cpos = ch % CPS
            gs = slice(ch * CH, (ch + 1) * CH)
            if cpos == CPS - 1:
                init_state(Sb, s_, 1)
            cp("dve", Ss.ap, Sb.ap, Sb.d, Ss.d)
            po = nps()
            for h in (0, 2, 1, 3):
                hp, j = h // 2, h % 2
                mm(po[j * 64:(j + 1) * 64, hp * CH:(hp + 1) * CH], Ss[j * 64:(j + 1) * 64, hp, :], qbb[j * 64:(j + 1) * 64, hp, gs], True, True,
                   Ss.d + qbb.d, po.d, rg=j, mode=(64, 64))
            tt("dve", oT[:, :, gs], oT[:, :, gs], po[:, 0:2 * CH].rearrange("p (h c) -> p h c", c=CH), ALU.add, oT.d + po.d, oT.d)
            for hp in range(2):
                stt(Sb[:, hp, :], Sb[:, hp, :], decb[:, ch, hp:hp + 1], kvb[:, ch, hp, :], ALU.mult, ALU.add, Sb.d + decb.d + kvb.d, Sb.d)
            if cpos == 0:
                out_state(Sb, s_, 1)
        if GSTOP == 10:
            raise _Stop()
        sg = la; t1 = ek; rs = ktok
        yb = [T(Am[i].ap.rearrange("p a b -> p (a b)"), 0) for i in range(2)]; yb[0].d = Am[0].d; yb[1].d = Am[1].d
        A_sq = T(xs.ap.rearrange("p a b -> p (a b)").bitcast(BF16)[:, 0:512], 0); A_sq.d = xs.d
        ysems = [SPL[4], SPL[5]]
        k = 0
        for t in range(NT):
            for hp in range(2):
                pg = proj_fm(wC, wCd, hp * 128, 128, t)
                act(sg.ap, pg.ap, AF.Silu, pg.d, sg.d)
                o_ = oT[:, hp, t * 512:(t + 1) * 512]
                sq2 = A_sq
                act(sq2.ap, o_, AF.Square, oT.d, sq2.d)
                pm = nps()
                mm(pm.ap, bd64.ap, sq2.ap, True, True, bd64.d + sq2.d, pm.d)
                act(rs.ap, pm.ap, AF.Sqrt, pm.d, rs.d, bias=EPS)
                S.op("dve", lambda e: e.reciprocal(out=rs.ap, in_=rs.ap), reads=rs.d, writes=rs.d)
                tt("dve", t1.ap, o_, rs.ap, ALU.mult, oT.d + rs.d, t1.d)
                y_ = yb[k % 2]
                stt(y_.ap, t1.ap, C(grow + hp), sg.ap, ALU.mult, ALU.mult, t1.d + sg.d + colT.d, y_.d)
                S.dma("sp", yTs[yc0 + hp, :, t * 512:(t + 1) * 512], y_.ap, reads=y_.d, writes=[yTs_d[t]], sem=ysems[k % 2])
                k += 1
        A.release(m)
        S.barrier()


    def mixer_hy(pi, l, L, NS, yTs, yTs_d, ysem):
        m = A.mark()
        Tn = L * NS; NT = Tn // 512; NCHL = L // 128; NCH = Tn // 128
        N = 6144 if L == 4096 else 512
        NFB = (N // 2) // 128
        Gc = I["gc%d" % L]; Gs = I["gs%d" % L]
        wl = I["w_in"][l]
        ucs = dscr("ucs%d_%d" % (pi, l), [Tn, 768]); ucs_d = [Dep() for _ in range(NT)]
        zscr = dscr("zscr%d_%d" % (pi, l), [Tn, 256]); zscr_d = [Dep() for _ in range(NCH)]
        ZW = NS * 256 + 256
        HC = NS * 256
        zh = A.alloc(BF16, [128, NCHL, ZW])
        h2T = A.alloc(F32, [128, L])
        decB = A.alloc(F32, [128, 512]); skipB = A.alloc(F32, [128, 512]); rsum = A.alloc(F32, [128, 512])
        ndist = A.alloc(F32, [128, NCHL]); ecs = A.alloc(F32, [128, 2, NFB])
        W3 = A.alloc(F32, [128, 512])
        S.op("pool", lambda e: e.memset(W3.ap, 0.0), writes=W3.d)
        S.op("pool", lambda e: e.memset(h2T.ap, 0.0), writes=h2T.d)
        S.dma("sp", decB.ap, I["hy_dec"][l].partition_broadcast(128), writes=decB.d, sem=SPL[0])
        S.dma("sp", skipB.ap, I["hy_skip"][l].partition_broadcast(128), writes=skipB.d, sem=SPL[1])
        S.dma("sp", ecs.ap, I["e%d" % L], writes=ecs.d, sem=SPL[2])
        S.dma("sp", W3[0:64, :], I["hy_w3"][l], writes=W3.d, sem=SPL[3])
        mf = A.mark()
        W1 = A.alloc(F32, [128, 128]); W2 = A.alloc(F32, [128, 128])
        S.op("pool", lambda e: e.memset(W1.ap, 0.0), writes=W1.d)
        S.op("pool", lambda e: e.memset(W2.ap, 0.0), writes=W2.d)
        S.dma("sp", W1[0:33, 0:64], I["hy_w1"][l], writes=W1.d, sem=SPL[4])
        S.dma("sp", W2[0:64, 0:64], I["hy_w2"][l], writes=W2.d, sem=SPL[5])
        posi = A.alloc(I32, [128, 512]); posf = A.alloc(F32, [128, 512]); arg = A.alloc(F32, [128, 512])
        ti = A.alloc(I32, [128, 512]); tf = A.alloc(F32, [128, 512]); pe = A.alloc(F32, [128, 512]); h1 = A.alloc(F32, [128, 512])
        pcr = SP_ROWS["pec%d" % L]
        S.op("pool", lambda e: e.memset(pe.ap, 0.0), writes=pe.d)
        S.op("pool", lambda e: e.memset(h1.ap, 0.0), writes=h1.d)
        for j in range(L // 512 if L >= 512 else 1):
            w_ = min(512, L)
            S.op("pool", lambda e, j=j, w_=w_: e.iota(posi[:, 0:w_], pattern=[[1, w_]], base=j * 512, channel_multiplier=0), writes=posi.d)
            cp("dve", posf[:, 0:w_], posi[:, 0:w_], posi.d, posf.d)
            ts("dve", arg[0:33, 0:w_], posf[0:33, 0:w_], C(pcr)[0:33, :], C(pcr + 1)[0:33, :], ALU.mult, ALU.add, posf.d + colT.d, arg.d)
            a33 = T(arg[0:33, 0:w_], 0); a33.d = arg.d
            i33 = T(ti[0:33, 0:w_], 0); i33.d = ti.d
            f33 = T(tf[0:33, 0:w_], 0); f33.d = tf.d
            sin_rr(pe[0:33, 0:w_], a33, i33, f33, [], pe.d)
            ts("dve", pe[0:1, 0:w_], posf[0:1, 0:w_], C(pcr)[0:1, :], None, ALU.mult, None, posf.d + colT.d + pe.d, pe.d)
            p = nps()
            mm(p[:, 0:w_], W1.ap, pe[:, 0:w_], True, True, W1.d + pe.d, p.d)
            act(arg[0:64, 0:w_], p[0:64, 0:w_], AF.Identity, p.d + colT.d, arg.d, bias=C(SP_ROWS["hyb1%d" % l])[0:64, :], scale=1.0)
            a64 = T(arg[0:64, 0:w_], 0); a64.d = arg.d
            i64 = T(ti[0:64, 0:w_], 0); i64.d = ti.d
            f64 = T(tf[0:64, 0:w_], 0); f64.d = tf.d
            sin_rr(h1[0:64, 0:w_], a64, i64, f64, [], h1.d)
            p = nps()
            mm(p[:, 0:w_], W2.ap, h1[:, 0:w_], True, True, W2.d + h1.d, p.d)
            act(arg[0:64, 0:w_], p[0:64, 0:w_], AF.Identity, p.d + colT.d, arg.d, bias=C(SP_ROWS["hyb2%d" % l])[0:64, :], scale=1.0)
            sin_rr(h2T[0:64, j * 512:j * 512 + w_], a64, i64, f64, [], h2T.d)
            if GSTOP == 30:
                raise _Stop()
        if GSTOP == 31:
            raise _Stop()
        ndi = T(posi[:, 0:NCHL], 0); ndi.d = posi.d
        S.op("pool", lambda e: e.iota(ndi.ap, pattern=[[128, NCHL]], base=0, channel_multiplier=1), writes=posi.d)
        cp("dve", ndist.ap, ndi.ap, posi.d, ndist.d)
        ts("dve", ndist.ap, ndist.ap, -float(L // 2), None, ALU.add, None, ndist.d, ndist.d)
        act(ndist.ap, ndist.ap, AF.Abs, ndist.d, ndist.d)
        ts("dve", ndist.ap, ndist.ap, -2.0 / L, None, ALU.mult, None, ndist.d, ndist.d)
        hr = pe; ee = h1; hab = A.alloc(BF16, [128, 512])
        acc = PS[7]
        for tc in range(NCHL):
            p = nps()
            mm(p.ap, h2T[:, tc * 128:(tc + 1) * 128], W3.ap, True, True, h2T.d + W3.d, p.d)
            act(ee.ap, decB.ap, AF.Exp, decB.d + ndist.d, ee.d, scale=ndist[:, tc:tc + 1])
            tt("dve", hr.ap, p.ap, ee.ap, ALU.mult, p.d + ee.d, hr.d)
            act(hab.ap, hr.ap, AF.Abs, hr.d, hab.d)
            mm(acc.ap, onesm.ap, hab.ap, tc == 0, tc == NCHL - 1, onesm.d + hab.d, acc.d)
        ts("dve", rsum.ap, acc.ap, float(D), None, ALU.mult, None, acc.d, rsum.d)
        S.op("dve", lambda e: e.reciprocal(out=rsum.ap, in_=rsum.ap), reads=rsum.d, writes=rsum.d)
        A.release(mf)
        S.barrier()
        if GSTOP == 32:
            raise _Stop()
        mc = A.mark()
        w1s, w1d = load_w(wl, 0, D, IN_OFF["c_v"], 512)
        w2s, w2d = load_w(wl, 0, D, IN_OFF["c_x2"], 256)
        xpad = A.alloc(F32, [128, NS, L + 2]); xc = A.alloc(F32, [128, NS, L])
        stg = [A.alloc(F32, [128, 4, 128]) for _ in range(2)]
        usem = [SPL[6], SPL[7]]
        hcw = SP_ROWS["hycw%d" % l]; hcb = SP_ROWS["hycb%d" % l]
        k = 0
        for ct in range(6):
            ws, wsd, c0 = (w1s, w1d, ct * 128) if ct < 4 else (w2s, w2d, (ct - 4) * 128)
            S.op("pool", lambda e: e.memset(xpad.ap.rearrange("p a b -> p (a b)"), 0.0), writes=xpad.d)
            for t in range(NT):
                p = proj_fm(ws, wsd, c0, 128, t)
                for (s_, off, n, a0) in seg_of(L, t):
                    cp("act", xpad[:, s_, 1 + off:1 + off + n], p[:, a0:a0 + n], p.d, xpad.d)
            for s_ in range(NS):
                ts("dve", xc[:, s_, :], xpad[:, s_, 0:L], C(hcw + 0 * 6 + ct), C(hcb + ct), ALU.mult, ALU.add, xpad.d + colT.d, xc.d)
                for j in (1, 2):
                    stt(xc[:, s_, :], xpad[:, s_, j:j + L], C(hcw + j * 6 + ct), xc[:, s_, :], ALU.mult, ALU.add, xpad.d + colT.d + xc.d, xc.d)
            for t in range(NT):
                p = nps()
                for ci in range(4):
                    ch = t * 4 + ci; s_ = ch // NCHL; tcs = ch % NCHL
                    S.op("pe", lambda e, p=p, ci=ci, s_=s_, tcs=tcs: e.transpose(p[:, ci * 128:(ci + 1) * 128], xc[:, s_, tcs * 128:(tcs + 1) * 128], ident),
                         reads=xc.d + cm.d, writes=p.d)
                sg_ = stg[k % 2]
                cp("act", sg_.ap, p.ap.rearrange("p (c n) -> p c n", c=4), p.d, sg_.d)
                S.dma("sp", ucs[t * 512:(t + 1) * 512, ct * 128:(ct + 1) * 128].rearrange("(c p) n -> p c n", p=128), sg_.ap,
                      reads=sg_.d, writes=[ucs_d[t]], sem=usem[k % 2])
                k += 1
                if ct < 2:
                    for ci in range(4):
                        ch = t * 4 + ci; s_ = ch // NCHL; tcs = ch % NCHL
                        cp("dve", zh[:, tcs, s_ * 256 + ct * 128:s_ * 256 + (ct + 1) * 128], sg_[:, ci, :], sg_.d, zh.d)
        A.release(mc)
        S.barrier()
        if GSTOP == 33:
            raise _Stop()
        uflat = uT.ap.rearrange("p a b -> p (a b)")
        PW = NFB * NS * 2 * 256
        P_ = T(uflat[:, 0:PW].rearrange("p (f s r n) -> p f s r n", f=NFB, s=NS, r=2), 1)
        tabs = [T(uflat[:, 12288 + i * 4096:12288 + (i + 1) * 4096].rearrange("p (c j) -> p c j", j=128), 1) for i in range(4)]
        tsem = [SPL[8 + i] for i in range(4)]
        tn_ = [0]

        def load_tab(G, blk, nchunk):
            i = tn_[0] % 4
            tn_[0] += 1
            tb_ = tabs[i]
            S.dma("sp", tb_[:, 0:nchunk, :], G[blk][:, 0:nchunk, :], writes=tb_.d, sem=tsem[i])
            return tb_

        mg = A.mark()
        AB = A.alloc(F32, [128, 2, 256]); tA = A.alloc(F32, [128, 256]); tB = A.alloc(F32, [128, 256])
        gt = [A.alloc(F32, [128, 256]) for _ in range(2)]; zt = [A.alloc(F32, [128, 256]) for _ in range(2)]
        gsm = [SPL[12], SPL[13]]; zsm = [SPL[14], SPL[15]]; zssem = [SPL[16], SPL[17]]; ysm = [SPL[18], SPL[19]]
        zn = [A.alloc(F32, [128, 256]) for _ in range(2)]
        ystg = [A.alloc(BF16, [128, 2, 512]) for _ in range(2)]
        hr2 = A.alloc(F32, [128, 256]); ee2 = A.alloc(F32, [128, 256])
        for n in range(2):
            nc0 = n * 256
            if GSTOP == 35 and n == 1:
                raise _Stop()
            for tc in range(NCHL):
                p = nps()
                mm(p[:, 0:256], h2T[:, tc * 128:(tc + 1) * 128], W3[:, nc0:nc0 + 256], True, True, h2T.d + W3.d, p.d)
                act(ee2.ap, decB[:, nc0:nc0 + 256], AF.Exp, decB.d + ndist.d, ee2.d, scale=ndist[:, tc:tc + 1])
                tt("dve", hr2.ap, p[:, 0:256], ee2.ap, ALU.mult, p.d + ee2.d, hr2.d)
                tt("dve", zh[:, tc, HC:HC + 256], hr2.ap, rsum[:, nc0:nc0 + 256], ALU.mult, hr2.d + rsum.d, zh.d)
            for fb in range(NFB):
                tcb = load_tab(Gc, fb, NCHL); tsb = load_tab(Gs, fb, NCHL)
                pcm = psm = None
                if NS == 1:
                    pcm = nps(); psm = nps()
                    for tc in range(NCHL):
                        mm(pcm.ap, tcb[:, tc, :], zh[:, tc, 0:512], tc == 0, tc == NCHL - 1, tcb.d + zh.d, pcm.d, last=(tc == NCHL - 1))
                    for tc in range(NCHL):
                        mm(psm.ap, tsb[:, tc, :], zh[:, tc, 0:512], tc == 0, tc == NCHL - 1, tsb.d + zh.d, psm.d, last=(tc == NCHL - 1))
                for g in [NS] + list(range(NS)):
                    c0 = g * 256
                    if NS == 1:
                        pc = T(pcm[:, c0:c0 + 256], 0); pc.d = pcm.d
                        ps_ = T(psm[:, c0:c0 + 256], 0); ps_.d = psm.d
                    else:
                        pc = nps(); ps_ = nps()
                        for tc in range(NCHL):
                            mm(pc[:, 0:256], tcb[:, tc, :], zh[:, tc, c0:c0 + 256], tc == 0, tc == NCHL - 1, tcb.d + zh.d, pc.d, last=(tc == NCHL - 1))
                        for tc in range(NCHL):
                            mm(ps_[:, 0:256], tsb[:, tc, :], zh[:, tc, c0:c0 + 256], tc == 0, tc == NCHL - 1, tsb.d + zh.d, ps_.d, last=(tc == NCHL - 1))
                    ec = ecs[:, 0, fb:fb + 1]; es = ecs[:, 1, fb:fb + 1]
                    if g == NS:
                        ts("dve", AB[:, 0, :], pc[:, 0:256], ec, None, ALU.mult, None, pc.d + ecs.d, AB.d)
                        stt(AB[:, 0, :], ps_[:, 0:256], es, AB[:, 0, :], ALU.mult, ALU.add, ps_.d + ecs.d + AB.d, AB.d)
                        ts("dve", AB[:, 1, :], pc[:, 0:256], es, None, ALU.mult, None, pc.d + ecs.d, AB.d)
                        ts("dve", tA.ap, ps_[:, 0:256], ec, None, ALU.mult, None, ps_.d + ecs.d, tA.d)
                        tt("dve", AB[:, 1, :], AB[:, 1, :], tA.ap, ALU.subtract, AB.d + tA.d, AB.d)
                    else:
                        tt("dve", tA.ap, pc[:, 0:256], AB[:, 0, :], ALU.mult, pc.d + AB.d, tA.d)
                        tt("dve", tB.ap, ps_[:, 0:256], AB[:, 1, :], ALU.mult, ps_.d + AB.d, tB.d)
                        tt("pool", P_[:, fb, g, 0, :], tA.ap, tB.ap, ALU.add, tA.d + tB.d, P_.d)
                        tt("dve", tA.ap, ps_[:, 0:256], AB[:, 0, :], ALU.mult, ps_.d + AB.d, tA.d)
                        tt("dve", tB.ap, pc[:, 0:256], AB[:, 1, :], ALU.mult, pc.d + AB.d, tB.d)
                        tt("pool", P_[:, fb, g, 1, :], tA.ap, tB.ap, ALU.subtract, tA.d + tB.d, P_.d)
            if GSTOP == 34:
                raise _Stop()
            k = 0
            for s_ in range(NS):
                for tb in range(NCHL):
                    ch = s_ * NCHL + tb
                    rows = slice(ch * 128, (ch + 1) * 128)
                    tcb = load_tab(Gc, tb, NFB); tsb = load_tab(Gs, tb, NFB)
                    g_ = gt[k % 2]; z_ = zt[k % 2]
                    S.dma("sp", g_.ap, ucs[rows, (n + 1) * 256:(n + 2) * 256], reads=[ucs_d[ch // 4]], writes=g_.d, sem=gsm[k % 2])
                    if n == 0:
                        S.dma("sp", z_.ap, ucs[rows, 0:256], reads=[ucs_d[ch // 4]], writes=z_.d, sem=zsm[k % 2])
                    else:
                        S.dma("sp", z_.ap, zscr[rows, :], reads=[zscr_d[ch]], writes=z_.d, sem=zsm[k % 2])
                    py = nps()
                    for fc in range(NFB):
                        mm(py[:, 0:256], tcb[:, fc, :], P_[:, fc, s_, 0, :], fc == 0, False, tcb.d + P_.d, py.d, last=False)
                    for fc in range(NFB):
                        mm(py[:, 0:256], tsb[:, fc, :], P_[:, fc, s_, 1, :], False, fc == NFB - 1, tsb.d + P_.d, py.d, last=(fc == NFB - 1))
                    o_ = zn[k % 2]
                    tt("pool", o_.ap, z_.ap, skipB[:, nc0:nc0 + 256], ALU.mult, z_.d + skipB.d, o_.d)
                    tt("dve", o_.ap, o_.ap, py[:, 0:256], ALU.add, o_.d + py.d, o_.d)
                    tt("dve", o_.ap, o_.ap, g_.ap, ALU.mult, o_.d + g_.d, o_.d)
                    if n == 0:
                        cp("act", zh[:, tb, s_ * 256:(s_ + 1) * 256], o_.ap, o_.d, zh.d)
                        S.dma("sp", zscr[rows, :], o_.ap, reads=o_.d, writes=[zscr_d[ch]], sem=zssem[k % 2])
                    else:
                        t = ch // 4; ci = ch % 4
                        ys_ = ystg[t % 2]
                        pt = nps()
                        for j in range(2):
                            S.op("pe", lambda e, pt=pt, j=j, o_=o_: e.transpose(pt[:, j * 128:(j + 1) * 128], o_[:, j * 128:(j + 1) * 128], ident),
                                 reads=o_.d + cm.d, writes=pt.d)
                        cp("act", ys_[:, :, ci * 128:(ci + 1) * 128], pt[:, 0:256].rearrange("p (j n) -> p j n", j=2), pt.d, ys_.d)
                        if ci == 3:
                            S.dma("sp", yTs[4:6, :, t * 512:(t + 1) * 512].rearrange("c p n -> p c n"), ys_.ap, reads=ys_.d, writes=[yTs_d[t]], sem=ysm[t % 2])
                    k += 1
        A.release(mg)
        A.release(m)
        S.barrier()

    def norm_mod(xT, xd, acols, bcols, out_ap, out_d, tmp):
        rs = tmp["rs"]
        p = nps()
        for c in range(8):
            sq = tmp["sq"][c % 2]
            act(sq.ap, xT[:, c, :], AF.Square, xd, sq.d)
            mm(p.ap, onesm.ap, sq.ap, c == 0, c == 7, sq.d + onesm.d, p.d, last=True)
        act(rs.ap, p.ap, AF.Sqrt, p.d, rs.d, bias=EPS)
        S.op("dve", lambda e: e.reciprocal(out=rs.ap, in_=rs.ap), reads=rs.d, writes=rs.d)
        for c in range(8):
            xn = tmp["xn"][c % 2]
            tt("dve", xn.ap, xT[:, c, :], rs.ap, ALU.mult, xd + rs.d, xn.d)
            if bcols is None:
                act(out_ap[:, c, :], xn.ap, AF.Identity, xn.d + colT.d, out_d, scale=acols[:, c:c + 1], bias=0.0)
            else:
                act(out_ap[:, c, :], xn.ap, AF.Identity, xn.d + acol.d + modc.d, out_d,
                    scale=acols[:, c:c + 1], bias=bcols[:, c:c + 1])

    def mk_tmp():
        return dict(sq=[A.alloc(BF16, [128, 512]) for _ in range(2)], rs=A.alloc(F32, [128, 512]),
                    xn=[A.alloc(F32, [128, 512]) for _ in range(2)])

    def run_part(pi, L, NS, x_in, y_out):
        Tn = L * NS
        NT = Tn // 512
        xTs = dscr("xTs%d" % pi, [8, 128, Tn]); xTs_d = [Dep() for _ in range(NT)]
        yTs = dscr("yTs%d" % pi, [8, 128, Tn], BF16); yTs_d = [Dep() for _ in range(NT)]
        xsem = S.new_dsem(); xssem = S.new_dsem(); ysem = S.new_dsem(); osem = [S.new_dsem(), S.new_dsem()]
        tsem = [S.new_dsem(), S.new_dsem()]

        m1 = A.mark()
        xtok = [A.alloc(F32, [128, D]) for _ in range(2)]
        xT = A.alloc(F32, [128, 8, 512])
        tmp = mk_tmp()
        if pi == 0:
            om = A.alloc(F32, [128, 512]); ph = A.alloc(F32, [128, 512]); rc = A.alloc(F32, [128, 2])
            pcol = A.alloc(F32, [128, 512]); prow = A.alloc(F32, [128, 512]); arg = A.alloc(F32, [128, 512])
            ti = A.alloc(I32, [128, 512]); tf = A.alloc(F32, [128, 512]); rv = A.alloc(F32, [128, 1])
            S.dma("sp", om.ap, I["gridc"][0, 0:512].partition_broadcast(128), writes=om.d, sem=gsem[2])
            S.dma("sp", ph.ap, I["gridc"][1, 0:512].partition_broadcast(128), writes=ph.d, sem=gsem[3])
            S.dma("sp", rc.ap, I["gridrc"], writes=rc.d, sem=gsem[4])
            stt(arg.ap, om.ap, rc[:, 1:2], ph.ap, ALU.mult, ALU.add, om.d + ph.d + rc.d, arg.d)
            sin_rr(pcol.ap, arg, ti, tf, [], pcol.d)
        for t in range(NT):
            for j in range(4):
                blk = t * 4 + j
                xk = xtok[blk % 2]
                S.dma("sp", xk.ap, x_in[blk * 128:(blk + 1) * 128, :], writes=xk.d, sem=tsem[blk % 2])
                if pi == 0:
                    ts("dve", rv.ap, rc[:, 0:1], float(2 * blk), None, ALU.add, None, rc.d, rv.d)
                    stt(arg.ap, om.ap, rv[:, 0:1], ph.ap, ALU.mult, ALU.add, om.d + ph.d + rv.d, arg.d)
                    sin_rr(prow.ap, arg, ti, tf, [], prow.d)
                    tt("dve", xk[:, 0:512], xk[:, 0:512], prow.ap, ALU.add, xk.d + prow.d, xk.d)
                    tt("dve", xk[:, 512:1024], xk[:, 512:1024], pcol.ap, ALU.add, xk.d + pcol.d, xk.d)
                for h in range(2):
                    p = nps()
                    for c4 in range(4):
                        c = h * 4 + c4
                        S.op("pe", lambda e, p=p, c4=c4, c=c, xk=xk: e.transpose(p[:, c4 * 128:(c4 + 1) * 128], xk[:, c * 128:(c + 1) * 128], ident),
                             reads=xk.d + cm.d, writes=p.d, inc=(c4 == 3))
                    cp("act", xT[:, h * 4:h * 4 + 4, j * 128:(j + 1) * 128], p.ap.rearrange("p (c n) -> p c n", c=4), p.d, xT.d)
            S.dma("sp", xTs[:, :, t * 512:(t + 1) * 512].rearrange("c p n -> p c n"), xT.ap, reads=xT.d, writes=[xTs_d[t]], sem=xssem)
            norm_mod(xT.ap, xT.d, acol[:, 0, 0, pi, :], modc[:, 0, 0:8, pi], uT[:, :, t * 512:(t + 1) * 512], [uT.d[t]], tmp)
        A.release(m1)
        S.barrier()
        if stop == 1:
            return

        for l in range(NL):
            m2 = A.mark()
            zt = A.alloc(BF16, [128, 8, 512])
            S.op("pool", lambda e: e.memset(zt.ap, 0.0), writes=zt.d)
            done = set()
            if "gla" in flags:
                mixer_gated(pi, l, L, NS, yTs, yTs_d, ysem, "gla"); done.add(0)
            if "rg" in flags:
                mixer_rg(pi, l, L, NS, yTs, yTs_d, ysem); done.add(1)
            if "hg" in flags:
                mixer_gated(pi, l, L, NS, yTs, yTs_d, ysem, "hg"); done.add(3)
            if "hy" in flags:
                mixer_hy(pi, l, L, NS, yTs, yTs_d, ysem); done.add(2)
            for q in range(4):
                if q not in done:
                    for t in range(NT):
                        S.dma("sp", yTs[2 * q:2 * q + 2, :, t * 512:(t + 1) * 512].rearrange("c p n -> p c n"), zt[:, 0:2, :],
                              reads=zt.d, writes=[yTs_d[t]], sem=ysem)
            A.release(m2)
            S.barrier()
            if stop == 2:
                return

            m3 = A.mark()
            NP = 2
            xTl = [A.alloc(F32, [128, 8, 512]) for _ in range(NP)]; yTl = [A.alloc(BF16, [128, 8, 512]) for _ in range(NP)]
            u2l = [A.alloc(BF16, [128, 8, 512]) for _ in range(NP)]
            hTl = [[A.alloc(BF16, [128, 4, 512]) for _ in range(2)] for _ in range(NP)]
            hs = [A.alloc(F32, [128, 512]) for _ in range(2)]
            tmp = mk_tmp()
            last = (l == NL - 1)
            if last:
                otok = [A.alloc(F32, [128, D]) for _ in range(2)]
            if l == 0:
                xsl = [S.new_dsem() for _ in range(NP)]; ysl = [S.new_dsem() for _ in range(NP)]
            g1 = modc[:, l, 16:24, pi]; g2 = modc[:, l, 40:48, pi]
            for tp in range(0, NT, NP):
                tl = list(range(tp, min(tp + NP, NT)))
                for i, t in enumerate(tl):
                    S.dma("sp", xTl[i].ap, xTs[:, :, t * 512:(t + 1) * 512].rearrange("c p n -> p c n"), reads=[xTs_d[t]], writes=xTl[i].d, sem=xsl[i])
                    S.dma("sp", yTl[i].ap, yTs[:, :, t * 512:(t + 1) * 512].rearrange("c p n -> p c n"), reads=[yTs_d[t]], writes=yTl[i].d, sem=ysl[i])
                for g in range(2):
                    w, wd = load_w(I["w_out"][l], 0, D, g * 512, 512)
                    for i, t in enumerate(tl):
                        xT = xTl[i]; yT = yTl[i]
                        for j in range(4):
                            oc = g * 4 + j
                            p = nps()
                            for c in range(8):
                                mm(p.ap, w[:, c, j * 128:(j + 1) * 128], yT[:, c, :], c == 0, c == 7, wd + yT.d, p.d, last=(c == 7))
                            stt(xT[:, oc, :], p.ap, g1[:, oc:oc + 1], xT[:, oc, :], ALU.mult, ALU.add, p.d + xT.d + modc.d, xT.d)
                for i, t in enumerate(tl):
                    norm_mod(xTl[i].ap, xTl[i].d, acol[:, l, 1, pi, :], modc[:, l, 24:32, pi], u2l[i].ap, u2l[i].d, tmp)
                ngrp = [(g * 512, min(512, DFF - g * 512)) for g in range(6)]
                for gi, (h0, hn) in enumerate(ngrp):
                    nb = hn // 128
                    w1, w1d = load_w(I["w1"][l], 0, D, h0, hn)
                    w3, w3d = load_w(I["w3"][l], 0, D, h0, hn)
                    w2, w2d = load_w(I["w2"][l], h0, hn, 0, D)
                    for i, t in enumerate(tl):
                        xT = xTl[i]; u2 = u2l[i]
                        hb = hTl[i][gi % 2]
                        for j in range(nb):
                            p1 = nps(); p3 = nps()
                            for c in range(8):
                                mm(p1.ap, w1[:, c, j * 128:(j + 1) * 128], u2[:, c, :], c == 0, c == 7, w1d + u2.d, p1.d, last=(c == 7))
                            for c in range(8):
                                mm(p3.ap, w3[:, c, j * 128:(j + 1) * 128], u2[:, c, :], c == 0, c == 7, w3d + u2.d, p3.d, last=(c == 7))
                            hh = hs[j % 2]
                            act(hh.ap, p1.ap, AF.Silu, p1.d, hh.d)
                            tt("dve", hb[:, j, :], hh.ap, p3.ap, ALU.mult, hh.d + p3.d, hb.d)
                        for oc in range(8):
                            p = nps()
                            for j in range(nb):
                                mm(p.ap, w2[:, j, oc * 128:(oc + 1) * 128], hb[:, j, :], j == 0, j == nb - 1, w2d + hb.d, p.d, last=(j == nb - 1))
                            stt(xT[:, oc, :], p.ap, g2[:, oc:oc + 1], xT[:, oc, :], ALU.mult, ALU.add, p.d + xT.d + modc.d, xT.d)
                for i, t in enumerate(tl):
                    xT = xTl[i]
                    if not last:
                        S.dma("sp", xTs[:, :, t * 512:(t + 1) * 512].rearrange("c p n -> p c n"), xT.ap, reads=xT.d, writes=[xTs_d[t]], sem=xsl[i])
                        norm_mod(xT.ap, xT.d, acol[:, l + 1, 0, pi, :], modc[:, l + 1, 0:8, pi], uT[:, :, t * 512:(t + 1) * 512], [uT.d[t]], tmp)
                    else:
                        yf = xT
                        norm_mod(xT.ap, xT.d, C(SP_ROWS["fng"], 8), None, yf.ap, yf.d, tmp)
                        for j in range(4):
                            blk = t * 4 + j
                            ok = otok[blk % 2]
                            for h in range(2):
                                p = nps()
                                for c4 in range(4):
                                    c = h * 4 + c4
                                    S.op("pe", lambda e, p=p, c4=c4, c=c, j=j, yf=yf: e.transpose(p[:, c4 * 128:(c4 + 1) * 128], yf[:, c, j * 128:(j + 1) * 128], ident),
                                         reads=yf.d + cm.d, writes=p.d, inc=(c4 == 3))
                                cp("act", ok[:, h * 512:(h + 1) * 512], p.ap, p.d, ok.d)
                            dd = Dep()
                            S.dma("sp", y_out[blk * 128:(blk + 1) * 128, :], ok.ap, reads=ok.d, writes=[dd], sem=osem[blk % 2])
                            out_deps.append(dd)
            A.release(m3)
            S.barrier()
            if stop == 3:
                return

    try:
        if stop >= 1:
            run_part(0, 4096, 1, I["xs"], O["ys"])
        if stop >= 5:
            run_part(1, 256, 4, I["xp"], O["yp"])
    except _Stop:
        S.barrier()
    w = S._waits("sp", out_deps, ())
    S.prog["sp"].append((w, None, None, 0))
    S.emit()
    return nc, st, S


FLAGS = ("gla", "rg", "hy", "hg")
STOP = 99
GSTOP = 0


class _Stop(Exception):
    pass

NCORES = 8
_CACHE = {}


def _consts():
    if "c" in _CACHE:
        return _CACHE["c"]
    p = np.arange(128)
    s_, t_ = p[:, None], p[None, :]
    cm = np.zeros((128, 6, 128), np.float32)
    cm[:, 0] = (s_ == t_); cm[:, 1] = (s_ <= t_); cm[:, 2] = (s_ >= t_); cm[:, 3] = (s_ > t_); cm[:, 4] = (s_ < t_)
    cm[:, 5] = ((s_ // 64) == (t_ // 64)) / 64.0
    om, ph, rc = _grid_consts(D)
    gc4, gs4, e4, _ = _dft_tables(4096, 6144)
    gc2, gs2, e2, _ = _dft_tables(256, 512)
    c = dict(cmat=cm, gc4096=gc4, gs4096=gs4, e4096=e4, gc256=gc2, gs256=gs2, e256=e2,
             gridc=np.stack([om, ph]).astype(np.float32), gridrc=rc)
    _CACHE["c"] = c
    return c


def kernel(**inp):
    inp = {k: np.asarray(v) for k, v in inp.items()}
    key = ("prog", FLAGS)
    if key not in _CACHE:
        _CACHE[key] = build_program(flags=FLAGS, stop=STOP)
    nc = _CACHE[key][0]
    cst = _consts()
    f32 = lambda a: np.ascontiguousarray(a, dtype=np.float32)
    shared = dict(
        w_mod=f32(inp["w_mod"]), w_in=f32(inp["w_in"]), w_out=f32(inp["w_out"]),
        w1=f32(inp["ffn_w1"]), w3=f32(inp["ffn_w3"]), w2=f32(inp["ffn_w2"]),
        gla_wg=f32(inp["gla_w_gate"].transpose(0, 2, 1, 3).reshape(NL, 16, 512)),
        gla_bg=f32(inp["gla_b_gate"].reshape(NL, 512)),
        rg_wa=f32(inp["rg_w_a"]), rg_wx=f32(inp["rg_w_x"]),
        hy_w1=f32(inp["hy_w1"]), hy_w2=f32(inp["hy_w2"]), hy_w3=f32(inp["hy_w3"]),
        hy_dec=f32(inp["hy_decay"]), hy_skip=f32(inp["hy_skip"]), hg_low=f32(inp["hg_lower"]),
        **cst)
    in_maps = []
    for k in range(8):
        b = k % 2
        m = dict(shared)
        m["xs"] = f32(inp["x_sample"][b])
        m["xp"] = f32(inp["x_prompt"][4 * k:4 * k + 4].reshape(1024, D))
        m["st_gla"] = f32(inp["state_gla"][b].reshape(NL, 2, 256, 64))
        m["st_hg"] = f32(inp["state_hgrn"][b].reshape(NL, 2, 256, 64))
        m["st_rg"] = f32(inp["state_rglru"][b])
        m["sp"] = _build_sp(inp, b)
        in_maps.append(m)
    if NCORES < 8:
        res = run_bass_kernel_spmd(nc, in_maps[:NCORES], core_ids=list(range(NCORES)))
        R = [res.results[k % NCORES] for k in range(8)]
    else:
        res = run_bass_kernel_spmd(nc, in_maps, core_ids=list(range(8)))
        R = res.results
    y_prompt = np.concatenate([R[k]["yp"].reshape(4, 256, D) for k in range(8)], axis=0)
    y_sample = np.stack([R[0]["ys"], R[1]["ys"]], axis=0)
    ns_gla = np.concatenate([R[k]["ns_gla"].reshape(4, NL, 2, 4, 64, 64) for k in range(8)], axis=0)
    ns_rg = np.concatenate([R[k]["ns_rg"].reshape(4, NL, 2, 256) for k in range(8)], axis=0)
    ns_hg = np.concatenate([R[k]["ns_hg"].reshape(4, NL, 2, 4, 64, 64) for k in range(8)], axis=0)
    return (y_prompt.astype(np.float32), y_sample.astype(np.float32), ns_gla.astype(np.float32),
            ns_rg.astype(np.float32), ns_hg.astype(np.float32))
```

```python
import numpy as np
from contextlib import ExitStack
import ml_dtypes
import concourse.bass as bass
import concourse.mybir as mybir
from concourse.bass_utils import run_bass_kernel_spmd

F32 = mybir.dt.float32
BF16 = mybir.dt.bfloat16
I32 = mybir.dt.int32
U8 = mybir.dt.uint8
AF = mybir.ActivationFunctionType
ALU = mybir.AluOpType
AX = mybir.AxisListType

D = 1024
DIN = 3600
DFF = 2816
NL = 2
EPS = 1e-6
PI = float(np.pi)


class Dep:
    __slots__ = ("w", "r", "x", "rg")

    def __init__(self, x=False):
        self.w = None
        self.r = {}
        self.x = x
        self.rg = None


class Sched:
    ENG = ("pe", "act", "dve", "pool", "sp")

    def __init__(self, nc, stack, n_dma_sems=96):
        self.nc = nc
        self.semh = {}
        self.cnt = {}
        for e in self.ENG:
            self.semh[e] = stack.enter_context(nc.semaphore("s_" + e))
            self.cnt[e] = 0
        for i in range(n_dma_sems):
            k = "d%d" % i
            self.semh[k] = stack.enter_context(nc.semaphore("s_" + k))
            self.cnt[k] = 0
        self.n_dma_sems = n_dma_sems
        self.next_dsem = 0
        self.prog = {e: [] for e in self.ENG}
        self.known = {e: {} for e in self.ENG}
        self.ninstr = 0
        self.pe_self = False
        self.pe_mode = None

    def new_dsem(self):
        k = "d%d" % self.next_dsem
        self.next_dsem += 1
        assert self.next_dsem <= self.n_dma_sems, "out of dma sems"
        return k

    def _waits(self, eng, reads, writes, pe_rg=2):
        waits = {}
        if eng.startswith("dmaq:"):
            known = self.known[eng[5:]]
            eng = "dma"
        else:
            known = self.known[eng]

        def need(k, v):
            if known.get(k, 0) >= v:
                return
            if waits.get(k, 0) < v:
                waits[k] = v
        for t in reads:
            if t.w is not None and not (eng == "pe" and t.w[0] == "pe" and not self.pe_self):
                need(*t.w)
            if t.x:
                for k, v in t.r.items():
                    if k != eng:
                        need(k, v)
        for t in writes:
            if eng == "pe":
                switch = (t.rg is not None and t.rg != pe_rg)
                t.rg = pe_rg
            else:
                switch = False
            if t.w is not None and not (eng == "pe" and t.w[0] == "pe" and not (self.pe_self or switch)):
                need(*t.w)
            for k, v in t.r.items():
                if not (eng == "pe" and k == "pe" and not self.pe_self):
                    need(k, v)
        for k, v in waits.items():
            known[k] = v
        return list(waits.items())

    def _mark(self, pt, reads, writes):
        k, v = pt
        for t in writes:
            t.w = pt
            t.r = {}
        for t in reads:
            if t.r.get(k, 0) < v:
                t.r[k] = v

    def op(self, eng, fn, reads=(), writes=(), inc=True, pe_rg=2, pe_mode=(128, 128)):
        if eng != "pe":
            inc = True
        waits = self._waits(eng, reads, writes, pe_rg)
        if eng == "pe":
            if self.pe_mode is not None and self.pe_mode != pe_mode and self.cnt["pe"] > 0:
                v = self.cnt["pe"]
                if self.known["pe"].get("pe", 0) < v:
                    waits = [w for w in waits if w[0] != "pe"] + [("pe", v)]
                    self.known["pe"]["pe"] = v
            self.pe_mode = pe_mode
        if inc:
            self.cnt[eng] += 1
            val = self.cnt[eng]
        else:
            val = self.cnt[eng] + 1
        ws = set(id(t) for t in writes)
        self._mark((eng, val), [t for t in reads if id(t) not in ws], writes)
        self.prog[eng].append((waits, fn, eng if inc else None, 1))
        self.ninstr += 1

    def dma(self, queue, out, in_, reads=(), writes=(), sem=None, **kw):
        waits = self._waits("dmaq:" + queue, reads, writes)
        self.cnt[sem] += 16
        ws = set(id(t) for t in writes)
        self._mark((sem, self.cnt[sem]), [t for t in reads if id(t) not in ws], writes)
        self.prog[queue].append((waits, (lambda e, o=out, i=in_, kw=kw: e.dma_start(out=o, in_=i, **kw)), sem, 16))
        self.ninstr += 1

    def barrier(self):
        for e in self.ENG:
            waits = []
            for k, v in self.cnt.items():
                if v > 0 and self.known[e].get(k, 0) < v:
                    waits.append((k, v))
                    self.known[e][k] = v
            if waits:
                self.prog[e].append((waits, None, None, 0))

    def emit(self):
        nc = self.nc
        semh = self.semh

        def run(e, name):
            for waits, fn, incsem, incv in self.prog[name]:
                for k, v in waits:
                    e.wait_ge(semh[k], v)
                if fn is None:
                    continue
                ins = fn(e)
                if incsem is not None:
                    ins.then_inc(semh[incsem], incv)

        with nc.Block() as block:
            @block.tensor
            def _(e):
                run(e, "pe")

            @block.scalar
            def _(e):
                run(e, "act")

            @block.vector
            def _(e):
                run(e, "dve")

            @block.gpsimd
            def _(e):
                run(e, "pool")

            @block.sync
            def _(e):
                run(e, "sp")


class T:
    def __init__(self, ap, ndeps=1):
        self.ap = ap
        self.d = [Dep() for _ in range(ndeps)]

    def __getitem__(self, idx):
        return self.ap[idx]


def _dft_tables(L, N):
    nb = L // 128
    a = np.arange(L, dtype=np.float64) + 0.5
    ang = 2.0 * np.pi * np.outer(a, a) / N
    out = []
    for fn in (np.cos, np.sin):
        G = fn(ang)
        Gt = G.reshape(nb, 128, nb, 128).transpose(2, 1, 0, 3)
        out.append(np.ascontiguousarray(Gt).astype(ml_dtypes.bfloat16))
    nfb = (N // 2) // 128
    f = np.arange(N // 2, dtype=np.float64) + 0.5
    phi = 2.0 * np.pi * f / N * (L / 2 + 0.5)
    ec = (2.0 / N) * np.cos(phi)
    es = (2.0 / N) * np.sin(phi)
    E = np.stack([ec.reshape(nfb, 128).T, es.reshape(nfb, 128).T], axis=1)
    return out[0], out[1], np.ascontiguousarray(E).astype(np.float32), nfb


def _pe_consts(L):
    bands = np.linspace(1e-4, 15, 16, dtype=np.float32)
    w = (np.float32(2.0 * np.pi) / np.float32(L)) * bands
    om = np.zeros(33, np.float32); ph = np.zeros(33, np.float32)
    om[1:17] = w;  ph[1:17] = np.float32(np.pi / 2)
    om[17:33] = w; ph[17:33] = np.float32(np.pi)
    om[0] = np.float32(1.0) / np.float32(L - 1)
    return np.stack([om, ph], axis=1).astype(np.float32)


def _grid_consts(dim):
    quarter = dim // 4
    omega = (1.0 / (np.float32(10000.0) ** (np.arange(quarter, dtype=np.float32) / np.float32(quarter)))).astype(np.float32)
    om = np.concatenate([omega, omega, omega, omega])
    ph = np.concatenate([np.zeros(quarter), np.full(quarter, np.pi / 2)] * 2).astype(np.float32)
    p = np.arange(128)
    rc = np.stack([(p >= 64).astype(np.float32), (p % 64).astype(np.float32)], axis=1)
    return om.astype(np.float32), ph, rc.astype(np.float32)


def _sp_layout():
    rows = {}
    n = 0

    def add(name, cnt):
        nonlocal n
        rows[name] = n
        n += cnt
    for l in range(NL):
        add("n1g%d" % l, 8); add("n2g%d" % l, 8); add("bmod%d" % l, 48)
        add("glang%d" % l, 2); add("rgcw%d" % l, 8); add("rgcb%d" % l, 2)
        add("rgba%d" % l, 4); add("rgbx%d" % l, 4); add("rglam%d" % l, 4)
        add("hycw%d" % l, 18); add("hycb%d" % l, 6); add("hyb1%d" % l, 1); add("hyb2%d" % l, 1)
        add("hglow%d" % l, 2); add("hgng%d" % l, 2)
    add("fng", 8); add("c", 8); add("cctx", 8); add("pec4096", 2); add("pec256", 2)
    return rows, ((n + 127) // 128) * 128


SP_ROWS, SP_N = _sp_layout()


def _build_sp(inp, b):
    sp = np.zeros((SP_N, 128), np.float32)

    def put(name, arr):
        a = np.asarray(arr, np.float32).reshape(-1, 128)
        sp[SP_ROWS[name]:SP_ROWS[name] + a.shape[0]] = a
    for l in range(NL):
        put("n1g%d" % l, inp["norm1_g"][l]); put("n2g%d" % l, inp["norm2_g"][l]); put("bmod%d" % l, inp["b_mod"][l])
        put("glang%d" % l, inp["gla_norm_g"][l]); put("rgcw%d" % l, inp["rg_conv_w"][l]); put("rgcb%d" % l, inp["rg_conv_b"][l])
        put("rgba%d" % l, inp["rg_b_a"][l]); put("rgbx%d" % l, inp["rg_b_x"][l]); put("rglam%d" % l, inp["rg_lambda"][l])
        put("hycw%d" % l, inp["hy_conv_w"][l]); put("hycb%d" % l, inp["hy_conv_b"][l])
        r = np.zeros(128, np.float32); r[:64] = inp["hy_b1"][l]; put("hyb1%d" % l, r)
        r = np.zeros(128, np.float32); r[:64] = inp["hy_b2"][l]; put("hyb2%d" % l, r)
        put("hglow%d" % l, inp["hg_lower"][l]); put("hgng%d" % l, inp["hg_norm_g"][l])
    put("fng", inp["final_norm_g"]); put("c", inp["c"][b]); put("cctx", inp["c_ctx"])
    for L in (4096, 256):
        pc = _pe_consts(L)
        r = np.zeros((2, 128), np.float32); r[0, :33] = pc[:, 0]; r[1, :33] = pc[:, 1]
        put("pec%d" % L, r)
    return sp


IN_OFF = dict(a_q=0, a_k=256, a_v=512, a_g=768, a_lr=1024, b_x=1040, b_g=1296, c_v=1552, c_x1=1808, c_x2=2064,
              d_q=2320, d_ff=2576, d_fb=2832, d_i=3088, d_g=3344)

ARENA_BYTES = 206 * 1024


class Arena:
    def __init__(self, nc, stack):
        self.t = stack.enter_context(nc.sbuf_tensor("arena", [128, ARENA_BYTES // 4], F32))
        self.off = 0

    def alloc(self, dtype, shape, ndeps=1):
        esz = 4 if dtype in (F32, I32) else (1 if dtype == U8 else 2)
        n = int(np.prod(shape[1:]))
        nb = ((n * esz + 31) // 32) * 32
        assert self.off + nb <= ARENA_BYTES, "arena overflow %d" % (self.off + nb)
        ap = self.t[:, self.off // 4:(self.off + nb) // 4]
        if dtype != F32:
            ap = ap.bitcast(dtype)
        ap = ap[0:shape[0], 0:n]
        if len(shape) > 2:
            names = "abcdefg"[:len(shape) - 1]
            kw = {names[i]: shape[i + 1] for i in range(len(shape) - 2)}
            ap = ap.rearrange("p (%s) -> p %s" % (" ".join(names), " ".join(names)), **kw)
        self.off += nb
        return T(ap, ndeps)

    def mark(self):
        return self.off

    def release(self, m):
        self.off = m


def build_program(flags=("gla", "rg", "hy", "hg"), dbg=False, stop=99):
    nc = bass.Bass("TRN2", target_bir_lowering=False)
    st = ExitStack()
    S = Sched(nc, st)
    A = Arena(nc, st)

    def din(name, shape, dt=F32):
        return nc.dram_tensor(name, list(shape), dt, kind="ExternalInput").ap()

    def dout(name, shape, dt=F32):
        return nc.dram_tensor(name, list(shape), dt, kind="ExternalOutput").ap()

    def dscr(name, shape, dt=F32):
        return nc.dram_tensor(name, list(shape), dt, kind="Internal").ap()

    I = {}
    I["xs"] = din("xs", [4096, D]); I["xp"] = din("xp", [1024, D])
    I["st_gla"] = din("st_gla", [NL, 2, 256, 64]); I["st_hg"] = din("st_hg", [NL, 2, 256, 64]); I["st_rg"] = din("st_rg", [NL, 2, 256])
    I["w_mod"] = din("w_mod", [NL, D, 6 * D]); I["w_in"] = din("w_in", [NL, D, DIN]); I["w_out"] = din("w_out", [NL, D, D])
    I["w1"] = din("w1", [NL, D, DFF]); I["w3"] = din("w3", [NL, D, DFF]); I["w2"] = din("w2", [NL, DFF, D])
    I["sp"] = din("sp", [SP_N, 128])
    I["gla_wg"] = din("gla_wg", [NL, 16, 512]); I["gla_bg"] = din("gla_bg", [NL, 512])
    I["rg_wa"] = din("rg_wa", [NL, 2, 4, 64, 64]); I["rg_wx"] = din("rg_wx", [NL, 2, 4, 64, 64])
    I["hy_w1"] = din("hy_w1", [NL, 33, 64]); I["hy_w2"] = din("hy_w2", [NL, 64, 64]); I["hy_w3"] = din("hy_w3", [NL, 64, 512])
    I["hy_dec"] = din("hy_dec", [NL, 512]); I["hy_skip"] = din("hy_skip", [NL, 512]); I["hg_low"] = din("hg_low", [NL, 256])
    I["cmat"] = din("cmat", [128, 6, 128])
    I["gc4096"] = din("gc4096", [32, 128, 32, 128], BF16); I["gs4096"] = din("gs4096", [32, 128, 32, 128], BF16)
    I["gc256"] = din("gc256", [2, 128, 2, 128], BF16); I["gs256"] = din("gs256", [2, 128, 2, 128], BF16)
    I["e4096"] = din("e4096", [128, 2, 24]); I["e256"] = din("e256", [128, 2, 2])
    I["gridc"] = din("gridc", [2, D]); I["gridrc"] = din("gridrc", [128, 2])
    O = {}
    O["ys"] = dout("ys", [4096, D]); O["yp"] = dout("yp", [1024, D])
    O["ns_gla"] = dout("ns_gla", [4, NL, 2, 256, 64]); O["ns_hg"] = dout("ns_hg", [4, NL, 2, 256, 64]); O["ns_rg"] = dout("ns_rg", [4, NL, 2, 256])
    out_deps = []
    DBG = {}

    cm = A.alloc(F32, [128, 6, 128])
    ident = cm[:, 0, :]; tril = cm[:, 1, :]; triu = cm[:, 2, :]; su = cm[:, 3, :]; sl = cm[:, 4, :]
    bd64 = A.alloc(BF16, [128, 128]); onesm = A.alloc(BF16, [128, 128])
    colT = A.alloc(F32, [128, SP_N])
    modc = A.alloc(F32, [128, NL, 48, 2])
    acol = A.alloc(F32, [128, NL, 2, 2, 8])
    nsp = A.alloc(F32, [128, NL, 2, 2, 2])
    lbc = A.alloc(F32, [128, NL, 2, 2])
    WSL = 4
    wring = [A.alloc(BF16, [128, 8, 512]) for _ in range(WSL)]
    wsem = [S.new_dsem() for _ in range(WSL)]
    wnext = [0]
    uT = A.alloc(BF16, [128, 8, 4096], ndeps=8)
    PS = [T(st.enter_context(nc.psum_tensor("ps%d" % i, [128, 512], F32))[:, :]) for i in range(8)]
    for p_ in PS:
        p_.d = [Dep(x=True)]
    psn = [0]

    def nps():
        p = PS[psn[0] % 7]
        psn[0] += 1
        return p

    gsem = [S.new_dsem() for _ in range(8)]
    SPL = [S.new_dsem() for _ in range(24)]

    def C(row, n=1):
        return colT[:, row:row + n]

    def mm(out, lhsT, rhs, start, stop, reads, writes, last=True, rg=2, mode=(128, 128)):
        S.op("pe", lambda e: e.matmul(out, lhsT=lhsT, rhs=rhs, start=start, stop=stop), reads=reads, writes=writes, inc=last, pe_rg=rg, pe_mode=mode)

    def act(out, in_, func, reads, writes, **kw):
        S.op("act", lambda e: e.activation(out=out, in_=in_, func=func, **kw), reads=reads, writes=writes)

    def tt(eng, out, in0, in1, op, reads, writes):
        S.op(eng, lambda e: e.tensor_tensor(out=out, in0=in0, in1=in1, op=op), reads=reads, writes=writes)

    def ts(eng, out, in0, s1, s2, op0, op1, reads, writes):
        if op1 is None:
            S.op(eng, lambda e: e.tensor_scalar(out=out, in0=in0, scalar1=s1, scalar2=None, op0=op0), reads=reads, writes=writes)
        else:
            S.op(eng, lambda e: e.tensor_scalar(out=out, in0=in0, scalar1=s1, scalar2=s2, op0=op0, op1=op1), reads=reads, writes=writes)

    def stt(out, in0, scalar, in1, op0, op1, reads, writes):
        S.op("dve", lambda e: e.scalar_tensor_tensor(out=out, in0=in0, scalar=scalar, in1=in1, op0=op0, op1=op1), reads=reads, writes=writes)

    def cp(eng, out, in_, reads, writes):
        if eng == "act":
            S.op("act", lambda e: e.copy(out=out, in_=in_), reads=reads, writes=writes)
        else:
            S.op(eng, lambda e: e.tensor_copy(out=out, in_=in_), reads=reads, writes=writes)

    def load_w(w2d, r0, nr, c0, ncol):
        i = wnext[0] % WSL
        wnext[0] += 1
        slot = wring[i]
        kc = nr // 128
        src = w2d[r0:r0 + nr, c0:c0 + ncol].rearrange("(c p) n -> p c n", p=128)
        flat = slot.ap.rearrange("p a b -> p (a b)")[:, 0:kc * ncol].rearrange("p (c n) -> p c n", c=kc)
        S.dma("pool", flat, src, writes=slot.d, sem=wsem[i])
        return T(flat, 0), slot.d

    def sin_rr(out, arg, tmp_i, tmp_f, reads, writes_t):
        ts("dve", tmp_i.ap, arg.ap, 1.0 / (2 * PI), None, ALU.mult, None, arg.d + reads, tmp_i.d)
        cp("dve", tmp_f.ap, tmp_i.ap, tmp_i.d, tmp_f.d)
        stt(arg.ap, tmp_f.ap, -2 * PI, arg.ap, ALU.mult, ALU.add, tmp_f.d + arg.d, arg.d)
        ts("dve", arg.ap, arg.ap, -PI, PI, ALU.max, ALU.min, arg.d, arg.d)
        act(out, arg.ap, AF.Sin, arg.d, writes_t)

    S.dma("sp", cm.ap, I["cmat"], writes=cm.d, sem=gsem[0])
    cp("dve", bd64.ap, cm[:, 5, :], cm.d, bd64.d)
    S.op("pool", lambda e: e.memset(onesm.ap, 1.0 / D), writes=onesm.d)
    m0 = A.mark()
    if stop == -3:
        S.emit(); return nc, st, S
    sprow = A.alloc(F32, [128, 128])
    for k in range(SP_N // 128):
        S.dma("sp", sprow.ap, I["sp"][k * 128:(k + 1) * 128, :], writes=sprow.d, sem=gsem[1])
        p = nps()
        S.op("pe", lambda e, p=p: e.transpose(p[:, 0:128], sprow.ap, ident), reads=sprow.d + cm.d, writes=p.d)
        cp("dve", colT[:, k * 128:(k + 1) * 128], p[:, 0:128], p.d, colT.d)
    if stop == -2:
        S.emit(); return nc, st, S
    scT = A.alloc(BF16, [128, 8, 128])
    S.op("pool", lambda e: e.memset(scT.ap.rearrange("p a b -> p (a b)"), 0.0), writes=scT.d)
    act(scT[:, :, 0], C(SP_ROWS["c"], 8), AF.Silu, colT.d, scT.d)
    act(scT[:, :, 1], C(SP_ROWS["cctx"], 8), AF.Silu, colT.d, scT.d)
    tmpc = A.alloc(F32, [128, 16])
    for l in range(NL):
        lam = C(SP_ROWS["rglam%d" % l], 4)
        act(tmpc[:, 0:4], lam, AF.Exp, colT.d, tmpc.d, scale=-1.0)
        act(tmpc[:, 4:8], tmpc[:, 0:4], AF.Ln, tmpc.d, tmpc.d, bias=1.0)
        nv = nsp[:, l, :, :, :].rearrange("p d c k -> p (d c) k")
        ts("dve", nv[:, :, 0], tmpc[:, 4:8], -8.0, None, ALU.mult, None, tmpc.d, nsp.d)
        ts("dve", nv[:, :, 1], tmpc[:, 4:8], -16.0, None, ALU.mult, None, tmpc.d, nsp.d)
    S.op("pool", lambda e: e.memset(lbc[:, 0, :, 0], 0.0), writes=lbc.d)
    S.op("pool", lambda e: e.memset(lbc[:, 0, :, 1], 1.0), writes=lbc.d)
    tt("dve", tmpc[:, 8:10], C(SP_ROWS["hglow1"], 2), C(SP_ROWS["hglow0"], 2), ALU.subtract, colT.d, tmpc.d)
    act(lbc[:, 1, :, 0], tmpc[:, 8:10], AF.Sigmoid, tmpc.d, lbc.d)
    act(lbc[:, 1, :, 1], tmpc[:, 8:10], AF.Sigmoid, tmpc.d, lbc.d, scale=-1.0)
    if stop == -1:
        S.emit(); return nc, st, S
    modrow = A.alloc(F32, [128, 6 * D])
    for l in range(NL):
        for g in range(12):
            w, wd = load_w(I["w_mod"][l], 0, D, g * 512, 512)
            pm = nps()
            for c in range(8):
                mm(pm.ap, scT[:, c, :], w[:, c, :], c == 0, c == 7, wd + scT.d, pm.d, last=(c == 7))
            cp("dve", modrow[:, g * 512:(g + 1) * 512], pm.ap, pm.d, modrow.d)
        if stop == -0.6:
            S.barrier(); S.emit(); return nc, st, S
        for g in range(12):
            pt = nps()
            for j in range(4):
                blk = g * 4 + j
                S.op("pe", lambda e, pt=pt, j=j, blk=blk: e.transpose(pt[:, j * 128:(j + 1) * 128], modrow[:, blk * 128:(blk + 1) * 128], ident),
                     reads=modrow.d + cm.d, writes=pt.d, inc=(j == 3))
            cp("dve", modc[:, l, g * 4:g * 4 + 4, :], pt.ap.rearrange("p (j n) -> p j n", j=4)[:, :, 0:2], pt.d, modc.d)
        if stop == -0.4:
            S.barrier(); S.emit(); return nc, st, S
        bm = C(SP_ROWS["bmod%d" % l], 48)
        tt("dve", modc[:, l, :, :], modc[:, l, :, :], bm.unsqueeze(2).to_broadcast([128, 48, 2]), ALU.add, modc.d + colT.d, modc.d)
        if stop == -0.2:
            S.barrier(); S.emit(); return nc, st, S
        for wn, (grow, scoff) in enumerate(((SP_ROWS["n1g%d" % l], 8), (SP_ROWS["n2g%d" % l], 32))):
            for part in range(2):
                stt(acol[:, l, wn, part, :], modc[:, l, scoff:scoff + 8, part], 1.0, C(grow, 8), ALU.add, ALU.mult,
                    modc.d + colT.d, acol.d)
    A.release(m0)
    S.barrier()


    def seg_of(L, t):
        if L >= 512:
            return [((t * 512) // L, (t * 512) % L, 512, 0)]
        k = 512 // L
        return [(t * k + j, 0, L, j * L) for j in range(k)]

    def proj_fm(w, wd, col0, ncol, t):
        p = nps()
        for c in range(8):
            mm(p[0:ncol, :], w[:, c, col0:col0 + ncol], uT[:, c, t * 512:(t + 1) * 512], c == 0, c == 7, wd + [uT.d[t]], p.d, last=(c == 7))
        return p

    def mixer_rg(pi, l, L, NS, yTs, yTs_d, ysem):
        m = A.mark()
        Tn = L * NS; NT = Tn // 512
        w, wd = load_w(I["w_in"][l], 0, D, IN_OFF["b_x"], 512)
        bd = A.alloc(BF16, [128, 8, 128])
        S.op("pool", lambda e: e.memset(bd.ap.rearrange("p a b -> p (a b)"), 0.0), writes=bd.d)
        bsem = SPL[0]
        for gate, key in enumerate(("rg_wa", "rg_wx")):
            for d in range(2):
                for ct in range(2):
                    for blk in range(2):
                        S.dma("pool", bd[blk * 64:(blk + 1) * 64, (gate * 2 + d) * 2 + ct, blk * 64:(blk + 1) * 64],
                              I[key][l, d, ct * 2 + blk], writes=bd.d, sem=bsem)
        xpad = A.alloc(F32, [128, NS, L + 3]); xc = A.alloc(F32, [128, NS, L]); xcb = A.alloc(BF16, [128, NS, L])
        hf = A.alloc(F32, [128, NS, L])
        h0 = A.alloc(F32, [128, 2])
        NR = 2
        tmps = [dict((k, A.alloc(F32, [128, 512])) for k in ("r", "i", "a", "u", "hb", "g")) for _ in range(NR)]
        yb = [A.alloc(BF16, [128, 512]) for _ in range(2)]
        hsem = SPL[1]; ssem = SPL[2]; ysems = [SPL[3], SPL[4]]
        rcw = SP_ROWS["rgcw%d" % l]; rcb = SP_ROWS["rgcb%d" % l]; rba = SP_ROWS["rgba%d" % l]; rbx = SP_ROWS["rgbx%d" % l]
        for ct in range(2):
            S.op("pool", lambda e: e.memset(xpad.ap.rearrange("p a b -> p (a b)"), 0.0), writes=xpad.d)
            if pi == 0:
                for d in range(2):
                    S.dma("sp", h0[:, d:d + 1], I["st_rg"][l, d, ct * 128:(ct + 1) * 128].rearrange("(p o) -> p o", o=1), writes=h0.d, sem=hsem)
            else:
                S.op("pool", lambda e: e.memset(h0.ap, 0.0), writes=h0.d)
            for t in range(NT):
                p = proj_fm(w, wd, ct * 128, 128, t)
                for (s_, off, n, a0) in seg_of(L, t):
                    cp("act", xpad[:, s_, 2 + off:2 + off + n], p[:, a0:a0 + n], p.d, xpad.d)
            for s_ in range(NS):
                ts("dve", xc[:, s_, :], xpad[:, s_, 0:L], C(rcw + 0 * 2 + ct), C(rcb + ct), ALU.mult, ALU.add, xpad.d + colT.d, xc.d)
                for j in range(1, 4):
                    stt(xc[:, s_, :], xpad[:, s_, j:j + L], C(rcw + j * 2 + ct), xc[:, s_, :], ALU.mult, ALU.add, xpad.d + colT.d + xc.d, xc.d)
                cp("act", xcb[:, s_, :], xc[:, s_, :], xc.d, xcb.d)

            def gates(d, s_, off, n, tm):
                sl_ = slice(off, off + n)
                pr = nps(); pq = nps()
                mm(pr[:, 0:n], bd[:, (0 * 2 + d) * 2 + ct, :], xcb[:, s_, sl_], True, True, bd.d + xcb.d, pr.d)
                mm(pq[:, 0:n], bd[:, (1 * 2 + d) * 2 + ct, :], xcb[:, s_, sl_], True, True, bd.d + xcb.d, pq.d)
                act(tm["r"][:, 0:n], pr[:, 0:n], AF.Sigmoid, pr.d + colT.d, tm["r"].d, bias=C(rba + d * 2 + ct))
                act(tm["i"][:, 0:n], pq[:, 0:n], AF.Sigmoid, pq.d + colT.d, tm["i"].d, bias=C(rbx + d * 2 + ct))
                act(tm["a"][:, 0:n], tm["r"][:, 0:n], AF.Exp, tm["r"].d + nsp.d, tm["a"].d, scale=nsp[:, l, d, ct, 0:1])
                act(tm["u"][:, 0:n], tm["r"][:, 0:n], AF.Exp, tm["r"].d + nsp.d, tm["u"].d, scale=nsp[:, l, d, ct, 1:2])
                ts("dve", tm["u"][:, 0:n], tm["u"][:, 0:n], -1.0, 1.0, ALU.mult, ALU.add, tm["u"].d, tm["u"].d)
                act(tm["u"][:, 0:n], tm["u"][:, 0:n], AF.Sqrt, tm["u"].d, tm["u"].d)
                tt("pool", tm["i"][:, 0:n], tm["i"][:, 0:n], xc[:, s_, sl_], ALU.mult, tm["i"].d + xc.d, tm["i"].d)
                tt("dve", tm["u"][:, 0:n], tm["u"][:, 0:n], tm["i"][:, 0:n], ALU.mult, tm["u"].d + tm["i"].d, tm["u"].d)

            k = 0
            for t in range(NT):
                for (s_, off, n, a0) in seg_of(L, t):
                    tm = tmps[k % NR]; k += 1
                    gates(0, s_, off, n, tm)
                    init = h0[:, 0:1] if off == 0 else hf[:, s_, off - 1:off]
                    S.op("dve", lambda e, tm=tm, s_=s_, off=off, n=n, init=init: e.tensor_tensor_scan(
                        out=hf[:, s_, off:off + n], data0=tm["a"][:, 0:n], data1=tm["u"][:, 0:n], initial=init, op0=ALU.mult, op1=ALU.add),
                        reads=tm["a"].d + tm["u"].d + hf.d + h0.d, writes=hf.d)
                    if pi == 1 and off + n == L:
                        dd = Dep()
                        S.dma("sp", O["ns_rg"][s_, l, 0, ct * 128:(ct + 1) * 128].rearrange("(p o) -> p o", o=1), hf[:, s_, L - 1:L],
                              reads=hf.d, writes=[dd], sem=ssem)
                        out_deps.append(dd)
            prev = None
            for t in reversed(range(NT)):
                pg = proj_fm(w, wd, 256 + ct * 128, 128, t)
                for (s_, off, n, a0) in seg_of(L, t):
                    tm = tmps[k % NR]; k += 1
                    gates(1, s_, off, n, tm)
                    init = h0[:, 1:2] if off + n == L else prev["hb"][:, 0:1]
                    rd = tm["a"].d + tm["u"].d + h0.d + (prev["hb"].d if prev is not None else [])
                    S.op("dve", lambda e, tm=tm, n=n, init=init: e.tensor_tensor_scan(
                        out=tm["hb"][:, n - 1::-1] if False else tm["hb"][:, 0:n][:, ::-1], data0=tm["a"][:, 0:n][:, ::-1], data1=tm["u"][:, 0:n][:, ::-1],
                        initial=init, op0=ALU.mult, op1=ALU.add), reads=rd, writes=tm["hb"].d)
                    prev = tm
                    if pi == 1 and off == 0:
                        dd = Dep()
                        S.dma("sp", O["ns_rg"][s_, l, 1, ct * 128:(ct + 1) * 128].rearrange("(p o) -> p o", o=1), tm["hb"][:, 0:1],
                              reads=tm["hb"].d, writes=[dd], sem=ssem)
                        out_deps.append(dd)
                    g = tm["g"]; ps_ = slice(a0, a0 + n)
                    act(g[:, 0:n], pg[:, ps_], AF.Square, pg.d, g.d)
                    ts("dve", g[:, 0:n], g[:, 0:n], 0.044715, 1.0, ALU.mult, ALU.add, g.d, g.d)
                    tt("dve", g[:, 0:n], g[:, 0:n], pg[:, ps_], ALU.mult, g.d + pg.d, g.d)
                    act(g[:, 0:n], g[:, 0:n], AF.Sigmoid, g.d, g.d, scale=1.5957691216057308)
                    tt("dve", g[:, 0:n], g[:, 0:n], pg[:, ps_], ALU.mult, g.d + pg.d, g.d)
                    tt("pool", tm["r"][:, 0:n], tm["hb"][:, 0:n], hf[:, s_, off:off + n], ALU.add, tm["hb"].d + hf.d, tm["r"].d)
                    y_ = yb[t % 2]
                    tt("dve", y_[:, a0:a0 + n], tm["r"][:, 0:n], g[:, 0:n], ALU.mult, tm["r"].d + g.d, y_.d)
                S.dma("sp", yTs[2 + ct, :, t * 512:(t + 1) * 512], yb[t % 2].ap, reads=yb[t % 2].d, writes=[yTs_d[t]], sem=ysems[t % 2])
        A.release(m)
        S.barrier()


    def mixer_gated(pi, l, L, NS, yTs, yTs_d, ysem, kind):
        m = A.mark()
        CH = 128
        Tn = L * NS; NT = Tn // 512; NCH = Tn // CH; CPS = L // CH
        gla = (kind == "gla")
        yc0 = 0 if gla else 6
        st_in = I["st_gla"] if gla else I["st_hg"]
        st_out = O["ns_gla"] if gla else O["ns_hg"]
        grow = SP_ROWS[("glang%d" if gla else "hgng%d") % l]
        wl = I["w_in"][l]
        la_early = A.alloc(F32, [128, 512])
        wnext[0] = 0
        scr = wring[3].ap.rearrange("p a b -> p (a b)").bitcast(F32)
        EB = [T(scr[:, i * 512:(i + 1) * 512].rearrange("p (b c) -> p b c", c=128), 1) for i in range(4)]
        if gla:
            wA, wAd = load_w(wl, 0, D, 0, 512)
            wB, wBd = load_w(wl, 0, D, 256, 512)
            wC, wCd = load_w(wl, 0, D, 768, 272)
            wg = A.alloc(BF16, [128, 512]); bgb = A.alloc(F32, [128, 512])
            S.op("pool", lambda e: e.memset(wg.ap, 0.0), writes=wg.d)
            S.dma("pool", wg[112:128, :], I["gla_wg"][l], writes=wg.d, sem=SPL[0])
            S.dma("sp", bgb.ap, I["gla_bg"][l].partition_broadcast(128), writes=bgb.d, sem=SPL[1])
            lrT = A.alloc(BF16, [128, 512])
        else:
            wA, wAd = load_w(wl, 0, D, 2320, 512)
            wB, wBd = load_w(wl, 0, D, 2832, 512)
            wC, wCd = load_w(wl, 0, D, 3344, 256)
            lb2 = A.alloc(F32, [128, 256]); om2 = A.alloc(F32, [128, 256])
            if l == 0:
                S.op("pool", lambda e: e.memset(lb2.ap, 0.0), writes=lb2.d)
                S.op("pool", lambda e: e.memset(om2.ap, 1.0), writes=om2.d)
            else:
                hl = T(la_early.ap.rearrange("p (a b) -> p a b", a=2), 0); hl.d = la_early.d
                S.dma("sp", hl[:, 0, :], I["hg_low"][0].partition_broadcast(128), writes=hl.d, sem=SPL[0])
                S.dma("sp", hl[:, 1, :], I["hg_low"][1].partition_broadcast(128), writes=hl.d, sem=SPL[1])
                tt("dve", hl[:, 0, :], hl[:, 1, :], hl[:, 0, :], ALU.subtract, hl.d, hl.d)
                act(lb2.ap, hl[:, 0, :], AF.Sigmoid, hl.d, lb2.d)
                act(om2.ap, hl[:, 0, :], AF.Sigmoid, hl.d, om2.d, scale=-1.0)
        oT = A.alloc(F32, [128, 2, Tn]); qbb = A.alloc(BF16, [128, 2, Tn])
        kvb = A.alloc(F32, [128, NCH, 2, 64]); decb = A.alloc(F32, [128, NCH, 2])
        Sf = A.alloc(F32, [128, 2, 64]); Sb = A.alloc(F32, [128, 2, 64]); Ss = A.alloc(BF16, [128, 2, 64])
        qT = A.alloc(BF16, [128, 2, 512])
        kT = [A.alloc(BF16, [128, 2, 512])] if gla else [A.alloc(BF16, [128, 2, 512]) for _ in range(2)]
        ktok = A.alloc(F32, [128, 512]); vbf = A.alloc(BF16, [128, 256]); la = la_early
        xs = A.alloc(F32, [128, 4, CH]); eq = xs; ekn = A.alloc(F32, [128, 4, CH]); ek = A.alloc(F32, [128, 512])
        dec = A.alloc(F32, [128, 4])
        qhf = A.alloc(BF16, [128, 2, CH])
        Q1 = [A.alloc(BF16, [128, 2, CH]) for _ in range(2)]; ktl = [A.alloc(BF16, [128, 2, CH]) for _ in range(2)]
        Q2 = [A.alloc(BF16, [128, 2, CH]) for _ in range(2)]; K2 = [A.alloc(BF16, [128, 2, CH]) for _ in range(2)]
        khat = [A.alloc(BF16, [128, 256]) for _ in range(2)]; Am = [A.alloc(BF16, [128, 4, CH]) for _ in range(2)]
        tri = (tril, triu)
        ssem = SPL[2]; isem = SPL[3]
        mask8 = [A.alloc(U8, [128, 4, CH]) for _ in range(2)]
        for d in range(2):
            cp("dve", mask8[d].ap, tri[d].unsqueeze(1).to_broadcast([128, 4, CH]), cm.d, mask8[d].d)
            S.op("pool", lambda e, d=d: e.memset(Am[d].ap.rearrange("p a b -> p (a b)"), 0.0), writes=Am[d].d)

        def proj_tm(w, wd, col0, ncol, ch):
            p = nps()
            for c in range(8):
                mm(p[:, 0:ncol], uT[:, c, ch * CH:(ch + 1) * CH], w[:, c, col0:col0 + ncol], c == 0, c == 7, wd + [uT.d[ch // 4]], p.d, last=(c == 7))
            return p

        def init_state(Sx, s_, d):
            if pi == 0:
                S.dma("sp", Sx.ap, st_in[l, d].rearrange("(h p) v -> p h v", p=128), writes=Sx.d, sem=(isem if d == 0 else SPL[20]))
            else:
                S.op("pool", lambda e: e.memset(Sx.ap.rearrange("p a b -> p (a b)"), 0.0), writes=Sx.d)

        def out_state(Sx, s_, d):
            if pi == 1:
                dd = Dep()
                S.dma("sp", st_out[s_, l, d].rearrange("(h p) v -> p h v", p=128), Sx.ap, reads=Sx.d, writes=[dd], sem=ssem)
                out_deps.append(dd)

        for t in range(NT):
            if gla:
                for hp in range(2):
                    p = proj_fm(wA, wAd, hp * 128, 128, t)
                    act(qT[:, hp, :], p.ap, AF.Identity, p.d, qT.d, scale=0.125, bias=0.0)
                    p = proj_fm(wA, wAd, 256 + hp * 128, 128, t)
                    cp("dve", kT[0][:, hp, :], p.ap, p.d, kT[0].d)
                p = proj_fm(wC, wCd, 144, 128, t)
                cp("dve", lrT.ap, p.ap, p.d, lrT.d)
            else:
                for hp in range(2):
                    p = proj_fm(wA, wAd, hp * 128, 128, t)
                    act(qT[:, hp, :], p.ap, AF.Silu, p.d, qT.d)
                    p = proj_fm(wA, wAd, 256 + hp * 128, 128, t)
                    act(kT[0][:, hp, :], p.ap, AF.Sigmoid, p.d, kT[0].d, scale=-1.0)
                    ts("dve", kT[0][:, hp, :], kT[0][:, hp, :], lbc[:, l, hp, 1:2], None, ALU.mult, None, kT[0].d + lbc.d, kT[0].d)
                    p = proj_fm(wB, wBd, hp * 128, 128, t)
                    act(kT[1][:, hp, :], p.ap, AF.Sigmoid, p.d, kT[1].d, scale=-1.0)
                    ts("dve", kT[1][:, hp, :], kT[1][:, hp, :], lbc[:, l, hp, 1:2], None, ALU.mult, None, kT[1].d + lbc.d, kT[1].d)
            for ci in range(4):
                ch = t * 4 + ci; s_ = ch // CPS; cpos = ch % CPS
                cs = slice(ci * CH, (ci + 1) * CH); gs = slice(ch * CH, (ch + 1) * CH)
                if cpos == 0:
                    init_state(Sf, s_, 0)
                if GSTOP == 2:
                    raise _Stop()
                if gla:
                    p = proj_tm(wB, wBd, 0, 512, ch)
                    cp("act", ktok[:, 0:256], p[:, 0:256], p.d, ktok.d)
                    cp("dve", vbf.ap, p[:, 256:512], p.d, vbf.d)
                    if GSTOP == 21:
                        raise _Stop()
                    pz = nps()
                    mm(pz.ap, lrT[:, cs], wg.ap, True, True, lrT.d + wg.d, pz.d)
                    tt("dve", la.ap, pz.ap, bgb.ap, ALU.add, pz.d + bgb.d, la.d)
                    if GSTOP == 22:
                        raise _Stop()
                    act(la.ap, la.ap, AF.Exp, la.d, la.d, scale=-1.0)
                    act(la.ap, la.ap, AF.Ln, la.d, la.d, bias=1.0)
                    ts("dve", la.ap, la.ap, -1.0 / 16.0, None, ALU.mult, None, la.d, la.d)
                    kt_d = [ktok[:, 0:256], ktok[:, 0:256]]
                else:
                    p = nps()
                    for c in range(8):
                        mm(p[:, 0:256], uT[:, c, ch * CH:(ch + 1) * CH], wA[:, c, 256:512], c == 0, c == 7, wAd + [uT.d[ch // 4]], p.d)
                    for c in range(8):
                        mm(p[:, 256:512], uT[:, c, ch * CH:(ch + 1) * CH], wB[:, c, 0:256], c == 0, c == 7, wBd + [uT.d[ch // 4]], p.d)
                    act(la.ap, p.ap, AF.Sigmoid, p.d, la.d)
                    la3 = la.ap.rearrange("p (a b) -> p a b", a=2); kt3 = ktok.ap.rearrange("p (a b) -> p a b", a=2)
                    omB = om2.ap.unsqueeze(1).to_broadcast([128, 2, 256]); lbB = lb2.ap.unsqueeze(1).to_broadcast([128, 2, 256])
                    tt("dve", la3, la3, omB, ALU.mult, la.d + om2.d, la.d)
                    tt("dve", kt3, omB, la3, ALU.subtract, la.d + om2.d, ktok.d)
                    tt("dve", la3, la3, lbB, ALU.add, la.d + lb2.d, la.d)
                    act(la.ap, la.ap, AF.Ln, la.d, la.d)
                    p = proj_tm(wB, wBd, 256, 256, ch)
                    cp("dve", vbf.ap, p[:, 0:256], p.d, vbf.d)
                    kt_d = [ktok[:, 0:256], ktok[:, 256:512]]
                if GSTOP == 3:
                    raise _Stop()
                pb = nps(); pt = nps()
                for d in range(2):
                    for hp in range(2):
                        blk = d * 2 + hp
                        mm(pb[:, blk * CH:(blk + 1) * CH], la[:, d * 256 + hp * 128:d * 256 + (hp + 1) * 128], tri[d], True, True, la.d + cm.d, pb.d)
                mm(pt[:, 0:256], su, la[:, 0:256], True, True, la.d + cm.d, pt.d)
                mm(pt[:, 256:512], sl, la[:, 256:512], True, True, la.d + cm.d, pt.d)
                H2 = CH // 2
                pbv = pb.ap.rearrange("p (b c) -> p b c", c=CH)
                for d in range(2):
                    endc = CH - 1 if d == 0 else 0
                    act(dec[:, d * 2:d * 2 + 2], pbv[:, d * 2:d * 2 + 2, endc], AF.Exp, pb.d, dec.d)
                act(ek.ap, pt.ap, AF.Exp, pt.d, ek.d)
                for d in range(2):
                    kTd = kT[0] if gla else kT[d]
                    tt("dve", khat[d].ap, kt_d[d], ek[:, d * 256:(d + 1) * 256], ALU.mult, ktok.d + ek.d, khat[d].d)
                x2 = EB[0]
                for d in range(2):
                    bcol = H2 - 1 + d
                    for hp in range(2):
                        blk = d * 2 + hp
                        for hf_ in range(2):
                            mid = hf_ * H2 + H2 // 2 - 1 + d
                            cols = slice(hf_ * H2, (hf_ + 1) * H2)
                            ts("dve", xs[:, blk, cols], pb[:, blk * CH + hf_ * H2:blk * CH + (hf_ + 1) * H2], pb[:, blk * CH + mid:blk * CH + mid + 1], -80.0,
                               ALU.subtract, ALU.max, pb.d, xs.d)
                        ts("dve", x2[:, blk, :], pb[:, blk * CH:(blk + 1) * CH], pb[:, blk * CH + bcol:blk * CH + bcol + 1], -80.0,
                           ALU.subtract, ALU.max, pb.d, x2.d)
                ts("dve", xs.ap, xs.ap, 80.0, None, ALU.min, None, xs.d, xs.d)
                ts("dve", x2.ap, x2.ap, 80.0, None, ALU.min, None, x2.d, x2.d)
                e1p = ekn; e1n = EB[1]; e2p = EB[2]; e2n = EB[3]
                act(e1p.ap, xs.ap, AF.Exp, xs.d, e1p.d)
                act(e1n.ap, xs.ap, AF.Exp, xs.d, e1n.d, scale=-1.0)
                act(e2p.ap, x2.ap, AF.Exp, x2.d, e2p.d)
                act(e2n.ap, x2.ap, AF.Exp, x2.d, e2n.d, scale=-1.0)
                for d in range(2):
                    kTd = kT[0] if gla else kT[d]
                    tt("pool" if d else "dve", Q1[d].ap, qT[:, :, cs], e1p[:, d * 2:d * 2 + 2, :], ALU.mult, qT.d + e1p.d, Q1[d].d)
                    tt("pool" if d else "dve", ktl[d].ap, kTd[:, :, cs], e1n[:, d * 2:d * 2 + 2, :], ALU.mult, kTd.d + e1n.d, ktl[d].d)
                    tt("pool", Q2[d].ap, qT[:, :, cs], e2p[:, d * 2:d * 2 + 2, :], ALU.mult, qT.d + e2p.d, Q2[d].d)
                    tt("pool" if d else "dve", K2[d].ap, kTd[:, :, cs], e2n[:, d * 2:d * 2 + 2, :], ALU.mult, kTd.d + e2n.d, K2[d].d)
                ebq = xs
                act(ebq.ap, pbv, AF.Exp, pb.d + xs.d, ebq.d)
                tt("dve", qhf.ap, qT[:, :, cs], ebq[:, 0:2, :], ALU.mult, qT.d + ebq.d, qhf.d)
                tt("pool", qbb[:, :, gs], qT[:, :, cs], ebq[:, 2:4, :], ALU.mult, qT.d + ebq.d, qbb.d)
                if GSTOP == 5:
                    raise _Stop()
                for d in range(2):
                    psc = nps()
                    for h in (0, 2, 1, 3):
                        hp, j = h // 2, h % 2
                        r_ = slice(j * 64, (j + 1) * 64)
                        lo = slice(0, H2); hi = slice(H2, CH)
                        mm(psc[0:H2, h * CH:h * CH + H2], ktl[d][r_, hp, lo], Q1[d][r_, hp, lo], True, True, ktl[d].d + Q1[d].d, psc.d, rg=j, mode=(64, 64))
                        mm(psc[H2:CH, h * CH + H2:(h + 1) * CH], ktl[d][r_, hp, hi], Q1[d][r_, hp, hi], True, True, ktl[d].d + Q1[d].d, psc.d, rg=j, mode=(64, 64))
                        if d == 0:
                            mm(psc[0:H2, h * CH + H2:(h + 1) * CH], K2[0][r_, hp, lo], Q2[0][r_, hp, hi], True, True, K2[0].d + Q2[0].d, psc.d, rg=j, mode=(64, 64))
                        else:
                            mm(psc[H2:CH, h * CH:h * CH + H2], K2[1][r_, hp, hi], Q2[1][r_, hp, lo], True, True, K2[1].d + Q2[1].d, psc.d, rg=j, mode=(64, 64))
                    S.op("dve", lambda e, d=d, psc=psc: e.copy_predicated(out=Am[d].ap, mask=mask8[d].ap, data=psc.ap.rearrange("p (h c) -> p h c", c=CH)),
                         reads=psc.d + mask8[d].d + Am[d].d, writes=Am[d].d)
                if GSTOP == 6:
                    raise _Stop()
                cp("dve", Ss.ap, Sf.ap, Sf.d, Ss.d)
                po = nps(); pq = nps()
                for h in range(4):
                    hp, j = h // 2, h % 2
                    o_ = po[j * 64:(j + 1) * 64, hp * CH:(hp + 1) * CH]
                    mm(o_, vbf[:, h * 64:(h + 1) * 64], Am[0][:, h, :], True, False, vbf.d + Am[0].d, po.d, mode=(128, 64))
                    mm(o_, vbf[:, h * 64:(h + 1) * 64], Am[1][:, h, :], False, True, vbf.d + Am[1].d, po.d, mode=(128, 64))
                for h in (0, 2, 1, 3):
                    hp, j = h // 2, h % 2
                    mm(pq[j * 64:(j + 1) * 64, hp * CH:(hp + 1) * CH], Ss[j * 64:(j + 1) * 64, hp, :], qhf[j * 64:(j + 1) * 64, hp, :], True, True,
                       Ss.d + qhf.d, pq.d, rg=j, mode=(64, 64))
                cp("act", oT[:, :, gs], po[:, 0:2 * CH].rearrange("p (h c) -> p h c", c=CH), po.d, oT.d)
                tt("dve", oT[:, :, gs], oT[:, :, gs], pq[:, 0:2 * CH].rearrange("p (h c) -> p h c", c=CH), ALU.add, oT.d + pq.d, oT.d)
                if GSTOP == 7:
                    raise _Stop()
                pkv = nps()
                for d in range(2):
                    for hp in range(2):
                        blk = d * 2 + hp
                        mm(pkv[:, blk * 128:(blk + 1) * 128], khat[d][:, hp * 128:(hp + 1) * 128], vbf[:, hp * 128:(hp + 1) * 128], True, True,
                           khat[d].d + vbf.d, pkv.d)
                for hp in range(2):
                    for j in range(2):
                        r_ = slice(j * 64, (j + 1) * 64)
                        stt(Sf[r_, hp, :], Sf[r_, hp, :], dec[r_, hp:hp + 1], pkv[r_, hp * 128 + j * 64:hp * 128 + (j + 1) * 64], ALU.mult, ALU.add,
                            Sf.d + dec.d + pkv.d, Sf.d)
                        cp("act", kvb[r_, ch, hp, :], pkv[r_, (2 + hp) * 128 + j * 64:(2 + hp) * 128 + (j + 1) * 64], pkv.d, kvb.d)
                cp("act", decb[:, ch, :], dec[:, 2:4], dec.d, decb.d)
                if cpos == CPS - 1:
                    out_state(Sf, s_, 0)
                if GSTOP == 8:
                    raise _Stop()
        if GSTOP == 9:
            raise _Stop()
        Ss2 = [Ss, A.alloc(BF16, [128, 2, 64])]
        pend = None
        for idx, ch in enumerate(reversed(range(NCH))):
            s_ = ch // CPS; cpos = ch % CPS
            gs = slice(ch * CH, (ch + 1) * CH)
            if cpos == CPS - 1:
                init_state(Sb, s_, 1)
            Sx_ = Ss2[idx % 2]
            cp("dve", Sx_.ap, Sb.ap, Sb.d, Sx_.d)
            po = nps()
            for h in (0, 2, 1, 3):
                hp, j = h // 2, h % 2
                mm(po[j * 64:(j + 1) * 64, hp * CH:(hp + 1) * CH], Sx_[j * 64:(j + 1) * 64, hp, :], qbb[j * 64:(j + 1) * 64, hp, gs], True, True,
                   Sx_.d + qbb.d, po.d, rg=j, mode=(64, 64))
            for hp in range(2):
                stt(Sb[:, hp, :], Sb[:, hp, :], decb[:, ch, hp:hp + 1], kvb[:, ch, hp, :], ALU.mult, ALU.add, Sb.d + decb.d + kvb.d, Sb.d)
            if cpos == 0:
                out_state(Sb, s_, 1)
            if pend is not None:
                ppo, pgs = pend
                tt("dve", oT[:, :, pgs], oT[:, :, pgs], ppo[:, 0:2 * CH].rearrange("p (h c) -> p h c", c=CH), ALU.add, oT.d + ppo.d, oT.d)
            pend = (po, gs)
        ppo, pgs = pend
        tt("dve", oT[:, :, pgs], oT[:, :, pgs], ppo[:, 0:2 * CH].rearrange("p (h c) -> p h c", c=CH), ALU.add, oT.d + ppo.d, oT.d)
        if GSTOP == 10:
            raise _Stop()
        sg = la; t1 = ek; rs = ktok
        yb = [T(Am[i].ap.rearrange("p a b -> p (a b)"), 0) for i in range(2)]; yb[0].d = Am[0].d; yb[1].d = Am[1].d
        A_sq = T(xs.ap.rearrange("p a b -> p (a b)").bitcast(BF16)[:, 0:512], 0); A_sq.d = xs.d
        ysems = [SPL[4], SPL[5]]
        k = 0
        for t in range(NT):
            for hp in range(2):
                pg = proj_fm(wC, wCd, hp * 128, 128, t)
                act(sg.ap, pg.ap, AF.Silu, pg.d, sg.d)
                o_ = oT[:, hp, t * 512:(t + 1) * 512]
                sq2 = A_sq
                act(sq2.ap, o_, AF.Square, oT.d, sq2.d)
                pm = nps()
                mm(pm.ap, bd64.ap, sq2.ap, True, True, bd64.d + sq2.d, pm.d)
                act(rs.ap, pm.ap, AF.Sqrt, pm.d, rs.d, bias=EPS)
                S.op("dve", lambda e: e.reciprocal(out=rs.ap, in_=rs.ap), reads=rs.d, writes=rs.d)
                tt("dve", t1.ap, o_, rs.ap, ALU.mult, oT.d + rs.d, t1.d)
                y_ = yb[k % 2]
                stt(y_.ap, t1.ap, C(grow + hp), sg.ap, ALU.mult, ALU.mult, t1.d + sg.d + colT.d, y_.d)
                S.dma("sp", yTs[yc0 + hp, :, t * 512:(t + 1) * 512], y_.ap, reads=y_.d, writes=[yTs_d[t]], sem=ysems[k % 2])
                k += 1
        A.release(m)
        S.barrier()


    def mixer_hy(pi, l, L, NS, yTs, yTs_d, ysem):
        m = A.mark()
        Tn = L * NS; NT = Tn // 512; NCHL = L // 128; NCH = Tn // 128
        N = 6144 if L == 4096 else 512
        NFB = (N // 2) // 128
        Gc = I["gc%d" % L]; Gs = I["gs%d" % L]
        wl = I["w_in"][l]
        ucs = dscr("ucs%d_%d" % (pi, l), [Tn, 768]); ucs_d = [Dep() for _ in range(NT)]
        zscr = dscr("zscr%d_%d" % (pi, l), [Tn, 256]); zscr_d = [Dep() for _ in range(NCH)]
        ZW = NS * 256 + 256
        HC = NS * 256
        zh = A.alloc(BF16, [128, NCHL, ZW])
        h2T = A.alloc(F32, [128, L])
        decB = A.alloc(F32, [128, 512]); skipB = A.alloc(F32, [128, 512]); rsum = A.alloc(F32, [128, 512])
        ndist = A.alloc(F32, [128, NCHL]); ecs = A.alloc(F32, [128, 2, NFB])
        W3 = A.alloc(F32, [128, 512])
        S.op("pool", lambda e: e.memset(W3.ap, 0.0), writes=W3.d)
        S.op("pool", lambda e: e.memset(h2T.ap, 0.0), writes=h2T.d)
        S.dma("sp", decB.ap, I["hy_dec"][l].partition_broadcast(128), writes=decB.d, sem=SPL[0])
        S.dma("sp", skipB.ap, I["hy_skip"][l].partition_broadcast(128), writes=skipB.d, sem=SPL[1])
        S.dma("sp", ecs.ap, I["e%d" % L], writes=ecs.d, sem=SPL[2])
        S.dma("sp", W3[0:64, :], I["hy_w3"][l], writes=W3.d, sem=SPL[3])
        mf = A.mark()
        W1 = A.alloc(F32, [128, 128]); W2 = A.alloc(F32, [128, 128])
        S.op("pool", lambda e: e.memset(W1.ap, 0.0), writes=W1.d)
        S.op("pool", lambda e: e.memset(W2.ap, 0.0), writes=W2.d)
        S.dma("sp", W1[0:33, 0:64], I["hy_w1"][l], writes=W1.d, sem=SPL[4])
        S.dma("sp", W2[0:64, 0:64], I["hy_w2"][l], writes=W2.d, sem=SPL[5])
        posi = A.alloc(I32, [128, 512]); posf = A.alloc(F32, [128, 512]); arg = A.alloc(F32, [128, 512])
        ti = A.alloc(I32, [128, 512]); tf = A.alloc(F32, [128, 512]); pe = A.alloc(F32, [128, 512]); h1 = A.alloc(F32, [128, 512])
        pcr = SP_ROWS["pec%d" % L]
        S.op("pool", lambda e: e.memset(pe.ap, 0.0), writes=pe.d)
        S.op("pool", lambda e: e.memset(h1.ap, 0.0), writes=h1.d)
        for j in range(L // 512 if L >= 512 else 1):
            w_ = min(512, L)
            S.op("pool", lambda e, j=j, w_=w_: e.iota(posi[:, 0:w_], pattern=[[1, w_]], base=j * 512, channel_multiplier=0), writes=posi.d)
            cp("dve", posf[:, 0:w_], posi[:, 0:w_], posi.d, posf.d)
            ts("dve", arg[0:33, 0:w_], posf[0:33, 0:w_], C(pcr)[0:33, :], C(pcr + 1)[0:33, :], ALU.mult, ALU.add, posf.d + colT.d, arg.d)
            a33 = T(arg[0:33, 0:w_], 0); a33.d = arg.d
            i33 = T(ti[0:33, 0:w_], 0); i33.d = ti.d
            f33 = T(tf[0:33, 0:w_], 0); f33.d = tf.d
            sin_rr(pe[0:33, 0:w_], a33, i33, f33, [], pe.d)
            ts("dve", pe[0:1, 0:w_], posf[0:1, 0:w_], C(pcr)[0:1, :], None, ALU.mult, None, posf.d + colT.d + pe.d, pe.d)
            p = nps()
            mm(p[:, 0:w_], W1.ap, pe[:, 0:w_], True, True, W1.d + pe.d, p.d)
            act(arg[0:64, 0:w_], p[0:64, 0:w_], AF.Identity, p.d + colT.d, arg.d, bias=C(SP_ROWS["hyb1%d" % l])[0:64, :], scale=1.0)
            a64 = T(arg[0:64, 0:w_], 0); a64.d = arg.d
            i64 = T(ti[0:64, 0:w_], 0); i64.d = ti.d
            f64 = T(tf[0:64, 0:w_], 0); f64.d = tf.d
            sin_rr(h1[0:64, 0:w_], a64, i64, f64, [], h1.d)
            p = nps()
            mm(p[:, 0:w_], W2.ap, h1[:, 0:w_], True, True, W2.d + h1.d, p.d)
            act(arg[0:64, 0:w_], p[0:64, 0:w_], AF.Identity, p.d + colT.d, arg.d, bias=C(SP_ROWS["hyb2%d" % l])[0:64, :], scale=1.0)
            sin_rr(h2T[0:64, j * 512:j * 512 + w_], a64, i64, f64, [], h2T.d)
            if GSTOP == 30:
                raise _Stop()
        if GSTOP == 31:
            raise _Stop()
        ndi = T(posi[:, 0:NCHL], 0); ndi.d = posi.d
        S.op("pool", lambda e: e.iota(ndi.ap, pattern=[[128, NCHL]], base=0, channel_multiplier=1), writes=posi.d)
        cp("dve", ndist.ap, ndi.ap, posi.d, ndist.d)
        ts("dve", ndist.ap, ndist.ap, -float(L // 2), None, ALU.add, None, ndist.d, ndist.d)
        act(ndist.ap, ndist.ap, AF.Abs, ndist.d, ndist.d)
        ts("dve", ndist.ap, ndist.ap, -2.0 / L, None, ALU.mult, None, ndist.d, ndist.d)
        hr = pe; ee = h1; hab = A.alloc(BF16, [128, 512])
        acc = PS[7]
        for tc in range(NCHL):
            p = nps()
            mm(p.ap, h2T[:, tc * 128:(tc + 1) * 128], W3.ap, True, True, h2T.d + W3.d, p.d)
            act(ee.ap, decB.ap, AF.Exp, decB.d + ndist.d, ee.d, scale=ndist[:, tc:tc + 1])
            tt("dve", hr.ap, p.ap, ee.ap, ALU.mult, p.d + ee.d, hr.d)
            act(hab.ap, hr.ap, AF.Abs, hr.d, hab.d)
            mm(acc.ap, onesm.ap, hab.ap, tc == 0, tc == NCHL - 1, onesm.d + hab.d, acc.d)
        ts("dve", rsum.ap, acc.ap, float(D), None, ALU.mult, None, acc.d, rsum.d)
        S.op("dve", lambda e: e.reciprocal(out=rsum.ap, in_=rsum.ap), reads=rsum.d, writes=rsum.d)
        A.release(mf)
        S.barrier()
        if GSTOP == 32:
            raise _Stop()
        mc = A.mark()
        w1s, w1d = load_w(wl, 0, D, IN_OFF["c_v"], 512)
        w2s, w2d = load_w(wl, 0, D, IN_OFF["c_x2"], 256)
        xpad = A.alloc(F32, [128, NS, L + 2]); xc = A.alloc(F32, [128, NS, L])
        stg = [A.alloc(F32, [128, 4, 128]) for _ in range(2)]
        usem = [SPL[6], SPL[7]]
        hcw = SP_ROWS["hycw%d" % l]; hcb = SP_ROWS["hycb%d" % l]
        k = 0
        for ct in range(6):
            ws, wsd, c0 = (w1s, w1d, ct * 128) if ct < 4 else (w2s, w2d, (ct - 4) * 128)
            S.op("pool", lambda e: e.memset(xpad.ap.rearrange("p a b -> p (a b)"), 0.0), writes=xpad.d)
            for t in range(NT):
                p = proj_fm(ws, wsd, c0, 128, t)
                for (s_, off, n, a0) in seg_of(L, t):
                    cp("act", xpad[:, s_, 1 + off:1 + off + n], p[:, a0:a0 + n], p.d, xpad.d)
            for s_ in range(NS):
                ts("dve", xc[:, s_, :], xpad[:, s_, 0:L], C(hcw + 0 * 6 + ct), C(hcb + ct), ALU.mult, ALU.add, xpad.d + colT.d, xc.d)
                for j in (1, 2):
                    stt(xc[:, s_, :], xpad[:, s_, j:j + L], C(hcw + j * 6 + ct), xc[:, s_, :], ALU.mult, ALU.add, xpad.d + colT.d + xc.d, xc.d)
            for t in range(NT):
                p = nps()
                for ci in range(4):
                    ch = t * 4 + ci; s_ = ch // NCHL; tcs = ch % NCHL
                    S.op("pe", lambda e, p=p, ci=ci, s_=s_, tcs=tcs: e.transpose(p[:, ci * 128:(ci + 1) * 128], xc[:, s_, tcs * 128:(tcs + 1) * 128], ident),
                         reads=xc.d + cm.d, writes=p.d)
                sg_ = stg[k % 2]
                cp("act", sg_.ap, p.ap.rearrange("p (c n) -> p c n", c=4), p.d, sg_.d)
                S.dma("sp", ucs[t * 512:(t + 1) * 512, ct * 128:(ct + 1) * 128].rearrange("(c p) n -> p c n", p=128), sg_.ap,
                      reads=sg_.d, writes=[ucs_d[t]], sem=usem[k % 2])
                k += 1
                if ct < 2:
                    for ci in range(4):
                        ch = t * 4 + ci; s_ = ch // NCHL; tcs = ch % NCHL
                        cp("dve", zh[:, tcs, s_ * 256 + ct * 128:s_ * 256 + (ct + 1) * 128], sg_[:, ci, :], sg_.d, zh.d)
        A.release(mc)
        S.barrier()
        if GSTOP == 33:
            raise _Stop()
        uflat = uT.ap.rearrange("p a b -> p (a b)")
        PW = NFB * NS * 2 * 256
        P_ = T(uflat[:, 0:PW].rearrange("p (f s r n) -> p f s r n", f=NFB, s=NS, r=2), 1)
        tabs = [T(uflat[:, 12288 + i * 4096:12288 + (i + 1) * 4096].rearrange("p (c j) -> p c j", j=128), 1) for i in range(4)]
        tsem = [SPL[8 + i] for i in range(4)]
        tn_ = [0]

        def load_tab(G, blk, nchunk):
            i = tn_[0] % 4
            tn_[0] += 1
            tb_ = tabs[i]
            S.dma("sp", tb_[:, 0:nchunk, :], G[blk][:, 0:nchunk, :], writes=tb_.d, sem=tsem[i])
            return tb_

        mg = A.mark()
        AB = A.alloc(F32, [128, 2, 256]); tA = A.alloc(F32, [128, 256]); tB = A.alloc(F32, [128, 256])
        gt = [A.alloc(F32, [128, 256]) for _ in range(2)]; zt = [A.alloc(F32, [128, 256]) for _ in range(2)]
        gsm = [SPL[12], SPL[13]]; zsm = [SPL[14], SPL[15]]; zssem = [SPL[16], SPL[17]]; ysm = [SPL[18], SPL[19]]
        zn = [A.alloc(F32, [128, 256]) for _ in range(2)]
        ystg = [A.alloc(BF16, [128, 2, 512]) for _ in range(2)]
        hr2 = A.alloc(F32, [128, 256]); ee2 = A.alloc(F32, [128, 256])
        for n in range(2):
            nc0 = n * 256
            if GSTOP == 35 and n == 1:
                raise _Stop()
            for tc in range(NCHL):
                p = nps()
                mm(p[:, 0:256], h2T[:, tc * 128:(tc + 1) * 128], W3[:, nc0:nc0 + 256], True, True, h2T.d + W3.d, p.d)
                act(ee2.ap, decB[:, nc0:nc0 + 256], AF.Exp, decB.d + ndist.d, ee2.d, scale=ndist[:, tc:tc + 1])
                tt("dve", hr2.ap, p[:, 0:256], ee2.ap, ALU.mult, p.d + ee2.d, hr2.d)
                tt("dve", zh[:, tc, HC:HC + 256], hr2.ap, rsum[:, nc0:nc0 + 256], ALU.mult, hr2.d + rsum.d, zh.d)
            for fb in range(NFB):
                tcb = load_tab(Gc, fb, NCHL); tsb = load_tab(Gs, fb, NCHL)
                pcm = psm = None
                if NS == 1:
                    pcm = nps(); psm = nps()
                    for tc in range(NCHL):
                        mm(pcm.ap, tcb[:, tc, :], zh[:, tc, 0:512], tc == 0, tc == NCHL - 1, tcb.d + zh.d, pcm.d, last=(tc == NCHL - 1))
                    for tc in range(NCHL):
                        mm(psm.ap, tsb[:, tc, :], zh[:, tc, 0:512], tc == 0, tc == NCHL - 1, tsb.d + zh.d, psm.d, last=(tc == NCHL - 1))
                for g in [NS] + list(range(NS)):
                    c0 = g * 256
                    if NS == 1:
                        pc = T(pcm[:, c0:c0 + 256], 0); pc.d = pcm.d
                        ps_ = T(psm[:, c0:c0 + 256], 0); ps_.d = psm.d
                    else:
                        pc = nps(); ps_ = nps()
                        for tc in range(NCHL):
                            mm(pc[:, 0:256], tcb[:, tc, :], zh[:, tc, c0:c0 + 256], tc == 0, tc == NCHL - 1, tcb.d + zh.d, pc.d, last=(tc == NCHL - 1))
                        for tc in range(NCHL):
                            mm(ps_[:, 0:256], tsb[:, tc, :], zh[:, tc, c0:c0 + 256], tc == 0, tc == NCHL - 1, tsb.d + zh.d, ps_.d, last=(tc == NCHL - 1))
                    ec = ecs[:, 0, fb:fb + 1]; es = ecs[:, 1, fb:fb + 1]
                    if g == NS:
                        ts("dve", AB[:, 0, :], pc[:, 0:256], ec, None, ALU.mult, None, pc.d + ecs.d, AB.d)
                        stt(AB[:, 0, :], ps_[:, 0:256], es, AB[:, 0, :], ALU.mult, ALU.add, ps_.d + ecs.d + AB.d, AB.d)
                        ts("dve", AB[:, 1, :], pc[:, 0:256], es, None, ALU.mult, None, pc.d + ecs.d, AB.d)
                        ts("dve", tA.ap, ps_[:, 0:256], ec, None, ALU.mult, None, ps_.d + ecs.d, tA.d)
                        tt("dve", AB[:, 1, :], AB[:, 1, :], tA.ap, ALU.subtract, AB.d + tA.d, AB.d)
                    else:
                        tt("dve", tA.ap, pc[:, 0:256], AB[:, 0, :], ALU.mult, pc.d + AB.d, tA.d)
                        tt("dve", tB.ap, ps_[:, 0:256], AB[:, 1, :], ALU.mult, ps_.d + AB.d, tB.d)
                        tt("pool", P_[:, fb, g, 0, :], tA.ap, tB.ap, ALU.add, tA.d + tB.d, P_.d)
                        tt("dve", tA.ap, ps_[:, 0:256], AB[:, 0, :], ALU.mult, ps_.d + AB.d, tA.d)
                        tt("dve", tB.ap, pc[:, 0:256], AB[:, 1, :], ALU.mult, pc.d + AB.d, tB.d)
                        tt("pool", P_[:, fb, g, 1, :], tA.ap, tB.ap, ALU.subtract, tA.d + tB.d, P_.d)
            if GSTOP == 34:
                raise _Stop()
            k = 0
            for s_ in range(NS):
                for tb in range(NCHL):
                    ch = s_ * NCHL + tb
                    rows = slice(ch * 128, (ch + 1) * 128)
                    tcb = load_tab(Gc, tb, NFB); tsb = load_tab(Gs, tb, NFB)
                    g_ = gt[k % 2]; z_ = zt[k % 2]
                    S.dma("sp", g_.ap, ucs[rows, (n + 1) * 256:(n + 2) * 256], reads=[ucs_d[ch // 4]], writes=g_.d, sem=gsm[k % 2])
                    if n == 0:
                        S.dma("sp", z_.ap, ucs[rows, 0:256], reads=[ucs_d[ch // 4]], writes=z_.d, sem=zsm[k % 2])
                    else:
                        S.dma("sp", z_.ap, zscr[rows, :], reads=[zscr_d[ch]], writes=z_.d, sem=zsm[k % 2])
                    py = nps()
                    for fc in range(NFB):
                        mm(py[:, 0:256], tcb[:, fc, :], P_[:, fc, s_, 0, :], fc == 0, False, tcb.d + P_.d, py.d, last=False)
                    for fc in range(NFB):
                        mm(py[:, 0:256], tsb[:, fc, :], P_[:, fc, s_, 1, :], False, fc == NFB - 1, tsb.d + P_.d, py.d, last=(fc == NFB - 1))
                    o_ = zn[k % 2]
                    tt("pool", o_.ap, z_.ap, skipB[:, nc0:nc0 + 256], ALU.mult, z_.d + skipB.d, o_.d)
                    tt("dve", o_.ap, o_.ap, py[:, 0:256], ALU.add, o_.d + py.d, o_.d)
                    tt("dve", o_.ap, o_.ap, g_.ap, ALU.mult, o_.d + g_.d, o_.d)
                    if n == 0:
                        cp("act", zh[:, tb, s_ * 256:(s_ + 1) * 256], o_.ap, o_.d, zh.d)
                        S.dma("sp", zscr[rows, :], o_.ap, reads=o_.d, writes=[zscr_d[ch]], sem=zssem[k % 2])
                    else:
                        t = ch // 4; ci = ch % 4
                        ys_ = ystg[t % 2]
                        pt = nps()
                        for j in range(2):
                            S.op("pe", lambda e, pt=pt, j=j, o_=o_: e.transpose(pt[:, j * 128:(j + 1) * 128], o_[:, j * 128:(j + 1) * 128], ident),
                                 reads=o_.d + cm.d, writes=pt.d)
                        cp("act", ys_[:, :, ci * 128:(ci + 1) * 128], pt[:, 0:256].rearrange("p (j n) -> p j n", j=2), pt.d, ys_.d)
                        if ci == 3:
                            S.dma("sp", yTs[4:6, :, t * 512:(t + 1) * 512].rearrange("c p n -> p c n"), ys_.ap, reads=ys_.d, writes=[yTs_d[t]], sem=ysm[t % 2])
                    k += 1
        A.release(mg)
        A.release(m)
        S.barrier()

    def norm_mod(xT, xd, acols, bcols, out_ap, out_d, tmp):
        rs = tmp["rs"]
        p = nps()
        for c in range(8):
            sq = tmp["sq"][c % 2]
            act(sq.ap, xT[:, c, :], AF.Square, xd, sq.d)
            mm(p.ap, onesm.ap, sq.ap, c == 0, c == 7, sq.d + onesm.d, p.d, last=True)
        act(rs.ap, p.ap, AF.Sqrt, p.d, rs.d, bias=EPS)
        S.op("dve", lambda e: e.reciprocal(out=rs.ap, in_=rs.ap), reads=rs.d, writes=rs.d)
        for c in range(8):
            xn = tmp["xn"][c % 2]
            tt("dve", xn.ap, xT[:, c, :], rs.ap, ALU.mult, xd + rs.d, xn.d)
            if bcols is None:
                act(out_ap[:, c, :], xn.ap, AF.Identity, xn.d + colT.d, out_d, scale=acols[:, c:c + 1], bias=0.0)
            else:
                act(out_ap[:, c, :], xn.ap, AF.Identity, xn.d + acol.d + modc.d, out_d,
                    scale=acols[:, c:c + 1], bias=bcols[:, c:c + 1])

    def mk_tmp():
        return dict(sq=[A.alloc(BF16, [128, 512]) for _ in range(2)], rs=A.alloc(F32, [128, 512]),
                    xn=[A.alloc(F32, [128, 512]) for _ in range(2)])

    def run_part(pi, L, NS, x_in, y_out):
        Tn = L * NS
        NT = Tn // 512
        xTs = dscr("xTs%d" % pi, [8, 128, Tn]); xTs_d = [Dep() for _ in range(NT)]
        yTs = dscr("yTs%d" % pi, [8, 128, Tn], BF16); yTs_d = [Dep() for _ in range(NT)]
        xsem = S.new_dsem(); xssem = S.new_dsem(); ysem = S.new_dsem(); osem = [S.new_dsem(), S.new_dsem()]
        tsem = [S.new_dsem(), S.new_dsem()]

        m1 = A.mark()
        xtok = [A.alloc(F32, [128, D]) for _ in range(2)]
        xT = A.alloc(F32, [128, 8, 512])
        tmp = mk_tmp()
        if pi == 0:
            om = A.alloc(F32, [128, 512]); ph = A.alloc(F32, [128, 512]); rc = A.alloc(F32, [128, 2])
            pcol = A.alloc(F32, [128, 512]); prow = A.alloc(F32, [128, 512]); arg = A.alloc(F32, [128, 512])
            ti = A.alloc(I32, [128, 512]); tf = A.alloc(F32, [128, 512]); rv = A.alloc(F32, [128, 1])
            S.dma("sp", om.ap, I["gridc"][0, 0:512].partition_broadcast(128), writes=om.d, sem=gsem[2])
            S.dma("sp", ph.ap, I["gridc"][1, 0:512].partition_broadcast(128), writes=ph.d, sem=gsem[3])
            S.dma("sp", rc.ap, I["gridrc"], writes=rc.d, sem=gsem[4])
            stt(arg.ap, om.ap, rc[:, 1:2], ph.ap, ALU.mult, ALU.add, om.d + ph.d + rc.d, arg.d)
            sin_rr(pcol.ap, arg, ti, tf, [], pcol.d)
        for t in range(NT):
            for j in range(4):
                blk = t * 4 + j
                xk = xtok[blk % 2]
                S.dma("sp", xk.ap, x_in[blk * 128:(blk + 1) * 128, :], writes=xk.d, sem=tsem[blk % 2])
                if pi == 0:
                    ts("dve", rv.ap, rc[:, 0:1], float(2 * blk), None, ALU.add, None, rc.d, rv.d)
                    stt(arg.ap, om.ap, rv[:, 0:1], ph.ap, ALU.mult, ALU.add, om.d + ph.d + rv.d, arg.d)
                    sin_rr(prow.ap, arg, ti, tf, [], prow.d)
                    tt("dve", xk[:, 0:512], xk[:, 0:512], prow.ap, ALU.add, xk.d + prow.d, xk.d)
                    tt("dve", xk[:, 512:1024], xk[:, 512:1024], pcol.ap, ALU.add, xk.d + pcol.d, xk.d)
                for h in range(2):
                    p = nps()
                    for c4 in range(4):
                        c = h * 4 + c4
                        S.op("pe", lambda e, p=p, c4=c4, c=c, xk=xk: e.transpose(p[:, c4 * 128:(c4 + 1) * 128], xk[:, c * 128:(c + 1) * 128], ident),
                             reads=xk.d + cm.d, writes=p.d, inc=(c4 == 3))
                    cp("act", xT[:, h * 4:h * 4 + 4, j * 128:(j + 1) * 128], p.ap.rearrange("p (c n) -> p c n", c=4), p.d, xT.d)
            S.dma("sp", xTs[:, :, t * 512:(t + 1) * 512].rearrange("c p n -> p c n"), xT.ap, reads=xT.d, writes=[xTs_d[t]], sem=xssem)
            norm_mod(xT.ap, xT.d, acol[:, 0, 0, pi, :], modc[:, 0, 0:8, pi], uT[:, :, t * 512:(t + 1) * 512], [uT.d[t]], tmp)
        A.release(m1)
        S.barrier()
        if stop == 1:
            return

        for l in range(NL):
            m2 = A.mark()
            zt = A.alloc(BF16, [128, 8, 512])
            S.op("pool", lambda e: e.memset(zt.ap, 0.0), writes=zt.d)
            done = set()
            if "gla" in flags:
                mixer_gated(pi, l, L, NS, yTs, yTs_d, ysem, "gla"); done.add(0)
            if "rg" in flags:
                mixer_rg(pi, l, L, NS, yTs, yTs_d, ysem); done.add(1)
            if "hg" in flags:
                mixer_gated(pi, l, L, NS, yTs, yTs_d, ysem, "hg"); done.add(3)
            if "hy" in flags:
                mixer_hy(pi, l, L, NS, yTs, yTs_d, ysem); done.add(2)
            for q in range(4):
                if q not in done:
                    for t in range(NT):
                        S.dma("sp", yTs[2 * q:2 * q + 2, :, t * 512:(t + 1) * 512].rearrange("c p n -> p c n"), zt[:, 0:2, :],
                              reads=zt.d, writes=[yTs_d[t]], sem=ysem)
            A.release(m2)
            S.barrier()
            if stop == 2:
                return

            m3 = A.mark()
            NP = 2
            xTl = [A.alloc(F32, [128, 8, 512]) for _ in range(NP)]; yTl = [A.alloc(BF16, [128, 8, 512]) for _ in range(NP)]
            u2l = [A.alloc(BF16, [128, 8, 512]) for _ in range(NP)]
            hTl = [[A.alloc(BF16, [128, 4, 512]) for _ in range(2)] for _ in range(NP)]
            hs = [A.alloc(F32, [128, 512]) for _ in range(2)]
            tmp = mk_tmp()
            last = (l == NL - 1)
            if last:
                otok = [A.alloc(F32, [128, D]) for _ in range(2)]
            if l == 0:
                xsl = [S.new_dsem() for _ in range(NP)]; ysl = [S.new_dsem() for _ in range(NP)]
            g1 = modc[:, l, 16:24, pi]; g2 = modc[:, l, 40:48, pi]
            for tp in range(0, NT, NP):
                tl = list(range(tp, min(tp + NP, NT)))
                for i, t in enumerate(tl):
                    S.dma("sp", xTl[i].ap, xTs[:, :, t * 512:(t + 1) * 512].rearrange("c p n -> p c n"), reads=[xTs_d[t]], writes=xTl[i].d, sem=xsl[i])
                    S.dma("sp", yTl[i].ap, yTs[:, :, t * 512:(t + 1) * 512].rearrange("c p n -> p c n"), reads=[yTs_d[t]], writes=yTl[i].d, sem=ysl[i])
                for g in range(2):
                    w, wd = load_w(I["w_out"][l], 0, D, g * 512, 512)
                    for i, t in enumerate(tl):
                        xT = xTl[i]; yT = yTl[i]
                        for j in range(4):
                            oc = g * 4 + j
                            p = nps()
                            for c in range(8):
                                mm(p.ap, w[:, c, j * 128:(j + 1) * 128], yT[:, c, :], c == 0, c == 7, wd + yT.d, p.d, last=(c == 7))
                            stt(xT[:, oc, :], p.ap, g1[:, oc:oc + 1], xT[:, oc, :], ALU.mult, ALU.add, p.d + xT.d + modc.d, xT.d)
                for i, t in enumerate(tl):
                    norm_mod(xTl[i].ap, xTl[i].d, acol[:, l, 1, pi, :], modc[:, l, 24:32, pi], u2l[i].ap, u2l[i].d, tmp)
                ngrp = [(g * 512, min(512, DFF - g * 512)) for g in range(6)]
                for gi, (h0, hn) in enumerate(ngrp):
                    nb = hn // 128
                    w1, w1d = load_w(I["w1"][l], 0, D, h0, hn)
                    w3, w3d = load_w(I["w3"][l], 0, D, h0, hn)
                    w2, w2d = load_w(I["w2"][l], h0, hn, 0, D)
                    for i, t in enumerate(tl):
                        xT = xTl[i]; u2 = u2l[i]
                        hb = hTl[i][gi % 2]
                        for j in range(nb):
                            p1 = nps(); p3 = nps()
                            for c in range(8):
                                mm(p1.ap, w1[:, c, j * 128:(j + 1) * 128], u2[:, c, :], c == 0, c == 7, w1d + u2.d, p1.d, last=(c == 7))
                            for c in range(8):
                                mm(p3.ap, w3[:, c, j * 128:(j + 1) * 128], u2[:, c, :], c == 0, c == 7, w3d + u2.d, p3.d, last=(c == 7))
                            hh = hs[j % 2]
                            act(hh.ap, p1.ap, AF.Silu, p1.d, hh.d)
                            tt("dve", hb[:, j, :], hh.ap, p3.ap, ALU.mult, hh.d + p3.d, hb.d)
                        for oc in range(8):
                            p = nps()
                            for j in range(nb):
                                mm(p.ap, w2[:, j, oc * 128:(oc + 1) * 128], hb[:, j, :], j == 0, j == nb - 1, w2d + hb.d, p.d, last=(j == nb - 1))
                            stt(xT[:, oc, :], p.ap, g2[:, oc:oc + 1], xT[:, oc, :], ALU.mult, ALU.add, p.d + xT.d + modc.d, xT.d)
                for i, t in enumerate(tl):
                    xT = xTl[i]
                    if not last:
                        S.dma("sp", xTs[:, :, t * 512:(t + 1) * 512].rearrange("c p n -> p c n"), xT.ap, reads=xT.d, writes=[xTs_d[t]], sem=xsl[i])
                        norm_mod(xT.ap, xT.d, acol[:, l + 1, 0, pi, :], modc[:, l + 1, 0:8, pi], uT[:, :, t * 512:(t + 1) * 512], [uT.d[t]], tmp)
                    else:
                        yf = xT
                        norm_mod(xT.ap, xT.d, C(SP_ROWS["fng"], 8), None, yf.ap, yf.d, tmp)
                        for j in range(4):
                            blk = t * 4 + j
                            ok = otok[blk % 2]
                            for h in range(2):
                                p = nps()
                                for c4 in range(4):
                                    c = h * 4 + c4
                                    S.op("pe", lambda e, p=p, c4=c4, c=c, j=j, yf=yf: e.transpose(p[:, c4 * 128:(c4 + 1) * 128], yf[:, c, j * 128:(j + 1) * 128], ident),
                                         reads=yf.d + cm.d, writes=p.d, inc=(c4 == 3))
                                cp("act", ok[:, h * 512:(h + 1) * 512], p.ap, p.d, ok.d)
                            dd = Dep()
                            S.dma("sp", y_out[blk * 128:(blk + 1) * 128, :], ok.ap, reads=ok.d, writes=[dd], sem=osem[blk % 2])
                            out_deps.append(dd)
            A.release(m3)
            S.barrier()
            if stop == 3:
                return

    try:
        if stop >= 1:
            run_part(0, 4096, 1, I["xs"], O["ys"])
        if stop >= 5:
            run_part(1, 256, 4, I["xp"], O["yp"])
    except _Stop:
        S.barrier()
    w = S._waits("sp", out_deps, ())
    S.prog["sp"].append((w, None, None, 0))
    S.emit()
    return nc, st, S


FLAGS = ("gla", "rg", "hy", "hg")
STOP = 99
GSTOP = 0


class _Stop(Exception):
    pass

NCORES = 8
_CACHE = {}


def _consts():
    if "c" in _CACHE:
        return _CACHE["c"]
    p = np.arange(128)
    s_, t_ = p[:, None], p[None, :]
    cm = np.zeros((128, 6, 128), np.float32)
    cm[:, 0] = (s_ == t_); cm[:, 1] = (s_ <= t_); cm[:, 2] = (s_ >= t_); cm[:, 3] = (s_ > t_); cm[:, 4] = (s_ < t_)
    cm[:, 5] = ((s_ // 64) == (t_ // 64)) / 64.0
    om, ph, rc = _grid_consts(D)
    gc4, gs4, e4, _ = _dft_tables(4096, 6144)
    gc2, gs2, e2, _ = _dft_tables(256, 512)
    c = dict(cmat=cm, gc4096=gc4, gs4096=gs4, e4096=e4, gc256=gc2, gs256=gs2, e256=e2,
             gridc=np.stack([om, ph]).astype(np.float32), gridrc=rc)
    _CACHE["c"] = c
    return c


def kernel(**inp):
    inp = {k: np.asarray(v) for k, v in inp.items()}
    key = ("prog", FLAGS)
    if key not in _CACHE:
        _CACHE[key] = build_program(flags=FLAGS, stop=STOP)
    nc = _CACHE[key][0]
    cst = _consts()
    f32 = lambda a: np.ascontiguousarray(a, dtype=np.float32)
    shared = dict(
        w_mod=f32(inp["w_mod"]), w_in=f32(inp["w_in"]), w_out=f32(inp["w_out"]),
        w1=f32(inp["ffn_w1"]), w3=f32(inp["ffn_w3"]), w2=f32(inp["ffn_w2"]),
        gla_wg=f32(inp["gla_w_gate"].transpose(0, 2, 1, 3).reshape(NL, 16, 512)),
        gla_bg=f32(inp["gla_b_gate"].reshape(NL, 512)),
        rg_wa=f32(inp["rg_w_a"]), rg_wx=f32(inp["rg_w_x"]),
        hy_w1=f32(inp["hy_w1"]), hy_w2=f32(inp["hy_w2"]), hy_w3=f32(inp["hy_w3"]),
        hy_dec=f32(inp["hy_decay"]), hy_skip=f32(inp["hy_skip"]), hg_low=f32(inp["hg_lower"]),
        **cst)
    in_maps = []
    for k in range(8):
        b = k % 2
        m = dict(shared)
        m["xs"] = f32(inp["x_sample"][b])
        m["xp"] = f32(inp["x_prompt"][4 * k:4 * k + 4].reshape(1024, D))
        m["st_gla"] = f32(inp["state_gla"][b].reshape(NL, 2, 256, 64))
        m["st_hg"] = f32(inp["state_hgrn"][b].reshape(NL, 2, 256, 64))
        m["st_rg"] = f32(inp["state_rglru"][b])
        m["sp"] = _build_sp(inp, b)
        in_maps.append(m)
    if NCORES < 8:
        res = run_bass_kernel_spmd(nc, in_maps[:NCORES], core_ids=list(range(NCORES)))
        R = [res.results[k % NCORES] for k in range(8)]
    else:
        res = run_bass_kernel_spmd(nc, in_maps, core_ids=list(range(8)))
        R = res.results
    y_prompt = np.concatenate([R[k]["yp"].reshape(4, 256, D) for k in range(8)], axis=0)
    y_sample = np.stack([R[0]["ys"], R[1]["ys"]], axis=0)
    ns_gla = np.concatenate([R[k]["ns_gla"].reshape(4, NL, 2, 4, 64, 64) for k in range(8)], axis=0)
    ns_rg = np.concatenate([R[k]["ns_rg"].reshape(4, NL, 2, 256) for k in range(8)], axis=0)
    ns_hg = np.concatenate([R[k]["ns_hg"].reshape(4, NL, 2, 4, 64, 64) for k in range(8)], axis=0)
    return (y_prompt.astype(np.float32), y_sample.astype(np.float32), ns_gla.astype(np.float32),
            ns_rg.astype(np.float32), ns_hg.astype(np.float32))
```

```python
import numpy as np
from contextlib import ExitStack
import ml_dtypes
import concourse.bass as bass
import concourse.mybir as mybir
from concourse.bass_utils import run_bass_kernel_spmd

F32 = mybir.dt.float32
BF16 = mybir.dt.bfloat16
I32 = mybir.dt.int32
U8 = mybir.dt.uint8
AF = mybir.ActivationFunctionType
ALU = mybir.AluOpType
AX = mybir.AxisListType

D = 1024
DIN = 3600
DFF = 2816
NL = 2
EPS = 1e-6
PI = float(np.pi)


class Dep:
    __slots__ = ("w", "r", "x", "rg")

    def __init__(self, x=False):
        self.w = None
        self.r = {}
        self.x = x
        self.rg = None


class Sched:
    ENG = ("pe", "act", "dve", "pool", "sp")

    def __init__(self, nc, stack, n_dma_sems=96):
        self.nc = nc
        self.semh = {}
        self.cnt = {}
        for e in self.ENG:
            self.semh[e] = stack.enter_context(nc.semaphore("s_" + e))
            self.cnt[e] = 0
        for i in range(n_dma_sems):
            k = "d%d" % i
            self.semh[k] = stack.enter_context(nc.semaphore("s_" + k))
            self.cnt[k] = 0
        self.n_dma_sems = n_dma_sems
        self.next_dsem = 0
        self.prog = {e: [] for e in self.ENG}
        self.known = {e: {} for e in self.ENG}
        self.ninstr = 0
        self.pe_self = False
        self.pe_mode = None

    def new_dsem(self):
        k = "d%d" % self.next_dsem
        self.next_dsem += 1
        assert self.next_dsem <= self.n_dma_sems, "out of dma sems"
        return k

    def _waits(self, eng, reads, writes, pe_rg=2):
        waits = {}
        if eng.startswith("dmaq:"):
            known = self.known[eng[5:]]
            eng = "dma"
        else:
            known = self.known[eng]

        def need(k, v):
            if known.get(k, 0) >= v:
                return
            if waits.get(k, 0) < v:
                waits[k] = v
        for t in reads:
            if t.w is not None and not (eng == "pe" and t.w[0] == "pe" and not self.pe_self):
                need(*t.w)
            if t.x:
                for k, v in t.r.items():
                    if k != eng:
                        need(k, v)
        for t in writes:
            if eng == "pe":
                switch = (t.rg is not None and t.rg != pe_rg)
                t.rg = pe_rg
            else:
                switch = False
            if t.w is not None and not (eng == "pe" and t.w[0] == "pe" and not (self.pe_self or switch)):
                need(*t.w)
            for k, v in t.r.items():
                if not (eng == "pe" and k == "pe" and not self.pe_self):
                    need(k, v)
        for k, v in waits.items():
            known[k] = v
        return list(waits.items())

    def _mark(self, pt, reads, writes):
        k, v = pt
        for t in writes:
            t.w = pt
            t.r = {}
        for t in reads:
            if t.r.get(k, 0) < v:
                t.r[k] = v

    def op(self, eng, fn, reads=(), writes=(), inc=True, pe_rg=2, pe_mode=(128, 128)):
        if eng != "pe":
            inc = True
        waits = self._waits(eng, reads, writes, pe_rg)
        if eng == "pe":
            if self.pe_mode is not None and self.pe_mode != pe_mode and self.cnt["pe"] > 0:
                v = self.cnt["pe"]
                if self.known["pe"].get("pe", 0) < v:
                    waits = [w for w in waits if w[0] != "pe"] + [("pe", v)]
                    self.known["pe"]["pe"] = v
            self.pe_mode = pe_mode
        if inc:
            self.cnt[eng] += 1
            val = self.cnt[eng]
        else:
            val = self.cnt[eng] + 1
        ws = set(id(t) for t in writes)
        self._mark((eng, val), [t for t in reads if id(t) not in ws], writes)
        self.prog[eng].append((waits, fn, eng if inc else None, 1))
        self.ninstr += 1

    def dma(self, queue, out, in_, reads=(), writes=(), sem=None, **kw):
        waits = self._waits("dmaq:" + queue, reads, writes)
        self.cnt[sem] += 16
        ws = set(id(t) for t in writes)
        self._mark((sem, self.cnt[sem]), [t for t in reads if id(t) not in ws], writes)
        self.prog[queue].append((waits, (lambda e, o=out, i=in_, kw=kw: e.dma_start(out=o, in_=i, **kw)), sem, 16))
        self.ninstr += 1

    def barrier(self):
        for e in self.ENG:
            waits = []
            for k, v in self.cnt.items():
                if v > 0 and self.known[e].get(k, 0) < v:
                    waits.append((k, v))
                    self.known[e][k] = v
            if waits:
                self.prog[e].append((waits, None, None, 0))

    def emit(self):
        nc = self.nc
        semh = self.semh

        def run(e, name):
            for waits, fn, incsem, incv in self.prog[name]:
                for k, v in waits:
                    e.wait_ge(semh[k], v)
                if fn is None:
                    continue
                ins = fn(e)
                if incsem is not None:
                    ins.then_inc(semh[incsem], incv)

        with nc.Block() as block:
            @block.tensor
            def _(e):
                run(e, "pe")

            @block.scalar
            def _(e):
                run(e, "act")

            @block.vector
            def _(e):
                run(e, "dve")

            @block.gpsimd
            def _(e):
                run(e, "pool")

            @block.sync
            def _(e):
                run(e, "sp")


class T:
    def __init__(self, ap, ndeps=1):
        self.ap = ap
        self.d = [Dep() for _ in range(ndeps)]

    def __getitem__(self, idx):
        return self.ap[idx]


def _dft_tables(L, N):
    nb = L // 128
    a = np.arange(L, dtype=np.float64) + 0.5
    ang = 2.0 * np.pi * np.outer(a, a) / N
    out = []
    for fn in (np.cos, np.sin):
        G = fn(ang)
        Gt = G.reshape(nb, 128, nb, 128).transpose(2, 1, 0, 3)
        out.append(np.ascontiguousarray(Gt).astype(ml_dtypes.bfloat16))
    nfb = (N // 2) // 128
    f = np.arange(N // 2, dtype=np.float64) + 0.5
    phi = 2.0 * np.pi * f / N * (L / 2 + 0.5)
    ec = (2.0 / N) * np.cos(phi)
    es = (2.0 / N) * np.sin(phi)
    E = np.stack([ec.reshape(nfb, 128).T, es.reshape(nfb, 128).T], axis=1)
    return out[0], out[1], np.ascontiguousarray(E).astype(np.float32), nfb


def _pe_consts(L):
    bands = np.linspace(1e-4, 15, 16, dtype=np.float32)
    w = (np.float32(2.0 * np.pi) / np.float32(L)) * bands
    om = np.zeros(33, np.float32); ph = np.zeros(33, np.float32)
    om[1:17] = w;  ph[1:17] = np.float32(np.pi / 2)
    om[17:33] = w; ph[17:33] = np.float32(np.pi)
    om[0] = np.float32(1.0) / np.float32(L - 1)
    return np.stack([om, ph], axis=1).astype(np.float32)


def _grid_consts(dim):
    quarter = dim // 4
    omega = (1.0 / (np.float32(10000.0) ** (np.arange(quarter, dtype=np.float32) / np.float32(quarter)))).astype(np.float32)
    om = np.concatenate([omega, omega, omega, omega])
    ph = np.concatenate([np.zeros(quarter), np.full(quarter, np.pi / 2)] * 2).astype(np.float32)
    p = np.arange(128)
    rc = np.stack([(p >= 64).astype(np.float32), (p % 64).astype(np.float32)], axis=1)
    return om.astype(np.float32), ph, rc.astype(np.float32)


def _sp_layout():
    rows = {}
    n = 0

    def add(name, cnt):
        nonlocal n
        rows[name] = n
        n += cnt
    for l in range(NL):
        add("n1g%d" % l, 8); add("n2g%d" % l, 8); add("bmod%d" % l, 48)
        add("glang%d" % l, 2); add("rgcw%d" % l, 8); add("rgcb%d" % l, 2)
        add("rgba%d" % l, 4); add("rgbx%d" % l, 4); add("rglam%d" % l, 4)
        add("hycw%d" % l, 18); add("hycb%d" % l, 6); add("hyb1%d" % l, 1); add("hyb2%d" % l, 1)
        add("hglow%d" % l, 2); add("hgng%d" % l, 2)
    add("fng", 8); add("c", 8); add("cctx", 8); add("pec4096", 2); add("pec256", 2)
    return rows, ((n + 127) // 128) * 128


SP_ROWS, SP_N = _sp_layout()


def _build_sp(inp, b):
    sp = np.zeros((SP_N, 128), np.float32)

    def put(name, arr):
        a = np.asarray(arr, np.float32).reshape(-1, 128)
        sp[SP_ROWS[name]:SP_ROWS[name] + a.shape[0]] = a
    for l in range(NL):
        put("n1g%d" % l, inp["norm1_g"][l]); put("n2g%d" % l, inp["norm2_g"][l]); put("bmod%d" % l, inp["b_mod"][l])
        put("glang%d" % l, inp["gla_norm_g"][l]); put("rgcw%d" % l, inp["rg_conv_w"][l]); put("rgcb%d" % l, inp["rg_conv_b"][l])
        put("rgba%d" % l, inp["rg_b_a"][l]); put("rgbx%d" % l, inp["rg_b_x"][l]); put("rglam%d" % l, inp["rg_lambda"][l])
        put("hycw%d" % l, inp["hy_conv_w"][l]); put("hycb%d" % l, inp["hy_conv_b"][l])
        r = np.zeros(128, np.float32); r[:64] = inp["hy_b1"][l]; put("hyb1%d" % l, r)
        r = np.zeros(128, np.float32); r[:64] = inp["hy_b2"][l]; put("hyb2%d" % l, r)
        put("hglow%d" % l, inp["hg_lower"][l]); put("hgng%d" % l, inp["hg_norm_g"][l])
    put("fng", inp["final_norm_g"]); put("c", inp["c"][b]); put("cctx", inp["c_ctx"])
    for L in (4096, 256):
        pc = _pe_consts(L)
        r = np.zeros((2, 128), np.float32); r[0, :33] = pc[:, 0]; r[1, :33] = pc[:, 1]
        put("pec%d" % L, r)
    return sp


IN_OFF = dict(a_q=0, a_k=256, a_v=512, a_g=768, a_lr=1024, b_x=1040, b_g=1296, c_v=1552, c_x1=1808, c_x2=2064,
              d_q=2320, d_ff=2576, d_fb=2832, d_i=3088, d_g=3344)

ARENA_BYTES = 206 * 1024


class Arena:
    def __init__(self, nc, stack):
        self.t = stack.enter_context(nc.sbuf_tensor("arena", [128, ARENA_BYTES // 4], F32))
        self.off = 0

    def alloc(self, dtype, shape, ndeps=1):
        esz = 4 if dtype in (F32, I32) else (1 if dtype == U8 else 2)
        n = int(np.prod(shape[1:]))
        nb = ((n * esz + 31) // 32) * 32
        assert self.off + nb <= ARENA_BYTES, "arena overflow %d" % (self.off + nb)
        ap = self.t[:, self.off // 4:(self.off + nb) // 4]
        if dtype != F32:
            ap = ap.bitcast(dtype)
        ap = ap[0:shape[0], 0:n]
        if len(shape) > 2:
            names = "abcdefg"[:len(shape) - 1]
            kw = {names[i]: shape[i + 1] for i in range(len(shape) - 2)}
            ap = ap.rearrange("p (%s) -> p %s" % (" ".join(names), " ".join(names)), **kw)
        self.off += nb
        return T(ap, ndeps)

    def mark(self):
        return self.off

    def release(self, m):
        self.off = m


def build_program(flags=("gla", "rg", "hy", "hg"), dbg=False, stop=99):
    nc = bass.Bass("TRN2", target_bir_lowering=False)
    st = ExitStack()
    S = Sched(nc, st)
    A = Arena(nc, st)

    def din(name, shape, dt=F32):
        return nc.dram_tensor(name, list(shape), dt, kind="ExternalInput").ap()

    def dout(name, shape, dt=F32):
        return nc.dram_tensor(name, list(shape), dt, kind="ExternalOutput").ap()

    def dscr(name, shape, dt=F32):
        return nc.dram_tensor(name, list(shape), dt, kind="Internal").ap()

    I = {}
    I["xs"] = din("xs", [4096, D]); I["xp"] = din("xp", [1024, D])
    I["st_gla"] = din("st_gla", [NL, 2, 256, 64]); I["st_hg"] = din("st_hg", [NL, 2, 256, 64]); I["st_rg"] = din("st_rg", [NL, 2, 256])
    I["w_mod"] = din("w_mod", [NL, D, 6 * D]); I["w_in"] = din("w_in", [NL, D, DIN]); I["w_out"] = din("w_out", [NL, D, D])
    I["w1"] = din("w1", [NL, D, DFF]); I["w3"] = din("w3", [NL, D, DFF]); I["w2"] = din("w2", [NL, DFF, D])
    I["sp"] = din("sp", [SP_N, 128])
    I["gla_wg"] = din("gla_wg", [NL, 16, 512]); I["gla_bg"] = din("gla_bg", [NL, 512])
    I["rg_wa"] = din("rg_wa", [NL, 2, 4, 64, 64]); I["rg_wx"] = din("rg_wx", [NL, 2, 4, 64, 64])
    I["hy_w1"] = din("hy_w1", [NL, 33, 64]); I["hy_w2"] = din("hy_w2", [NL, 64, 64]); I["hy_w3"] = din("hy_w3", [NL, 64, 512])
    I["hy_dec"] = din("hy_dec", [NL, 512]); I["hy_skip"] = din("hy_skip", [NL, 512]); I["hg_low"] = din("hg_low", [NL, 256])
    I["cmat"] = din("cmat", [128, 6, 128])
    I["gc4096"] = din("gc4096", [32, 128, 32, 128], BF16); I["gs4096"] = din("gs4096", [32, 128, 32, 128], BF16)
    I["gc256"] = din("gc256", [2, 128, 2, 128], BF16); I["gs256"] = din("gs256", [2, 128, 2, 128], BF16)
    I["e4096"] = din("e4096", [128, 2, 24]); I["e256"] = din("e256", [128, 2, 2])
    I["gridc"] = din("gridc", [2, D]); I["gridrc"] = din("gridrc", [128, 2])
    O = {}
    O["ys"] = dout("ys", [4096, D]); O["yp"] = dout("yp", [1024, D])
    O["ns_gla"] = dout("ns_gla", [4, NL, 2, 256, 64]); O["ns_hg"] = dout("ns_hg", [4, NL, 2, 256, 64]); O["ns_rg"] = dout("ns_rg", [4, NL, 2, 256])
    out_deps = []
    DBG = {}

    cm = A.alloc(F32, [128, 6, 128])
    ident = cm[:, 0, :]; tril = cm[:, 1, :]; triu = cm[:, 2, :]; su = cm[:, 3, :]; sl = cm[:, 4, :]
    bd64 = A.alloc(BF16, [128, 128]); onesm = A.alloc(BF16, [128, 128])
    colT = A.alloc(F32, [128, SP_N])
    modc = A.alloc(F32, [128, NL, 48, 2])
    acol = A.alloc(F32, [128, NL, 2, 2, 8])
    nsp = A.alloc(F32, [128, NL, 2, 2, 2])
    lbc = A.alloc(F32, [128, NL, 2, 2])
    WSL = 4
    wring = [A.alloc(BF16, [128, 8, 512]) for _ in range(WSL)]
    wsem = [S.new_dsem() for _ in range(WSL)]
    wnext = [0]
    uT = A.alloc(BF16, [128, 8, 4096], ndeps=8)
    PS = [T(st.enter_context(nc.psum_tensor("ps%d" % i, [128, 512], F32))[:, :]) for i in range(8)]
    for p_ in PS:
        p_.d = [Dep(x=True)]
    psn = [0]

    def nps():
        p = PS[psn[0] % 7]
        psn[0] += 1
        return p

    gsem = [S.new_dsem() for _ in range(8)]
    SPL = [S.new_dsem() for _ in range(24)]

    def C(row, n=1):
        return colT[:, row:row + n]

    def mm(out, lhsT, rhs, start, stop, reads, writes, last=True, rg=2, mode=(128, 128)):
        S.op("pe", lambda e: e.matmul(out, lhsT=lhsT, rhs=rhs, start=start, stop=stop), reads=reads, writes=writes, inc=last, pe_rg=rg, pe_mode=mode)

    def act(out, in_, func, reads, writes, **kw):
        S.op("act", lambda e: e.activation(out=out, in_=in_, func=func, **kw), reads=reads, writes=writes)

    def tt(eng, out, in0, in1, op, reads, writes):
        S.op(eng, lambda e: e.tensor_tensor(out=out, in0=in0, in1=in1, op=op), reads=reads, writes=writes)

    def ts(eng, out, in0, s1, s2, op0, op1, reads, writes):
        if op1 is None:
            S.op(eng, lambda e: e.tensor_scalar(out=out, in0=in0, scalar1=s1, scalar2=None, op0=op0), reads=reads, writes=writes)
        else:
            S.op(eng, lambda e: e.tensor_scalar(out=out, in0=in0, scalar1=s1, scalar2=s2, op0=op0, op1=op1), reads=reads, writes=writes)

    def stt(out, in0, scalar, in1, op0, op1, reads, writes):
        S.op("dve", lambda e: e.scalar_tensor_tensor(out=out, in0=in0, scalar=scalar, in1=in1, op0=op0, op1=op1), reads=reads, writes=writes)

    def cp(eng, out, in_, reads, writes):
        if eng == "act":
            S.op("act", lambda e: e.copy(out=out, in_=in_), reads=reads, writes=writes)
        else:
            S.op(eng, lambda e: e.tensor_copy(out=out, in_=in_), reads=reads, writes=writes)

    def load_w(w2d, r0, nr, c0, ncol):
        i = wnext[0] % WSL
        wnext[0] += 1
        slot = wring[i]
        kc = nr // 128
        src = w2d[r0:r0 + nr, c0:c0 + ncol].rearrange("(c p) n -> p c n", p=128)
        flat = slot.ap.rearrange("p a b -> p (a b)")[:, 0:kc * ncol].rearrange("p (c n) -> p c n", c=kc)
        S.dma("pool", flat, src, writes=slot.d, sem=wsem[i])
        return T(flat, 0), slot.d

    def sin_rr(out, arg, tmp_i, tmp_f, reads, writes_t):
        ts("dve", tmp_i.ap, arg.ap, 1.0 / (2 * PI), None, ALU.mult, None, arg.d + reads, tmp_i.d)
        cp("dve", tmp_f.ap, tmp_i.ap, tmp_i.d, tmp_f.d)
        stt(arg.ap, tmp_f.ap, -2 * PI, arg.ap, ALU.mult, ALU.add, tmp_f.d + arg.d, arg.d)
        ts("dve", arg.ap, arg.ap, -PI, PI, ALU.max, ALU.min, arg.d, arg.d)
        act(out, arg.ap, AF.Sin, arg.d, writes_t)

    S.dma("sp", cm.ap, I["cmat"], writes=cm.d, sem=gsem[0])
    cp("dve", bd64.ap, cm[:, 5, :], cm.d, bd64.d)
    S.op("pool", lambda e: e.memset(onesm.ap, 1.0 / D), writes=onesm.d)
    m0 = A.mark()
    if stop == -3:
        S.emit(); return nc, st, S
    sprow = A.alloc(F32, [128, 128])
    for k in range(SP_N // 128):
        S.dma("sp", sprow.ap, I["sp"][k * 128:(k + 1) * 128, :], writes=sprow.d, sem=gsem[1])
        p = nps()
        S.op("pe", lambda e, p=p: e.transpose(p[:, 0:128], sprow.ap, ident), reads=sprow.d + cm.d, writes=p.d)
        cp("dve", colT[:, k * 128:(k + 1) * 128], p[:, 0:128], p.d, colT.d)
    if stop == -2:
        S.emit(); return nc, st, S
    scT = A.alloc(BF16, [128, 8, 128])
    S.op("pool", lambda e: e.memset(scT.ap.rearrange("p a b -> p (a b)"), 0.0), writes=scT.d)
    act(scT[:, :, 0], C(SP_ROWS["c"], 8), AF.Silu, colT.d, scT.d)
    act(scT[:, :, 1], C(SP_ROWS["cctx"], 8), AF.Silu, colT.d, scT.d)
    tmpc = A.alloc(F32, [128, 16])
    for l in range(NL):
        lam = C(SP_ROWS["rglam%d" % l], 4)
        act(tmpc[:, 0:4], lam, AF.Exp, colT.d, tmpc.d, scale=-1.0)
        act(tmpc[:, 4:8], tmpc[:, 0:4], AF.Ln, tmpc.d, tmpc.d, bias=1.0)
        nv = nsp[:, l, :, :, :].rearrange("p d c k -> p (d c) k")
        ts("dve", nv[:, :, 0], tmpc[:, 4:8], -8.0, None, ALU.mult, None, tmpc.d, nsp.d)
        ts("dve", nv[:, :, 1], tmpc[:, 4:8], -16.0, None, ALU.mult, None, tmpc.d, nsp.d)
    S.op("pool", lambda e: e.memset(lbc[:, 0, :, 0], 0.0), writes=lbc.d)
    S.op("pool", lambda e: e.memset(lbc[:, 0, :, 1], 1.0), writes=lbc.d)
    tt("dve", tmpc[:, 8:10], C(SP_ROWS["hglow1"], 2), C(SP_ROWS["hglow0"], 2), ALU.subtract, colT.d, tmpc.d)
    act(lbc[:, 1, :, 0], tmpc[:, 8:10], AF.Sigmoid, tmpc.d, lbc.d)
    act(lbc[:, 1, :, 1], tmpc[:, 8:10], AF.Sigmoid, tmpc.d, lbc.d, scale=-1.0)
    if stop == -1:
        S.emit(); return nc, st, S
    modrow = A.alloc(F32, [128, 6 * D])
    for l in range(NL):
        for g in range(12):
            w, wd = load_w(I["w_mod"][l], 0, D, g * 512, 512)
            pm = nps()
            for c in range(8):
                mm(pm.ap, scT[:, c, :], w[:, c, :], c == 0, c == 7, wd + scT.d, pm.d, last=(c == 7))
            cp("dve", modrow[:, g * 512:(g + 1) * 512], pm.ap, pm.d, modrow.d)
        if stop == -0.6:
            S.barrier(); S.emit(); return nc, st, S
        for g in range(12):
            pt = nps()
            for j in range(4):
                blk = g * 4 + j
                S.op("pe", lambda e, pt=pt, j=j, blk=blk: e.transpose(pt[:, j * 128:(j + 1) * 128], modrow[:, blk * 128:(blk + 1) * 128], ident),
                     reads=modrow.d + cm.d, writes=pt.d, inc=(j == 3))
            cp("dve", modc[:, l, g * 4:g * 4 + 4, :], pt.ap.rearrange("p (j n) -> p j n", j=4)[:, :, 0:2], pt.d, modc.d)
        if stop == -0.4:
            S.barrier(); S.emit(); return nc, st, S
        bm = C(SP_ROWS["bmod%d" % l], 48)
        tt("dve", modc[:, l, :, :], modc[:, l, :, :], bm.unsqueeze(2).to_broadcast([128, 48, 2]), ALU.add, modc.d + colT.d, modc.d)
        if stop == -0.2:
            S.barrier(); S.emit(); return nc, st, S
        for wn, (grow, scoff) in enumerate(((SP_ROWS["n1g%d" % l], 8), (SP_ROWS["n2g%d" % l], 32))):
            for part in range(2):
                stt(acol[:, l, wn, part, :], modc[:, l, scoff:scoff + 8, part], 1.0, C(grow, 8), ALU.add, ALU.mult,
                    modc.d + colT.d, acol.d)
    A.release(m0)
    S.barrier()


    def seg_of(L, t):
        if L >= 512:
            return [((t * 512) // L, (t * 512) % L, 512, 0)]
        k = 512 // L
        return [(t * k + j, 0, L, j * L) for j in range(k)]

    def proj_fm(w, wd, col0, ncol, t):
        p = nps()
        for c in range(8):
            mm(p[0:ncol, :], w[:, c, col0:col0 + ncol], uT[:, c, t * 512:(t + 1) * 512], c == 0, c == 7, wd + [uT.d[t]], p.d, last=(c == 7))
        return p

    def mixer_rg(pi, l, L, NS, yTs, yTs_d, ysem):
        m = A.mark()
        Tn = L * NS; NT = Tn // 512
        w, wd = load_w(I["w_in"][l], 0, D, IN_OFF["b_x"], 512)
        bd = A.alloc(BF16, [128, 8, 128])
        S.op("pool", lambda e: e.memset(bd.ap.rearrange("p a b -> p (a b)"), 0.0), writes=bd.d)
        bsem = SPL[0]
        for gate, key in enumerate(("rg_wa", "rg_wx")):
            for d in range(2):
                for ct in range(2):
                    for blk in range(2):
                        S.dma("pool", bd[blk * 64:(blk + 1) * 64, (gate * 2 + d) * 2 + ct, blk * 64:(blk + 1) * 64],
                              I[key][l, d, ct * 2 + blk], writes=bd.d, sem=bsem)
        xpad = A.alloc(F32, [128, NS, L + 3]); xc = A.alloc(F32, [128, NS, L]); xcb = A.alloc(BF16, [128, NS, L])
        hf = A.alloc(F32, [128, NS, L])
        h0 = A.alloc(F32, [128, 2])
        NR = 2
        tmps = [dict((k, A.alloc(F32, [128, 512])) for k in ("r", "i", "a", "u", "hb", "g")) for _ in range(NR)]
        yb = [A.alloc(BF16, [128, 512]) for _ in range(2)]
        hsem = SPL[1]; ssem = SPL[2]; ysems = [SPL[3], SPL[4]]
        rcw = SP_ROWS["rgcw%d" % l]; rcb = SP_ROWS["rgcb%d" % l]; rba = SP_ROWS["rgba%d" % l]; rbx = SP_ROWS["rgbx%d" % l]
        for ct in range(2):
            S.op("pool", lambda e: e.memset(xpad.ap.rearrange("p a b -> p (a b)"), 0.0), writes=xpad.d)
            if pi == 0:
                for d in range(2):
                    S.dma("sp", h0[:, d:d + 1], I["st_rg"][l, d, ct * 128:(ct + 1) * 128].rearrange("(p o) -> p o", o=1), writes=h0.d, sem=hsem)
            else:
                S.op("pool", lambda e: e.memset(h0.ap, 0.0), writes=h0.d)
            for t in range(NT):
                p = proj_fm(w, wd, ct * 128, 128, t)
                for (s_, off, n, a0) in seg_of(L, t):
                    cp("act", xpad[:, s_, 2 + off:2 + off + n], p[:, a0:a0 + n], p.d, xpad.d)
            for s_ in range(NS):
                ts("dve", xc[:, s_, :], xpad[:, s_, 0:L], C(rcw + 0 * 2 + ct), C(rcb + ct), ALU.mult, ALU.add, xpad.d + colT.d, xc.d)
                for j in range(1, 4):
                    stt(xc[:, s_, :], xpad[:, s_, j:j + L], C(rcw + j * 2 + ct), xc[:, s_, :], ALU.mult, ALU.add, xpad.d + colT.d + xc.d, xc.d)
                cp("act", xcb[:, s_, :], xc[:, s_, :], xc.d, xcb.d)

            def gates(d, s_, off, n, tm):
                sl_ = slice(off, off + n)
                pr = nps(); pq = nps()
                mm(pr[:, 0:n], bd[:, (0 * 2 + d) * 2 + ct, :], xcb[:, s_, sl_], True, True, bd.d + xcb.d, pr.d)
                mm(pq[:, 0:n], bd[:, (1 * 2 + d) * 2 + ct, :], xcb[:, s_, sl_], True, True, bd.d + xcb.d, pq.d)
                act(tm["r"][:, 0:n], pr[:, 0:n], AF.Sigmoid, pr.d + colT.d, tm["r"].d, bias=C(rba + d * 2 + ct))
                act(tm["i"][:, 0:n], pq[:, 0:n], AF.Sigmoid, pq.d + colT.d, tm["i"].d, bias=C(rbx + d * 2 + ct))
                act(tm["a"][:, 0:n], tm["r"][:, 0:n], AF.Exp, tm["r"].d + nsp.d, tm["a"].d, scale=nsp[:, l, d, ct, 0:1])
                act(tm["u"][:, 0:n], tm["r"][:, 0:n], AF.Exp, tm["r"].d + nsp.d, tm["u"].d, scale=nsp[:, l, d, ct, 1:2])
                ts("dve", tm["u"][:, 0:n], tm["u"][:, 0:n], -1.0, 1.0, ALU.mult, ALU.add, tm["u"].d, tm["u"].d)
                act(tm["u"][:, 0:n], tm["u"][:, 0:n], AF.Sqrt, tm["u"].d, tm["u"].d)
                tt("pool", tm["i"][:, 0:n], tm["i"][:, 0:n], xc[:, s_, sl_], ALU.mult, tm["i"].d + xc.d, tm["i"].d)
                tt("dve", tm["u"][:, 0:n], tm["u"][:, 0:n], tm["i"][:, 0:n], ALU.mult, tm["u"].d + tm["i"].d, tm["u"].d)

            k = 0
            for t in range(NT):
                for (s_, off, n, a0) in seg_of(L, t):
                    tm = tmps[k % NR]; k += 1
                    gates(0, s_, off, n, tm)
                    init = h0[:, 0:1] if off == 0 else hf[:, s_, off - 1:off]
                    S.op("dve", lambda e, tm=tm, s_=s_, off=off, n=n, init=init: e.tensor_tensor_scan(
                        out=hf[:, s_, off:off + n], data0=tm["a"][:, 0:n], data1=tm["u"][:, 0:n], initial=init, op0=ALU.mult, op1=ALU.add),
                        reads=tm["a"].d + tm["u"].d + hf.d + h0.d, writes=hf.d)
                    if pi == 1 and off + n == L:
                        dd = Dep()
                        S.dma("sp", O["ns_rg"][s_, l, 0, ct * 128:(ct + 1) * 128].rearrange("(p o) -> p o", o=1), hf[:, s_, L - 1:L],
                              reads=hf.d, writes=[dd], sem=ssem)
                        out_deps.append(dd)
            prev = None
            for t in reversed(range(NT)):
                pg = proj_fm(w, wd, 256 + ct * 128, 128, t)
                for (s_, off, n, a0) in seg_of(L, t):
                    tm = tmps[k % NR]; k += 1
                    gates(1, s_, off, n, tm)
                    init = h0[:, 1:2] if off + n == L else prev["hb"][:, 0:1]
                    rd = tm["a"].d + tm["u"].d + h0.d + (prev["hb"].d if prev is not None else [])
                    S.op("dve", lambda e, tm=tm, n=n, init=init: e.tensor_tensor_scan(
                        out=tm["hb"][:, n - 1::-1] if False else tm["hb"][:, 0:n][:, ::-1], data0=tm["a"][:, 0:n][:, ::-1], data1=tm["u"][:, 0:n][:, ::-1],
                        initial=init, op0=ALU.mult, op1=ALU.add), reads=rd, writes=tm["hb"].d)
                    prev = tm
                    if pi == 1 and off == 0:
                        dd = Dep()
                        S.dma("sp", O["ns_rg"][s_, l, 1, ct * 128:(ct + 1) * 128].rearrange("(p o) -> p o", o=1), tm["hb"][:, 0:1],
                              reads=tm["hb"].d, writes=[dd], sem=ssem)
                        out_deps.append(dd)
                    g = tm["g"]; ps_ = slice(a0, a0 + n)
                    act(g[:, 0:n], pg[:, ps_], AF.Square, pg.d, g.d)
                    ts("dve", g[:, 0:n], g[:, 0:n], 0.044715, 1.0, ALU.mult, ALU.add, g.d, g.d)
                    tt("dve", g[:, 0:n], g[:, 0:n], pg[:, ps_], ALU.mult, g.d + pg.d, g.d)
                    act(g[:, 0:n], g[:, 0:n], AF.Sigmoid, g.d, g.d, scale=1.5957691216057308)
                    tt("dve", g[:, 0:n], g[:, 0:n], pg[:, ps_], ALU.mult, g.d + pg.d, g.d)
                    tt("pool", tm["r"][:, 0:n], tm["hb"][:, 0:n], hf[:, s_, off:off + n], ALU.add, tm["hb"].d + hf.d, tm["r"].d)
                    y_ = yb[t % 2]
                    tt("dve", y_[:, a0:a0 + n], tm["r"][:, 0:n], g[:, 0:n], ALU.mult, tm["r"].d + g.d, y_.d)
                S.dma("sp", yTs[2 + ct, :, t * 512:(t + 1) * 512], yb[t % 2].ap, reads=yb[t % 2].d, writes=[yTs_d[t]], sem=ysems[t % 2])
        A.release(m)
        S.barrier()


    def mixer_gated(pi, l, L, NS, yTs, yTs_d, ysem, kind):
        m = A.mark()
        CH = 128
        Tn = L * NS; NT = Tn // 512; NCH = Tn // CH; CPS = L // CH
        gla = (kind == "gla")
        yc0 = 0 if gla else 6
        st_in = I["st_gla"] if gla else I["st_hg"]
        st_out = O["ns_gla"] if gla else O["ns_hg"]
        grow = SP_ROWS[("glang%d" if gla else "hgng%d") % l]
        wl = I["w_in"][l]
        la_early = A.alloc(F32, [128, 512])
        wnext[0] = 0
        scr = wring[3].ap.rearrange("p a b -> p (a b)").bitcast(F32)
        EB = [T(scr[:, i * 512:(i + 1) * 512].rearrange("p (b c) -> p b c", c=128), 1) for i in range(4)]
        if gla:
            wA, wAd = load_w(wl, 0, D, 0, 512)
            wB, wBd = load_w(wl, 0, D, 256, 512)
            wC, wCd = load_w(wl, 0, D, 768, 272)
            wg = A.alloc(BF16, [128, 512]); bgb = A.alloc(F32, [128, 512])
            S.op("pool", lambda e: e.memset(wg.ap, 0.0), writes=wg.d)
            S.dma("pool", wg[112:128, :], I["gla_wg"][l], writes=wg.d, sem=SPL[0])
            S.dma("sp", bgb.ap, I["gla_bg"][l].partition_broadcast(128), writes=bgb.d, sem=SPL[1])
            lrT = A.alloc(BF16, [128, 512])
        else:
            wA, wAd = load_w(wl, 0, D, 2320, 512)
            wB, wBd = load_w(wl, 0, D, 2832, 512)
            wC, wCd = load_w(wl, 0, D, 3344, 256)
            lb2 = A.alloc(F32, [128, 256]); om2 = A.alloc(F32, [128, 256])
            if l == 0:
                S.op("pool", lambda e: e.memset(lb2.ap, 0.0), writes=lb2.d)
                S.op("pool", lambda e: e.memset(om2.ap, 1.0), writes=om2.d)
            else:
                hl = T(la_early.ap.rearrange("p (a b) -> p a b", a=2), 0); hl.d = la_early.d
                S.dma("sp", hl[:, 0, :], I["hg_low"][0].partition_broadcast(128), writes=hl.d, sem=SPL[0])
                S.dma("sp", hl[:, 1, :], I["hg_low"][1].partition_broadcast(128), writes=hl.d, sem=SPL[1])
                tt("dve", hl[:, 0, :], hl[:, 1, :], hl[:, 0, :], ALU.subtract, hl.d, hl.d)
                act(lb2.ap, hl[:, 0, :], AF.Sigmoid, hl.d, lb2.d)
                act(om2.ap, hl[:, 0, :], AF.Sigmoid, hl.d, om2.d, scale=-1.0)
        oT = A.alloc(F32, [128, 2, Tn]); qbb = A.alloc(BF16, [128, 2, Tn])
        kvb = A.alloc(F32, [128, NCH, 2, 64]); decb = A.alloc(F32, [128, NCH, 2])
        Sf = A.alloc(F32, [128, 2, 64]); Sb = A.alloc(F32, [128, 2, 64]); Ss = A.alloc(BF16, [128, 2, 64])
        qT = A.alloc(BF16, [128, 2, 512])
        kT = [A.alloc(BF16, [128, 2, 512])] if gla else [A.alloc(BF16, [128, 2, 512]) for _ in range(2)]
        ktok = A.alloc(F32, [128, 512]); vbf = A.alloc(BF16, [128, 256]); la = la_early
        xs = A.alloc(F32, [128, 4, CH]); eq = xs; ekn = A.alloc(F32, [128, 4, CH]); ek = A.alloc(F32, [128, 512])
        dec = A.alloc(F32, [128, 4])
        qhf = A.alloc(BF16, [128, 2, CH])
        Q1 = [A.alloc(BF16, [128, 2, CH]) for _ in range(2)]; ktl = [A.alloc(BF16, [128, 2, CH]) for _ in range(2)]
        Q2 = [A.alloc(BF16, [128, 2, CH]) for _ in range(2)]; K2 = [A.alloc(BF16, [128, 2, CH]) for _ in range(2)]
        khat = [A.alloc(BF16, [128, 256]) for _ in range(2)]; Am = [A.alloc(BF16, [128, 4, CH]) for _ in range(2)]
        tri = (tril, triu)
        ssem = SPL[2]; isem = SPL[3]
        mask8 = [A.alloc(U8, [128, 4, CH]) for _ in range(2)]
        for d in range(2):
            cp("dve", mask8[d].ap, tri[d].unsqueeze(1).to_broadcast([128, 4, CH]), cm.d, mask8[d].d)
            S.op("pool", lambda e, d=d: e.memset(Am[d].ap.rearrange("p a b -> p (a b)"), 0.0), writes=Am[d].d)

        def proj_tm(w, wd, col0, ncol, ch):
            p = nps()
            for c in range(8):
                mm(p[:, 0:ncol], uT[:, c, ch * CH:(ch + 1) * CH], w[:, c, col0:col0 + ncol], c == 0, c == 7, wd + [uT.d[ch // 4]], p.d, last=(c == 7))
            return p

        def init_state(Sx, s_, d):
            if pi == 0:
                S.dma("sp", Sx.ap, st_in[l, d].rearrange("(h p) v -> p h v", p=128), writes=Sx.d, sem=(isem if d == 0 else SPL[20]))
            else:
                S.op("pool", lambda e: e.memset(Sx.ap.rearrange("p a b -> p (a b)"), 0.0), writes=Sx.d)

        def out_state(Sx, s_, d):
            if pi == 1:
                dd = Dep()
                S.dma("sp", st_out[s_, l, d].rearrange("(h p) v -> p h v", p=128), Sx.ap, reads=Sx.d, writes=[dd], sem=ssem)
                out_deps.append(dd)

        for t in range(NT):
            if gla:
                for hp in range(2):
                    p = proj_fm(wA, wAd, hp * 128, 128, t)
                    act(qT[:, hp, :], p.ap, AF.Identity, p.d, qT.d, scale=0.125, bias=0.0)
                    p = proj_fm(wA, wAd, 256 + hp * 128, 128, t)
                    cp("dve", kT[0][:, hp, :], p.ap, p.d, kT[0].d)
                p = proj_fm(wC, wCd, 144, 128, t)
                cp("dve", lrT.ap, p.ap, p.d, lrT.d)
            else:
                for hp in range(2):
                    p = proj_fm(wA, wAd, hp * 128, 128, t)
                    act(qT[:, hp, :], p.ap, AF.Silu, p.d, qT.d)
                    p = proj_fm(wA, wAd, 256 + hp * 128, 128, t)
                    act(kT[0][:, hp, :], p.ap, AF.Sigmoid, p.d, kT[0].d, scale=-1.0)
                    ts("dve", kT[0][:, hp, :], kT[0][:, hp, :], lbc[:, l, hp, 1:2], None, ALU.mult, None, kT[0].d + lbc.d, kT[0].d)
                    p = proj_fm(wB, wBd, hp * 128, 128, t)
                    act(kT[1][:, hp, :], p.ap, AF.Sigmoid, p.d, kT[1].d, scale=-1.0)
                    ts("dve", kT[1][:, hp, :], kT[1][:, hp, :], lbc[:, l, hp, 1:2], None, ALU.mult, None, kT[1].d + lbc.d, kT[1].d)
            for ci in range(4):
                ch = t * 4 + ci; s_ = ch // CPS; cpos = ch % CPS
                cs = slice(ci * CH, (ci + 1) * CH); gs = slice(ch * CH, (ch + 1) * CH)
                if cpos == 0:
                    init_state(Sf, s_, 0)
                if GSTOP == 2:
                    raise _Stop()
                if gla:
                    p = proj_tm(wB, wBd, 0, 512, ch)
                    cp("act", ktok[:, 0:256], p[:, 0:256], p.d, ktok.d)
                    cp("act", vbf.ap, p[:, 256:512], p.d, vbf.d)
                    if GSTOP == 21:
                        raise _Stop()
                    pz = nps()
                    mm(pz.ap, lrT[:, cs], wg.ap, True, True, lrT.d + wg.d, pz.d)
                    tt("dve", la.ap, pz.ap, bgb.ap, ALU.add, pz.d + bgb.d, la.d)
                    if GSTOP == 22:
                        raise _Stop()
                    act(la.ap, la.ap, AF.Exp, la.d, la.d, scale=-1.0)
                    act(la.ap, la.ap, AF.Ln, la.d, la.d, bias=1.0)
                    ts("dve", la.ap, la.ap, -1.0 / 16.0, None, ALU.mult, None, la.d, la.d)
                    kt_d = [ktok[:, 0:256], ktok[:, 0:256]]
                else:
                    p = nps()
                    for c in range(8):
                        mm(p[:, 0:256], uT[:, c, ch * CH:(ch + 1) * CH], wA[:, c, 256:512], c == 0, c == 7, wAd + [uT.d[ch // 4]], p.d)
                    for c in range(8):
                        mm(p[:, 256:512], uT[:, c, ch * CH:(ch + 1) * CH], wB[:, c, 0:256], c == 0, c == 7, wBd + [uT.d[ch // 4]], p.d)
                    act(la.ap, p.ap, AF.Sigmoid, p.d, la.d)
                    la3 = la.ap.rearrange("p (a b) -> p a b", a=2); kt3 = ktok.ap.rearrange("p (a b) -> p a b", a=2)
                    omB = om2.ap.unsqueeze(1).to_broadcast([128, 2, 256]); lbB = lb2.ap.unsqueeze(1).to_broadcast([128, 2, 256])
                    tt("dve", la3, la3, omB, ALU.mult, la.d + om2.d, la.d)
                    tt("dve", kt3, omB, la3, ALU.subtract, la.d + om2.d, ktok.d)
                    tt("dve", la3, la3, lbB, ALU.add, la.d + lb2.d, la.d)
                    act(la.ap, la.ap, AF.Ln, la.d, la.d)
                    p = proj_tm(wB, wBd, 256, 256, ch)
                    cp("act", vbf.ap, p[:, 0:256], p.d, vbf.d)
                    kt_d = [ktok[:, 0:256], ktok[:, 256:512]]
                if GSTOP == 3:
                    raise _Stop()
                pb = nps(); pt = nps()
                for d in range(2):
                    for hp in range(2):
                        blk = d * 2 + hp
                        mm(pb[:, blk * CH:(blk + 1) * CH], la[:, d * 256 + hp * 128:d * 256 + (hp + 1) * 128], tri[d], True, True, la.d + cm.d, pb.d)
                mm(pt[:, 0:256], su, la[:, 0:256], True, True, la.d + cm.d, pt.d)
                mm(pt[:, 256:512], sl, la[:, 256:512], True, True, la.d + cm.d, pt.d)
                H2 = CH // 2
                pbv = pb.ap.rearrange("p (b c) -> p b c", c=CH)
                for d in range(2):
                    endc = CH - 1 if d == 0 else 0
                    act(dec[:, d * 2:d * 2 + 2], pbv[:, d * 2:d * 2 + 2, endc], AF.Exp, pb.d, dec.d)
                act(ek.ap, pt.ap, AF.Exp, pt.d, ek.d)
                for d in range(2):
                    kTd = kT[0] if gla else kT[d]
                    tt("pool" if d else "dve", khat[d].ap, kt_d[d], ek[:, d * 256:(d + 1) * 256], ALU.mult, ktok.d + ek.d, khat[d].d)
                x2 = EB[0]
                for d in range(2):
                    bcol = H2 - 1 + d
                    for hp in range(2):
                        blk = d * 2 + hp
                        for hf_ in range(2):
                            mid = hf_ * H2 + H2 // 2 - 1 + d
                            cols = slice(hf_ * H2, (hf_ + 1) * H2)
                            ts("dve", xs[:, blk, cols], pb[:, blk * CH + hf_ * H2:blk * CH + (hf_ + 1) * H2], pb[:, blk * CH + mid:blk * CH + mid + 1], -80.0,
                               ALU.subtract, ALU.max, pb.d, xs.d)
                        ts("dve", x2[:, blk, :], pb[:, blk * CH:(blk + 1) * CH], pb[:, blk * CH + bcol:blk * CH + bcol + 1], -80.0,
                           ALU.subtract, ALU.max, pb.d, x2.d)
                ts("dve", xs.ap, xs.ap, 80.0, None, ALU.min, None, xs.d, xs.d)
                ts("dve", x2.ap, x2.ap, 80.0, None, ALU.min, None, x2.d, x2.d)
                e1p = ekn; e1n = EB[1]; e2p = EB[2]; e2n = EB[3]
                act(e1p.ap, xs.ap, AF.Exp, xs.d, e1p.d)
                act(e1n.ap, xs.ap, AF.Exp, xs.d, e1n.d, scale=-1.0)
                act(e2p.ap, x2.ap, AF.Exp, x2.d, e2p.d)
                act(e2n.ap, x2.ap, AF.Exp, x2.d, e2n.d, scale=-1.0)
                for d in range(2):
                    kTd = kT[0] if gla else kT[d]
                    tt("pool" if d else "dve", Q1[d].ap, qT[:, :, cs], e1p[:, d * 2:d * 2 + 2, :], ALU.mult, qT.d + e1p.d, Q1[d].d)
                    tt("pool" if d else "dve", ktl[d].ap, kTd[:, :, cs], e1n[:, d * 2:d * 2 + 2, :], ALU.mult, kTd.d + e1n.d, ktl[d].d)
                    tt("pool", Q2[d].ap, qT[:, :, cs], e2p[:, d * 2:d * 2 + 2, :], ALU.mult, qT.d + e2p.d, Q2[d].d)
                    tt("pool" if d else "dve", K2[d].ap, kTd[:, :, cs], e2n[:, d * 2:d * 2 + 2, :], ALU.mult, kTd.d + e2n.d, K2[d].d)
                ebq = xs
                act(ebq.ap, pbv, AF.Exp, pb.d + xs.d, ebq.d)
                tt("dve", qhf.ap, qT[:, :, cs], ebq[:, 0:2, :], ALU.mult, qT.d + ebq.d, qhf.d)
                tt("pool", qbb[:, :, gs], qT[:, :, cs], ebq[:, 2:4, :], ALU.mult, qT.d + ebq.d, qbb.d)
                if GSTOP == 5:
                    raise _Stop()
                for d in range(2):
                    psc = nps()
                    for h in (0, 2, 1, 3):
                        hp, j = h // 2, h % 2
                        r_ = slice(j * 64, (j + 1) * 64)
                        lo = slice(0, H2); hi = slice(H2, CH)
                        mm(psc[0:H2, h * CH:h * CH + H2], ktl[d][r_, hp, lo], Q1[d][r_, hp, lo], True, True, ktl[d].d + Q1[d].d, psc.d, rg=j, mode=(64, 64))
                        mm(psc[H2:CH, h * CH + H2:(h + 1) * CH], ktl[d][r_, hp, hi], Q1[d][r_, hp, hi], True, True, ktl[d].d + Q1[d].d, psc.d, rg=j, mode=(64, 64))
                        if d == 0:
                            mm(psc[0:H2, h * CH + H2:(h + 1) * CH], K2[0][r_, hp, lo], Q2[0][r_, hp, hi], True, True, K2[0].d + Q2[0].d, psc.d, rg=j, mode=(64, 64))
                        else:
                            mm(psc[H2:CH, h * CH:h * CH + H2], K2[1][r_, hp, hi], Q2[1][r_, hp, lo], True, True, K2[1].d + Q2[1].d, psc.d, rg=j, mode=(64, 64))
                    S.op("dve", lambda e, d=d, psc=psc: e.copy_predicated(out=Am[d].ap, mask=mask8[d].ap, data=psc.ap.rearrange("p (h c) -> p h c", c=CH)),
                         reads=psc.d + mask8[d].d + Am[d].d, writes=Am[d].d)
                if GSTOP == 6:
                    raise _Stop()
                cp("pool", Ss.ap, Sf.ap, Sf.d, Ss.d)
                po = nps(); pq = nps()
                for h in range(4):
                    hp, j = h // 2, h % 2
                    o_ = po[j * 64:(j + 1) * 64, hp * CH:(hp + 1) * CH]
                    mm(o_, vbf[:, h * 64:(h + 1) * 64], Am[0][:, h, :], True, False, vbf.d + Am[0].d, po.d, mode=(128, 64))
                    mm(o_, vbf[:, h * 64:(h + 1) * 64], Am[1][:, h, :], False, True, vbf.d + Am[1].d, po.d, mode=(128, 64))
                for h in (0, 2, 1, 3):
                    hp, j = h // 2, h % 2
                    mm(pq[j * 64:(j + 1) * 64, hp * CH:(hp + 1) * CH], Ss[j * 64:(j + 1) * 64, hp, :], qhf[j * 64:(j + 1) * 64, hp, :], True, True,
                       Ss.d + qhf.d, pq.d, rg=j, mode=(64, 64))
                cp("act", oT[:, :, gs], po[:, 0:2 * CH].rearrange("p (h c) -> p h c", c=CH), po.d, oT.d)
                tt("dve", oT[:, :, gs], oT[:, :, gs], pq[:, 0:2 * CH].rearrange("p (h c) -> p h c", c=CH), ALU.add, oT.d + pq.d, oT.d)
                if GSTOP == 7:
                    raise _Stop()
                pkv = nps()
                for d in range(2):
                    for hp in range(2):
                        blk = d * 2 + hp
                        mm(pkv[:, blk * 128:(blk + 1) * 128], khat[d][:, hp * 128:(hp + 1) * 128], vbf[:, hp * 128:(hp + 1) * 128], True, True,
                           khat[d].d + vbf.d, pkv.d)
                for hp in range(2):
                    for j in range(2):
                        r_ = slice(j * 64, (j + 1) * 64)
                        stt(Sf[r_, hp, :], Sf[r_, hp, :], dec[r_, hp:hp + 1], pkv[r_, hp * 128 + j * 64:hp * 128 + (j + 1) * 64], ALU.mult, ALU.add,
                            Sf.d + dec.d + pkv.d, Sf.d)
                        cp("act", kvb[r_, ch, hp, :], pkv[r_, (2 + hp) * 128 + j * 64:(2 + hp) * 128 + (j + 1) * 64], pkv.d, kvb.d)
                cp("act", decb[:, ch, :], dec[:, 2:4], dec.d, decb.d)
                if cpos == CPS - 1:
                    out_state(Sf, s_, 0)
                if GSTOP == 8:
                    raise _Stop()
        if GSTOP == 9:
            raise _Stop()
        Ss2 = [Ss, A.alloc(BF16, [128, 2, 64])]
        pend = None
        for idx, ch in enumerate(reversed(range(NCH))):
            s_ = ch // CPS; cpos = ch % CPS
            gs = slice(ch * CH, (ch + 1) * CH)
            if cpos == CPS - 1:
                init_state(Sb, s_, 1)
            Sx_ = Ss2[idx % 2]
            cp("dve", Sx_.ap, Sb.ap, Sb.d, Sx_.d)
            po = nps()
            for h in (0, 2, 1, 3):
                hp, j = h // 2, h % 2
                mm(po[j * 64:(j + 1) * 64, hp * CH:(hp + 1) * CH], Sx_[j * 64:(j + 1) * 64, hp, :], qbb[j * 64:(j + 1) * 64, hp, gs], True, True,
                   Sx_.d + qbb.d, po.d, rg=j, mode=(64, 64))
            for hp in range(2):
                stt(Sb[:, hp, :], Sb[:, hp, :], decb[:, ch, hp:hp + 1], kvb[:, ch, hp, :], ALU.mult, ALU.add, Sb.d + decb.d + kvb.d, Sb.d)
            if cpos == 0:
                out_state(Sb, s_, 1)
            if pend is not None:
                ppo, pgs = pend
                tt("dve", oT[:, :, pgs], oT[:, :, pgs], ppo[:, 0:2 * CH].rearrange("p (h c) -> p h c", c=CH), ALU.add, oT.d + ppo.d, oT.d)
            pend = (po, gs)
        ppo, pgs = pend
        tt("dve", oT[:, :, pgs], oT[:, :, pgs], ppo[:, 0:2 * CH].rearrange("p (h c) -> p h c", c=CH), ALU.add, oT.d + ppo.d, oT.d)
        if GSTOP == 10:
            raise _Stop()
        sg = la; t1 = ek; rs = ktok
        yb = [T(Am[i].ap.rearrange("p a b -> p (a b)"), 0) for i in range(2)]; yb[0].d = Am[0].d; yb[1].d = Am[1].d
        A_sq = T(xs.ap.rearrange("p a b -> p (a b)").bitcast(BF16)[:, 0:512], 0); A_sq.d = xs.d
        ysems = [SPL[4], SPL[5]]
        k = 0
        for t in range(NT):
            for hp in range(2):
                pg = proj_fm(wC, wCd, hp * 128, 128, t)
                act(sg.ap, pg.ap, AF.Silu, pg.d, sg.d)
                o_ = oT[:, hp, t * 512:(t + 1) * 512]
                sq2 = A_sq
                act(sq2.ap, o_, AF.Square, oT.d, sq2.d)
                pm = nps()
                mm(pm.ap, bd64.ap, sq2.ap, True, True, bd64.d + sq2.d, pm.d)
                act(rs.ap, pm.ap, AF.Sqrt, pm.d, rs.d, bias=EPS)
                S.op("dve", lambda e: e.reciprocal(out=rs.ap, in_=rs.ap), reads=rs.d, writes=rs.d)
                tt("dve", t1.ap, o_, rs.ap, ALU.mult, oT.d + rs.d, t1.d)
                y_ = yb[k % 2]
                stt(y_.ap, t1.ap, C(grow + hp), sg.ap, ALU.mult, ALU.mult, t1.d + sg.d + colT.d, y_.d)
                S.dma("sp", yTs[yc0 + hp, :, t * 512:(t + 1) * 512], y_.ap, reads=y_.d, writes=[yTs_d[t]], sem=ysems[k % 2])
                k += 1
        A.release(m)
        S.barrier()


    def mixer_hy(pi, l, L, NS, yTs, yTs_d, ysem):
        m = A.mark()
        Tn = L * NS; NT = Tn // 512; NCHL = L // 128; NCH = Tn // 128
        N = 6144 if L == 4096 else 512
        NFB = (N // 2) // 128
        Gc = I["gc%d" % L]; Gs = I["gs%d" % L]
        wl = I["w_in"][l]
        ucs = dscr("ucs%d_%d" % (pi, l), [Tn, 768]); ucs_d = [Dep() for _ in range(NT)]
        zscr = dscr("zscr%d_%d" % (pi, l), [Tn, 256]); zscr_d = [Dep() for _ in range(NCH)]
        ZW = NS * 256 + 256
        HC = NS * 256
        zh = A.alloc(BF16, [128, NCHL, ZW])
        h2T = A.alloc(F32, [128, L])
        decB = A.alloc(F32, [128, 512]); skipB = A.alloc(F32, [128, 512]); rsum = A.alloc(F32, [128, 512])
        ndist = A.alloc(F32, [128, NCHL]); ecs = A.alloc(F32, [128, 2, NFB])
        W3 = A.alloc(F32, [128, 512])
        S.op("pool", lambda e: e.memset(W3.ap, 0.0), writes=W3.d)
        S.op("pool", lambda e: e.memset(h2T.ap, 0.0), writes=h2T.d)
        S.dma("sp", decB.ap, I["hy_dec"][l].partition_broadcast(128), writes=decB.d, sem=SPL[0])
        S.dma("sp", skipB.ap, I["hy_skip"][l].partition_broadcast(128), writes=skipB.d, sem=SPL[1])
        S.dma("sp", ecs.ap, I["e%d" % L], writes=ecs.d, sem=SPL[2])
        S.dma("sp", W3[0:64, :], I["hy_w3"][l], writes=W3.d, sem=SPL[3])
        mf = A.mark()
        W1 = A.alloc(F32, [128, 128]); W2 = A.alloc(F32, [128, 128])
        S.op("pool", lambda e: e.memset(W1.ap, 0.0), writes=W1.d)
        S.op("pool", lambda e: e.memset(W2.ap, 0.0), writes=W2.d)
        S.dma("sp", W1[0:33, 0:64], I["hy_w1"][l], writes=W1.d, sem=SPL[4])
        S.dma("sp", W2[0:64, 0:64], I["hy_w2"][l], writes=W2.d, sem=SPL[5])
        posi = A.alloc(I32, [128, 512]); posf = A.alloc(F32, [128, 512]); arg = A.alloc(F32, [128, 512])
        ti = A.alloc(I32, [128, 512]); tf = A.alloc(F32, [128, 512]); pe = A.alloc(F32, [128, 512]); h1 = A.alloc(F32, [128, 512])
        pcr = SP_ROWS["pec%d" % L]
        S.op("pool", lambda e: e.memset(pe.ap, 0.0), writes=pe.d)
        S.op("pool", lambda e: e.memset(h1.ap, 0.0), writes=h1.d)
        for j in range(L // 512 if L >= 512 else 1):
            w_ = min(512, L)
            S.op("pool", lambda e, j=j, w_=w_: e.iota(posi[:, 0:w_], pattern=[[1, w_]], base=j * 512, channel_multiplier=0), writes=posi.d)
            cp("dve", posf[:, 0:w_], posi[:, 0:w_], posi.d, posf.d)
            ts("dve", arg[0:33, 0:w_], posf[0:33, 0:w_], C(pcr)[0:33, :], C(pcr + 1)[0:33, :], ALU.mult, ALU.add, posf.d + colT.d, arg.d)
            a33 = T(arg[0:33, 0:w_], 0); a33.d = arg.d
            i33 = T(ti[0:33, 0:w_], 0); i33.d = ti.d
            f33 = T(tf[0:33, 0:w_], 0); f33.d = tf.d
            sin_rr(pe[0:33, 0:w_], a33, i33, f33, [], pe.d)
            ts("dve", pe[0:1, 0:w_], posf[0:1, 0:w_], C(pcr)[0:1, :], None, ALU.mult, None, posf.d + colT.d + pe.d, pe.d)
            p = nps()
            mm(p[:, 0:w_], W1.ap, pe[:, 0:w_], True, True, W1.d + pe.d, p.d)
            act(arg[0:64, 0:w_], p[0:64, 0:w_], AF.Identity, p.d + colT.d, arg.d, bias=C(SP_ROWS["hyb1%d" % l])[0:64, :], scale=1.0)
            a64 = T(arg[0:64, 0:w_], 0); a64.d = arg.d
            i64 = T(ti[0:64, 0:w_], 0); i64.d = ti.d
            f64 = T(tf[0:64, 0:w_], 0); f64.d = tf.d
            sin_rr(h1[0:64, 0:w_], a64, i64, f64, [], h1.d)
            p = nps()
            mm(p[:, 0:w_], W2.ap, h1[:, 0:w_], True, True, W2.d + h1.d, p.d)
            act(arg[0:64, 0:w_], p[0:64, 0:w_], AF.Identity, p.d + colT.d, arg.d, bias=C(SP_ROWS["hyb2%d" % l])[0:64, :], scale=1.0)
            sin_rr(h2T[0:64, j * 512:j * 512 + w_], a64, i64, f64, [], h2T.d)
            if GSTOP == 30:
                raise _Stop()
        if GSTOP == 31:
            raise _Stop()
        ndi = T(posi[:, 0:NCHL], 0); ndi.d = posi.d
        S.op("pool", lambda e: e.iota(ndi.ap, pattern=[[128, NCHL]], base=0, channel_multiplier=1), writes=posi.d)
        cp("dve", ndist.ap, ndi.ap, posi.d, ndist.d)
        ts("dve", ndist.ap, ndist.ap, -float(L // 2), None, ALU.add, None, ndist.d, ndist.d)
        act(ndist.ap, ndist.ap, AF.Abs, ndist.d, ndist.d)
        ts("dve", ndist.ap, ndist.ap, -2.0 / L, None, ALU.mult, None, ndist.d, ndist.d)
        hr = pe; ee = h1; hab = A.alloc(BF16, [128, 512])
        acc = PS[7]
        for tc in range(NCHL):
            p = nps()
            mm(p.ap, h2T[:, tc * 128:(tc + 1) * 128], W3.ap, True, True, h2T.d + W3.d, p.d)
            act(ee.ap, decB.ap, AF.Exp, decB.d + ndist.d, ee.d, scale=ndist[:, tc:tc + 1])
            tt("dve", hr.ap, p.ap, ee.ap, ALU.mult, p.d + ee.d, hr.d)
            act(hab.ap, hr.ap, AF.Abs, hr.d, hab.d)
            mm(acc.ap, onesm.ap, hab.ap, tc == 0, tc == NCHL - 1, onesm.d + hab.d, acc.d)
        ts("dve", rsum.ap, acc.ap, float(D), None, ALU.mult, None, acc.d, rsum.d)
        S.op("dve", lambda e: e.reciprocal(out=rsum.ap, in_=rsum.ap), reads=rsum.d, writes=rsum.d)
        A.release(mf)
        S.barrier()
        if GSTOP == 32:
            raise _Stop()
        mc = A.mark()
        w1s, w1d = load_w(wl, 0, D, IN_OFF["c_v"], 512)
        w2s, w2d = load_w(wl, 0, D, IN_OFF["c_x2"], 256)
        xpad = A.alloc(F32, [128, NS, L + 2]); xc = A.alloc(F32, [128, NS, L])
        stg = [A.alloc(F32, [128, 4, 128]) for _ in range(2)]
        usem = [SPL[6], SPL[7]]
        hcw = SP_ROWS["hycw%d" % l]; hcb = SP_ROWS["hycb%d" % l]
        k = 0
        for ct in range(6):
            ws, wsd, c0 = (w1s, w1d, ct * 128) if ct < 4 else (w2s, w2d, (ct - 4) * 128)
            S.op("pool", lambda e: e.memset(xpad.ap.rearrange("p a b -> p (a b)"), 0.0), writes=xpad.d)
            for t in range(NT):
                p = proj_fm(ws, wsd, c0, 128, t)
                for (s_, off, n, a0) in seg_of(L, t):
                    cp("act", xpad[:, s_, 1 + off:1 + off + n], p[:, a0:a0 + n], p.d, xpad.d)
            for s_ in range(NS):
                ts("dve", xc[:, s_, :], xpad[:, s_, 0:L], C(hcw + 0 * 6 + ct), C(hcb + ct), ALU.mult, ALU.add, xpad.d + colT.d, xc.d)
                for j in (1, 2):
                    stt(xc[:, s_, :], xpad[:, s_, j:j + L], C(hcw + j * 6 + ct), xc[:, s_, :], ALU.mult, ALU.add, xpad.d + colT.d + xc.d, xc.d)
            for t in range(NT):
                p = nps()
                for ci in range(4):
                    ch = t * 4 + ci; s_ = ch // NCHL; tcs = ch % NCHL
                    S.op("pe", lambda e, p=p, ci=ci, s_=s_, tcs=tcs: e.transpose(p[:, ci * 128:(ci + 1) * 128], xc[:, s_, tcs * 128:(tcs + 1) * 128], ident),
                         reads=xc.d + cm.d, writes=p.d)
                sg_ = stg[k % 2]
                cp("act", sg_.ap, p.ap.rearrange("p (c n) -> p c n", c=4), p.d, sg_.d)
                S.dma("sp", ucs[t * 512:(t + 1) * 512, ct * 128:(ct + 1) * 128].rearrange("(c p) n -> p c n", p=128), sg_.ap,
                      reads=sg_.d, writes=[ucs_d[t]], sem=usem[k % 2])
                k += 1
                if ct < 2:
                    for ci in range(4):
                        ch = t * 4 + ci; s_ = ch // NCHL; tcs = ch % NCHL
                        cp("dve", zh[:, tcs, s_ * 256 + ct * 128:s_ * 256 + (ct + 1) * 128], sg_[:, ci, :], sg_.d, zh.d)
        A.release(mc)
        S.barrier()
        if GSTOP == 33:
            raise _Stop()
        uflat = uT.ap.rearrange("p a b -> p (a b)")
        PW = NFB * NS * 2 * 256
        P_ = T(uflat[:, 0:PW].rearrange("p (f s r n) -> p f s r n", f=NFB, s=NS, r=2), 1)
        tabs = [T(uflat[:, 12288 + i * 4096:12288 + (i + 1) * 4096].rearrange("p (c j) -> p c j", j=128), 1) for i in range(4)]
        tsem = [SPL[8 + i] for i in range(4)]
        tn_ = [0]

        def load_tab(G, blk, nchunk):
            i = tn_[0] % 4
            tn_[0] += 1
            tb_ = tabs[i]
            S.dma("sp", tb_[:, 0:nchunk, :], G[blk][:, 0:nchunk, :], writes=tb_.d, sem=tsem[i])
            return tb_

        mg = A.mark()
        AB = A.alloc(F32, [128, 2, 256]); tA = A.alloc(F32, [128, 256]); tB = A.alloc(F32, [128, 256])
        gt = [A.alloc(F32, [128, 256]) for _ in range(2)]; zt = [A.alloc(F32, [128, 256]) for _ in range(2)]
        gsm = [SPL[12], SPL[13]]; zsm = [SPL[14], SPL[15]]; zssem = [SPL[16], SPL[17]]; ysm = [SPL[18], SPL[19]]
        zn = [A.alloc(F32, [128, 256]) for _ in range(2)]
        ystg = [A.alloc(BF16, [128, 2, 512]) for _ in range(2)]
        hr2 = A.alloc(F32, [128, 256]); ee2 = A.alloc(F32, [128, 256])
        for n in range(2):
            nc0 = n * 256
            if GSTOP == 35 and n == 1:
                raise _Stop()
            for tc in range(NCHL):
                p = nps()
                mm(p[:, 0:256], h2T[:, tc * 128:(tc + 1) * 128], W3[:, nc0:nc0 + 256], True, True, h2T.d + W3.d, p.d)
                act(ee2.ap, decB[:, nc0:nc0 + 256], AF.Exp, decB.d + ndist.d, ee2.d, scale=ndist[:, tc:tc + 1])
                tt("dve", hr2.ap, p[:, 0:256], ee2.ap, ALU.mult, p.d + ee2.d, hr2.d)
                tt("dve", zh[:, tc, HC:HC + 256], hr2.ap, rsum[:, nc0:nc0 + 256], ALU.mult, hr2.d + rsum.d, zh.d)
            for fb in range(NFB):
                tcb = load_tab(Gc, fb, NCHL); tsb = load_tab(Gs, fb, NCHL)
                pcm = psm = None
                if NS == 1:
                    pcm = nps(); psm = nps()
                    for tc in range(NCHL):
                        mm(pcm.ap, tcb[:, tc, :], zh[:, tc, 0:512], tc == 0, tc == NCHL - 1, tcb.d + zh.d, pcm.d, last=(tc == NCHL - 1))
                    for tc in range(NCHL):
                        mm(psm.ap, tsb[:, tc, :], zh[:, tc, 0:512], tc == 0, tc == NCHL - 1, tsb.d + zh.d, psm.d, last=(tc == NCHL - 1))
                for g in [NS] + list(range(NS)):
                    c0 = g * 256
                    if NS == 1:
                        pc = T(pcm[:, c0:c0 + 256], 0); pc.d = pcm.d
                        ps_ = T(psm[:, c0:c0 + 256], 0); ps_.d = psm.d
                    else:
                        pc = nps(); ps_ = nps()
                        for tc in range(NCHL):
                            mm(pc[:, 0:256], tcb[:, tc, :], zh[:, tc, c0:c0 + 256], tc == 0, tc == NCHL - 1, tcb.d + zh.d, pc.d, last=(tc == NCHL - 1))
                        for tc in range(NCHL):
                            mm(ps_[:, 0:256], tsb[:, tc, :], zh[:, tc, c0:c0 + 256], tc == 0, tc == NCHL - 1, tsb.d + zh.d, ps_.d, last=(tc == NCHL - 1))
                    ec = ecs[:, 0, fb:fb + 1]; es = ecs[:, 1, fb:fb + 1]
                    if g == NS:
                        ts("dve", AB[:, 0, :], pc[:, 0:256], ec, None, ALU.mult, None, pc.d + ecs.d, AB.d)
                        stt(AB[:, 0, :], ps_[:, 0:256], es, AB[:, 0, :], ALU.mult, ALU.add, ps_.d + ecs.d + AB.d, AB.d)
                        ts("dve", AB[:, 1, :], pc[:, 0:256], es, None, ALU.mult, None, pc.d + ecs.d, AB.d)
                        ts("dve", tA.ap, ps_[:, 0:256], ec, None, ALU.mult, None, ps_.d + ecs.d, tA.d)
                        tt("dve", AB[:, 1, :], AB[:, 1, :], tA.ap, ALU.subtract, AB.d + tA.d, AB.d)
                    else:
                        tt("dve", tA.ap, pc[:, 0:256], AB[:, 0, :], ALU.mult, pc.d + AB.d, tA.d)
                        tt("dve", tB.ap, ps_[:, 0:256], AB[:, 1, :], ALU.mult, ps_.d + AB.d, tB.d)
                        tt("pool", P_[:, fb, g, 0, :], tA.ap, tB.ap, ALU.add, tA.d + tB.d, P_.d)
                        tt("dve", tA.ap, ps_[:, 0:256], AB[:, 0, :], ALU.mult, ps_.d + AB.d, tA.d)
                        tt("dve", tB.ap, pc[:, 0:256], AB[:, 1, :], ALU.mult, pc.d + AB.d, tB.d)
                        tt("pool", P_[:, fb, g, 1, :], tA.ap, tB.ap, ALU.subtract, tA.d + tB.d, P_.d)
            if GSTOP == 34:
                raise _Stop()
            k = 0
            for s_ in range(NS):
                for tb in range(NCHL):
                    ch = s_ * NCHL + tb
                    rows = slice(ch * 128, (ch + 1) * 128)
                    tcb = load_tab(Gc, tb, NFB); tsb = load_tab(Gs, tb, NFB)
                    g_ = gt[k % 2]; z_ = zt[k % 2]
                    S.dma("sp", g_.ap, ucs[rows, (n + 1) * 256:(n + 2) * 256], reads=[ucs_d[ch // 4]], writes=g_.d, sem=gsm[k % 2])
                    if n == 0:
                        S.dma("sp", z_.ap, ucs[rows, 0:256], reads=[ucs_d[ch // 4]], writes=z_.d, sem=zsm[k % 2])
                    else:
                        S.dma("sp", z_.ap, zscr[rows, :], reads=[zscr_d[ch]], writes=z_.d, sem=zsm[k % 2])
                    py = nps()
                    for fc in range(NFB):
                        mm(py[:, 0:256], tcb[:, fc, :], P_[:, fc, s_, 0, :], fc == 0, False, tcb.d + P_.d, py.d, last=False)
                    for fc in range(NFB):
                        mm(py[:, 0:256], tsb[:, fc, :], P_[:, fc, s_, 1, :], False, fc == NFB - 1, tsb.d + P_.d, py.d, last=(fc == NFB - 1))
                    o_ = zn[k % 2]
                    tt("pool", o_.ap, z_.ap, skipB[:, nc0:nc0 + 256], ALU.mult, z_.d + skipB.d, o_.d)
                    tt("dve", o_.ap, o_.ap, py[:, 0:256], ALU.add, o_.d + py.d, o_.d)
                    tt("dve", o_.ap, o_.ap, g_.ap, ALU.mult, o_.d + g_.d, o_.d)
                    if n == 0:
                        cp("act", zh[:, tb, s_ * 256:(s_ + 1) * 256], o_.ap, o_.d, zh.d)
                        S.dma("sp", zscr[rows, :], o_.ap, reads=o_.d, writes=[zscr_d[ch]], sem=zssem[k % 2])
                    else:
                        t = ch // 4; ci = ch % 4
                        ys_ = ystg[t % 2]
                        pt = nps()
                        for j in range(2):
                            S.op("pe", lambda e, pt=pt, j=j, o_=o_: e.transpose(pt[:, j * 128:(j + 1) * 128], o_[:, j * 128:(j + 1) * 128], ident),
                                 reads=o_.d + cm.d, writes=pt.d)
                        cp("act", ys_[:, :, ci * 128:(ci + 1) * 128], pt[:, 0:256].rearrange("p (j n) -> p j n", j=2), pt.d, ys_.d)
                        if ci == 3:
                            S.dma("sp", yTs[4:6, :, t * 512:(t + 1) * 512].rearrange("c p n -> p c n"), ys_.ap, reads=ys_.d, writes=[yTs_d[t]], sem=ysm[t % 2])
                    k += 1
        A.release(mg)
        A.release(m)
        S.barrier()

    def norm_mod(xT, xd, acols, bcols, out_ap, out_d, tmp):
        rs = tmp["rs"]
        p = nps()
        for c in range(8):
            sq = tmp["sq"][c % 2]
            act(sq.ap, xT[:, c, :], AF.Square, xd, sq.d)
            mm(p.ap, onesm.ap, sq.ap, c == 0, c == 7, sq.d + onesm.d, p.d, last=True)
        act(rs.ap, p.ap, AF.Sqrt, p.d, rs.d, bias=EPS)
        S.op("dve", lambda e: e.reciprocal(out=rs.ap, in_=rs.ap), reads=rs.d, writes=rs.d)
        for c in range(8):
            xn = tmp["xn"][c % 2]
            tt("dve", xn.ap, xT[:, c, :], rs.ap, ALU.mult, xd + rs.d, xn.d)
            if bcols is None:
                act(out_ap[:, c, :], xn.ap, AF.Identity, xn.d + colT.d, out_d, scale=acols[:, c:c + 1], bias=0.0)
            else:
                act(out_ap[:, c, :], xn.ap, AF.Identity, xn.d + acol.d + modc.d, out_d,
                    scale=acols[:, c:c + 1], bias=bcols[:, c:c + 1])

    def mk_tmp():
        return dict(sq=[A.alloc(BF16, [128, 512]) for _ in range(2)], rs=A.alloc(F32, [128, 512]),
                    xn=[A.alloc(F32, [128, 512]) for _ in range(2)])

    def run_part(pi, L, NS, x_in, y_out):
        Tn = L * NS
        NT = Tn // 512
        xTs = dscr("xTs%d" % pi, [8, 128, Tn]); xTs_d = [Dep() for _ in range(NT)]
        yTs = dscr("yTs%d" % pi, [8, 128, Tn], BF16); yTs_d = [Dep() for _ in range(NT)]
        xsem = S.new_dsem(); xssem = S.new_dsem(); ysem = S.new_dsem(); osem = [S.new_dsem(), S.new_dsem()]
        tsem = [S.new_dsem(), S.new_dsem()]

        m1 = A.mark()
        xtok = [A.alloc(F32, [128, D]) for _ in range(2)]
        xT = A.alloc(F32, [128, 8, 512])
        tmp = mk_tmp()
        if pi == 0:
            om = A.alloc(F32, [128, 512]); ph = A.alloc(F32, [128, 512]); rc = A.alloc(F32, [128, 2])
            pcol = A.alloc(F32, [128, 512]); prow = A.alloc(F32, [128, 512]); arg = A.alloc(F32, [128, 512])
            ti = A.alloc(I32, [128, 512]); tf = A.alloc(F32, [128, 512]); rv = A.alloc(F32, [128, 1])
            S.dma("sp", om.ap, I["gridc"][0, 0:512].partition_broadcast(128), writes=om.d, sem=gsem[2])
            S.dma("sp", ph.ap, I["gridc"][1, 0:512].partition_broadcast(128), writes=ph.d, sem=gsem[3])
            S.dma("sp", rc.ap, I["gridrc"], writes=rc.d, sem=gsem[4])
            stt(arg.ap, om.ap, rc[:, 1:2], ph.ap, ALU.mult, ALU.add, om.d + ph.d + rc.d, arg.d)
            sin_rr(pcol.ap, arg, ti, tf, [], pcol.d)
        for t in range(NT):
            for j in range(4):
                blk = t * 4 + j
                xk = xtok[blk % 2]
                S.dma("sp", xk.ap, x_in[blk * 128:(blk + 1) * 128, :], writes=xk.d, sem=tsem[blk % 2])
                if pi == 0:
                    ts("dve", rv.ap, rc[:, 0:1], float(2 * blk), None, ALU.add, None, rc.d, rv.d)
                    stt(arg.ap, om.ap, rv[:, 0:1], ph.ap, ALU.mult, ALU.add, om.d + ph.d + rv.d, arg.d)
                    sin_rr(prow.ap, arg, ti, tf, [], prow.d)
                    tt("dve", xk[:, 0:512], xk[:, 0:512], prow.ap, ALU.add, xk.d + prow.d, xk.d)
                    tt("dve", xk[:, 512:1024], xk[:, 512:1024], pcol.ap, ALU.add, xk.d + pcol.d, xk.d)
                for h in range(2):
                    p = nps()
                    for c4 in range(4):
                        c = h * 4 + c4
                        S.op("pe", lambda e, p=p, c4=c4, c=c, xk=xk: e.transpose(p[:, c4 * 128:(c4 + 1) * 128], xk[:, c * 128:(c + 1) * 128], ident),
                             reads=xk.d + cm.d, writes=p.d, inc=(c4 == 3))
                    cp("act", xT[:, h * 4:h * 4 + 4, j * 128:(j + 1) * 128], p.ap.rearrange("p (c n) -> p c n", c=4), p.d, xT.d)
            S.dma("sp", xTs[:, :, t * 512:(t + 1) * 512].rearrange("c p n -> p c n"), xT.ap, reads=xT.d, writes=[xTs_d[t]], sem=xssem)
            norm_mod(xT.ap, xT.d, acol[:, 0, 0, pi, :], modc[:, 0, 0:8, pi], uT[:, :, t * 512:(t + 1) * 512], [uT.d[t]], tmp)
        A.release(m1)
        S.barrier()
        if stop == 1:
            return

        for l in range(NL):
            m2 = A.mark()
            zt = A.alloc(BF16, [128, 8, 512])
            S.op("pool", lambda e: e.memset(zt.ap, 0.0), writes=zt.d)
            done = set()
            if "gla" in flags:
                mixer_gated(pi, l, L, NS, yTs, yTs_d, ysem, "gla"); done.add(0)
            if "rg" in flags:
                mixer_rg(pi, l, L, NS, yTs, yTs_d, ysem); done.add(1)
            if "hg" in flags:
                mixer_gated(pi, l, L, NS, yTs, yTs_d, ysem, "hg"); done.add(3)
            if "hy" in flags:
                mixer_hy(pi, l, L, NS, yTs, yTs_d, ysem); done.add(2)
            for q in range(4):
                if q not in done:
                    for t in range(NT):
                        S.dma("sp", yTs[2 * q:2 * q + 2, :, t * 512:(t + 1) * 512].rearrange("c p n -> p c n"), zt[:, 0:2, :],
                              reads=zt.d, writes=[yTs_d[t]], sem=ysem)
            A.release(m2)
            S.barrier()
            if stop == 2:
                return

            m3 = A.mark()
            NP = 2
            xTl = [A.alloc(F32, [128, 8, 512]) for _ in range(NP)]; yTl = [A.alloc(BF16, [128, 8, 512]) for _ in range(NP)]
            u2l = [A.alloc(BF16, [128, 8, 512]) for _ in range(NP)]
            hTl = [[A.alloc(BF16, [128, 4, 512]) for _ in range(2)] for _ in range(NP)]
            hs = [A.alloc(F32, [128, 512]) for _ in range(2)]
            tmp = mk_tmp()
            last = (l == NL - 1)
            if last:
                otok = [A.alloc(F32, [128, D]) for _ in range(2)]
            if l == 0:
                xsl = [S.new_dsem() for _ in range(NP)]; ysl = [S.new_dsem() for _ in range(NP)]
            g1 = modc[:, l, 16:24, pi]; g2 = modc[:, l, 40:48, pi]
            for tp in range(0, NT, NP):
                tl = list(range(tp, min(tp + NP, NT)))
                for i, t in enumerate(tl):
                    S.dma("sp", xTl[i].ap, xTs[:, :, t * 512:(t + 1) * 512].rearrange("c p n -> p c n"), reads=[xTs_d[t]], writes=xTl[i].d, sem=xsl[i])
                    S.dma("sp", yTl[i].ap, yTs[:, :, t * 512:(t + 1) * 512].rearrange("c p n -> p c n"), reads=[yTs_d[t]], writes=yTl[i].d, sem=ysl[i])
                for g in range(2):
                    w, wd = load_w(I["w_out"][l], 0, D, g * 512, 512)
                    for i, t in enumerate(tl):
                        xT = xTl[i]; yT = yTl[i]
                        for j in range(4):
                            oc = g * 4 + j
                            p = nps()
                            for c in range(8):
                                mm(p.ap, w[:, c, j * 128:(j + 1) * 128], yT[:, c, :], c == 0, c == 7, wd + yT.d, p.d, last=(c == 7))
                            stt(xT[:, oc, :], p.ap, g1[:, oc:oc + 1], xT[:, oc, :], ALU.mult, ALU.add, p.d + xT.d + modc.d, xT.d)
                for i, t in enumerate(tl):
                    norm_mod(xTl[i].ap, xTl[i].d, acol[:, l, 1, pi, :], modc[:, l, 24:32, pi], u2l[i].ap, u2l[i].d, tmp)
                ngrp = [(g * 512, min(512, DFF - g * 512)) for g in range(6)]
                for gi, (h0, hn) in enumerate(ngrp):
                    nb = hn // 128
                    w1, w1d = load_w(I["w1"][l], 0, D, h0, hn)
                    w3, w3d = load_w(I["w3"][l], 0, D, h0, hn)
                    w2, w2d = load_w(I["w2"][l], h0, hn, 0, D)
                    for i, t in enumerate(tl):
                        xT = xTl[i]; u2 = u2l[i]
                        hb = hTl[i][gi % 2]
                        for j in range(nb):
                            p1 = nps(); p3 = nps()
                            for c in range(8):
                                mm(p1.ap, w1[:, c, j * 128:(j + 1) * 128], u2[:, c, :], c == 0, c == 7, w1d + u2.d, p1.d, last=(c == 7))
                            for c in range(8):
                                mm(p3.ap, w3[:, c, j * 128:(j + 1) * 128], u2[:, c, :], c == 0, c == 7, w3d + u2.d, p3.d, last=(c == 7))
                            hh = hs[j % 2]
                            act(hh.ap, p1.ap, AF.Silu, p1.d, hh.d)
                            tt("dve", hb[:, j, :], hh.ap, p3.ap, ALU.mult, hh.d + p3.d, hb.d)
                        for oc in range(8):
                            p = nps()
                            for j in range(nb):
                                mm(p.ap, w2[:, j, oc * 128:(oc + 1) * 128], hb[:, j, :], j == 0, j == nb - 1, w2d + hb.d, p.d, last=(j == nb - 1))
                            stt(xT[:, oc, :], p.ap, g2[:, oc:oc + 1], xT[:, oc, :], ALU.mult, ALU.add, p.d + xT.d + modc.d, xT.d)
                for i, t in enumerate(tl):
                    xT = xTl[i]
                    if not last:
                        S.dma("sp", xTs[:, :, t * 512:(t + 1) * 512].rearrange("c p n -> p c n"), xT.ap, reads=xT.d, writes=[xTs_d[t]], sem=xsl[i])
                        norm_mod(xT.ap, xT.d, acol[:, l + 1, 0, pi, :], modc[:, l + 1, 0:8, pi], uT[:, :, t * 512:(t + 1) * 512], [uT.d[t]], tmp)
                    else:
                        yf = xT
                        norm_mod(xT.ap, xT.d, C(SP_ROWS["fng"], 8), None, yf.ap, yf.d, tmp)
                        for j in range(4):
                            blk = t * 4 + j
                            ok = otok[blk % 2]
                            for h in range(2):
                                p = nps()
                                for c4 in range(4):
                                    c = h * 4 + c4
                                    S.op("pe", lambda e, p=p, c4=c4, c=c, j=j, yf=yf: e.transpose(p[:, c4 * 128:(c4 + 1) * 128], yf[:, c, j * 128:(j + 1) * 128], ident),
                                         reads=yf.d + cm.d, writes=p.d, inc=(c4 == 3))
                                cp("act", ok[:, h * 512:(h + 1) * 512], p.ap, p.d, ok.d)
                            dd = Dep()
                            S.dma("sp", y_out[blk * 128:(blk + 1) * 128, :], ok.ap, reads=ok.d, writes=[dd], sem=osem[blk % 2])
                            out_deps.append(dd)
            A.release(m3)
            S.barrier()
            if stop == 3:
                return

    try:
        if stop >= 1:
            run_part(0, 4096, 1, I["xs"], O["ys"])
        if stop >= 5:
            run_part(1, 256, 4, I["xp"], O["yp"])
    except _Stop:
        S.barrier()
    w = S._waits("sp", out_deps, ())
    S.prog["sp"].append((w, None, None, 0))
    S.emit()
    return nc, st, S


FLAGS = ("gla", "rg", "hy", "hg")
STOP = 99
GSTOP = 0


class _Stop(Exception):
    pass

NCORES = 8
_CACHE = {}


def _consts():
    if "c" in _CACHE:
        return _CACHE["c"]
    p = np.arange(128)
    s_, t_ = p[:, None], p[None, :]
    cm = np.zeros((128, 6, 128), np.float32)
    cm[:, 0] = (s_ == t_); cm[:, 1] = (s_ <= t_); cm[:, 2] = (s_ >= t_); cm[:, 3] = (s_ > t_); cm[:, 4] = (s_ < t_)
    cm[:, 5] = ((s_ // 64) == (t_ // 64)) / 64.0
    om, ph, rc = _grid_consts(D)
    gc4, gs4, e4, _ = _dft_tables(4096, 6144)
    gc2, gs2, e2, _ = _dft_tables(256, 512)
    c = dict(cmat=cm, gc4096=gc4, gs4096=gs4, e4096=e4, gc256=gc2, gs256=gs2, e256=e2,
             gridc=np.stack([om, ph]).astype(np.float32), gridrc=rc)
    _CACHE["c"] = c
    return c


def kernel(**inp):
    inp = {k: np.asarray(v) for k, v in inp.items()}
    key = ("prog", FLAGS)
    if key not in _CACHE:
        _CACHE[key] = build_program(flags=FLAGS, stop=STOP)
    nc = _CACHE[key][0]
    cst = _consts()
    f32 = lambda a: np.ascontiguousarray(a, dtype=np.float32)
    shared = dict(
        w_mod=f32(inp["w_mod"]), w_in=f32(inp["w_in"]), w_out=f32(inp["w_out"]),
        w1=f32(inp["ffn_w1"]), w3=f32(inp["ffn_w3"]), w2=f32(inp["ffn_w2"]),
        gla_wg=f32(inp["gla_w_gate"].transpose(0, 2, 1, 3).reshape(NL, 16, 512)),
        gla_bg=f32(inp["gla_b_gate"].reshape(NL, 512)),
        rg_wa=f32(inp["rg_w_a"]), rg_wx=f32(inp["rg_w_x"]),
        hy_w1=f32(inp["hy_w1"]), hy_w2=f32(inp["hy_w2"]), hy_w3=f32(inp["hy_w3"]),
        hy_dec=f32(inp["hy_decay"]), hy_skip=f32(inp["hy_skip"]), hg_low=f32(inp["hg_lower"]),
        **cst)
    in_maps = []
    for k in range(8):
        b = k % 2
        m = dict(shared)
        m["xs"] = f32(inp["x_sample"][b])
        m["xp"] = f32(inp["x_prompt"][4 * k:4 * k + 4].reshape(1024, D))
        m["st_gla"] = f32(inp["state_gla"][b].reshape(NL, 2, 256, 64))
        m["st_hg"] = f32(inp["state_hgrn"][b].reshape(NL, 2, 256, 64))
        m["st_rg"] = f32(inp["state_rglru"][b])
        m["sp"] = _build_sp(inp, b)
        in_maps.append(m)
    if NCORES < 8:
        res = run_bass_kernel_spmd(nc, in_maps[:NCORES], core_ids=list(range(NCORES)))
        R = [res.results[k % NCORES] for k in range(8)]
    else:
        res = run_bass_kernel_spmd(nc, in_maps, core_ids=list(range(8)))
        R = res.results
    y_prompt = np.concatenate([R[k]["yp"].reshape(4, 256, D) for k in range(8)], axis=0)
    y_sample = np.stack([R[0]["ys"], R[1]["ys"]], axis=0)
    ns_gla = np.concatenate([R[k]["ns_gla"].reshape(4, NL, 2, 4, 64, 64) for k in range(8)], axis=0)
    ns_rg = np.concatenate([R[k]["ns_rg"].reshape(4, NL, 2, 256) for k in range(8)], axis=0)
    ns_hg = np.concatenate([R[k]["ns_hg"].reshape(4, NL, 2, 4, 64, 64) for k in range(8)], axis=0)
    return (y_prompt.astype(np.float32), y_sample.astype(np.float32), ns_gla.astype(np.float32),
            ns_rg.astype(np.float32), ns_hg.astype(np.float32))
```
